# Optimizing a Trainium2 kernel written in Bass

```python
import jax
import jax.numpy as jnp
from jax import lax
import numpy as np

D_MODEL = 1024
BATCH = 8
SEQ = 4096
DEPTH = 2

CTX_LEN = 256
GRID_W = 64
MIX_WIDTH = D_MODEL
A_WIDTH = MIX_WIDTH // 2
A_GROUPS = 4
A_GROUP_CH = A_WIDTH // A_GROUPS
A_CHUNK = 128
NA_HEADS = 8
NA_HEAD_DIM = (MIX_WIDTH - A_WIDTH) // NA_HEADS
NA_WIDTH = NA_HEADS * NA_HEAD_DIM
NA_WIN_ROWS = 8
NA_WIN_COLS = 16
O_NA_Q = 2 * A_WIDTH
O_NA_K = O_NA_Q + NA_WIDTH
O_NA_V = O_NA_K + NA_WIDTH
AB_IN_WIDTH = O_NA_V + NA_WIDTH
FN_WIDTH = MIX_WIDTH // 2
FN_GROUPS = 4
FN_GROUP_CH = FN_WIDTH // FN_GROUPS
GLA_HEADS = 4
GLA_VAL_WIDTH = MIX_WIDTH - FN_WIDTH
GLA_KEY_WIDTH = GLA_VAL_WIDTH // 2
GLA_DK = GLA_KEY_WIDTH // GLA_HEADS
GLA_DV = GLA_VAL_WIDTH // GLA_HEADS
GLA_LOW_RANK = 16
GLA_GATE_TEMP = 16.0
GLA_CHUNK = 64
O_Q = FN_WIDTH
O_K = O_Q + GLA_KEY_WIDTH
O_V = O_K + GLA_KEY_WIDTH
O_G = O_V + GLA_VAL_WIDTH
O_A = O_G + GLA_VAL_WIDTH
CD_IN_WIDTH = O_A + 2 * GLA_LOW_RANK
FFN_HIDDEN = -(-8 * D_MODEL // (3 * 256)) * 256
ROPE_BASE = 10000.0
RMS_EPS = 1e-6
NEG_INF = -1e30
N_EVEN = (DEPTH + 1) // 2
N_ODD = DEPTH // 2

kernel_name = "hybrid_dit_gmlp_natten_fnet_gla"


def rmsnorm(x, w):
    xf = x.astype(jnp.float32)
    y = xf * lax.rsqrt(jnp.mean(xf * xf, axis=-1, keepdims=True) + RMS_EPS)
    return (y * w.astype(jnp.float32)).astype(x.dtype)


def modulate(h, shift, scale):
    return h * (1.0 + scale) + shift


def swiglu(h, w_gate, w_up, w_down):
    return (jax.nn.silu(h @ w_gate) * (h @ w_up)) @ w_down


def _rope_1d(xh, pos):
    half = xh.shape[-1] // 2
    inv_freq = ROPE_BASE ** (-jnp.arange(half, dtype=jnp.float32) / half)
    ang = pos.astype(jnp.float32)[:, None] * inv_freq[None, :]
    cos = jnp.cos(ang)[None, :, None, :].astype(xh.dtype)
    sin = jnp.sin(ang)[None, :, None, :].astype(xh.dtype)
    x1, x2 = xh[..., :half], xh[..., half:]
    return jnp.concatenate([x1 * cos - x2 * sin, x2 * cos + x1 * sin], axis=-1)


def axial_rope(x):
    t = jnp.arange(x.shape[1])
    d2 = x.shape[-1] // 2
    return jnp.concatenate([_rope_1d(x[..., :d2], t // GRID_W), _rope_1d(x[..., d2:], t % GRID_W)], axis=-1)


def chunk_token_mlp(uv, norm_w, sgu_w, sgu_b):
    B, T, _ = uv.shape
    uv = jax.nn.gelu(uv)
    u, v = uv[..., :A_WIDTH], uv[..., A_WIDTH:]
    v = rmsnorm(v, norm_w)
    vg = v.reshape(B, T // A_CHUNK, A_CHUNK, A_GROUPS, A_GROUP_CH)
    gate = jnp.einsum('gpq,bnqgc->bnpgc', sgu_w, vg) + sgu_b.T[None, None, :, :, None]
    return u * gate.reshape(B, T, A_WIDTH)


def _na_heads(t):
    return t.reshape(t.shape[0], t.shape[1], NA_HEADS, NA_HEAD_DIM)


def neighbourhood_attention(q, k, v, k_ctx, v_ctx, rel_bias):
    B, T, H, dh = q.shape
    rows = T // GRID_W
    wr = min(NA_WIN_ROWS, rows)
    scale = dh ** -0.5
    qg = q.reshape(B, rows, GRID_W, H, dh)
    kg = k.reshape(B, rows, GRID_W, H, dh)
    vg = v.reshape(B, rows, GRID_W, H, dh)
    col = jnp.arange(GRID_W)
    col_start = jnp.clip(col - NA_WIN_COLS // 2, 0, GRID_W - NA_WIN_COLS)
    col_mask = (col[None, :] >= col_start[:, None]) & (col[None, :] < col_start[:, None] + NA_WIN_COLS)
    col_idx = jnp.clip(col[None, :] - col[:, None] + NA_WIN_COLS - 1, 0, 2 * NA_WIN_COLS - 2)

    def row_block(r):
        row_start = jnp.clip(r - wr // 2, 0, rows - wr)
        qr = lax.dynamic_index_in_dim(qg, r, axis=1, keepdims=False)
        kb = lax.dynamic_slice_in_dim(kg, row_start, wr, axis=1)
        vb = lax.dynamic_slice_in_dim(vg, row_start, wr, axis=1)
        row_idx = row_start + jnp.arange(wr) - r + NA_WIN_ROWS - 1
        bias = jnp.transpose(rel_bias[:, row_idx][:, :, col_idx], (0, 2, 1, 3))
        s_nb = jnp.einsum('bqhd,bwkhd->bhqwk', qr, kb).astype(jnp.float32) * scale + bias[None].astype(jnp.float32)
        s_nb = jnp.where(col_mask[:, None, :], s_nb, NEG_INF).reshape(B, H, GRID_W, wr * GRID_W)
        s_cx = jnp.einsum('bqhd,bchd->bhqc', qr, k_ctx).astype(jnp.float32) * scale
        p = jax.nn.softmax(jnp.concatenate([s_nb, s_cx], axis=-1), axis=-1).astype(v.dtype)
        p_nb = p[..., :wr * GRID_W].reshape(B, H, GRID_W, wr, GRID_W)
        p_cx = p[..., wr * GRID_W:]
        return jnp.einsum('bhqwk,bwkhd->bqhd', p_nb, vb) + jnp.einsum('bhqc,bchd->bqhd', p_cx, v_ctx)

    out = lax.map(row_block, jnp.arange(rows))
    return jnp.moveaxis(out, 0, 1).reshape(B, T, H * dh)


def context_attention(q, k, v):
    B, T, H, dh = q.shape
    s = jnp.einsum('bqhd,bkhd->bhqk', q, k).astype(jnp.float32) * dh ** -0.5
    p = jax.nn.softmax(s, axis=-1).astype(v.dtype)
    return jnp.einsum('bhqk,bkhd->bqhd', p, v).reshape(B, T, H * dh)


def ab_mixer(h, hc, w_in, w_out, sgu_norm_w, sgu_w, sgu_b, rel_bias, with_ctx_out):
    p = h @ w_in
    a_out = chunk_token_mlp(p[..., :O_NA_Q], sgu_norm_w, sgu_w, sgu_b)
    q_l = _na_heads(p[..., O_NA_Q:O_NA_K])
    k_l = _na_heads(p[..., O_NA_K:O_NA_V])
    v_l = _na_heads(p[..., O_NA_V:])
    if with_ctx_out:
        pc = hc @ w_in
        kv_c = pc[..., O_NA_K:]
    else:
        pc = None
        kv_c = hc @ w_in[:, O_NA_K:]
    k_c = _na_heads(kv_c[..., :NA_WIDTH])
    v_c = _na_heads(kv_c[..., NA_WIDTH:])
    b_out = neighbourhood_attention(q_l, k_l, v_l, k_c, v_c, rel_bias)
    y = jnp.concatenate([a_out, b_out], axis=-1) @ w_out
    if not with_ctx_out:
        return y, None
    a_c = chunk_token_mlp(pc[..., :O_NA_Q], sgu_norm_w, sgu_w, sgu_b)
    b_c = context_attention(_na_heads(pc[..., O_NA_Q:O_NA_K]), k_c, v_c)
    y_c = jnp.concatenate([a_c, b_c], axis=-1) @ w_out
    return y, y_c


def fourier_mix(f):
    B, T, _ = f.shape
    fg = f.reshape(B, T, FN_GROUPS, FN_GROUP_CH).astype(jnp.float32)
    out = jnp.fft.fft2(fg, axes=(1, 3), norm='ortho').real
    return out.reshape(B, T, FN_WIDTH).astype(f.dtype)


def _to_heads(t, n_heads):
    B, T, _ = t.shape
    return jnp.transpose(t.reshape(B, T, n_heads, -1), (0, 2, 1, 3))


def log_decay(a_lr, w, b):
    return jax.nn.log_sigmoid((a_lr @ w + b).astype(jnp.float32)) / GLA_GATE_TEMP


def _gla_chunk_terms(k, v, log_a):
    B, H, T, dk = k.shape
    n = T // GLA_CHUNK
    kc = k.reshape(B, H, n, GLA_CHUNK, dk)
    vc = v.reshape(B, H, n, GLA_CHUNK, -1)
    b = jnp.cumsum(log_a.reshape(B, H, n, GLA_CHUNK, dk), axis=3)
    b_last = b[:, :, :, -1:, :]
    kv = jnp.einsum('bhnld,bhnle->bhnde', kc * jnp.exp(b_last - b), vc)
    return kc, vc, b, kv, jnp.exp(b_last[:, :, :, 0, :])


def _gla_scan(kv, decay, s0):
    def step(s, inp):
        d, kv_n = inp
        return d[..., None] * s + kv_n, s
    s_fin, s_prev = lax.scan(step, s0, (jnp.moveaxis(decay, 2, 0), jnp.moveaxis(kv, 2, 0)))
    return jnp.moveaxis(s_prev, 0, 2), s_fin


def gla_final_state(k, v, log_a, s0):
    _, _, _, kv, decay = _gla_chunk_terms(k, v, log_a)
    _, s_fin = _gla_scan(kv, decay, s0)
    return s_fin


def gla_chunked(q, k, v, log_a, s0):
    B, H, T, dk = q.shape
    kc, vc, b, kv, decay = _gla_chunk_terms(k, v, log_a)
    s_prev, s_fin = _gla_scan(kv, decay, s0)
    qe = q.reshape(B, H, -1, GLA_CHUNK, dk) * jnp.exp(b)
    lower = jnp.tril(jnp.ones((GLA_CHUNK, GLA_CHUNK), jnp.float32))
    att = jnp.einsum('bhnld,bhnmd->bhnlm', qe, kc * jnp.exp(-b)) * lower
    o = jnp.einsum('bhnlm,bhnme->bhnle', att, vc) + jnp.einsum('bhnld,bhnde->bhnle', qe, s_prev)
    return o.reshape(B, H, T, -1), s_fin


def gla_output(o, g, head_norm_w):
    B, H, T, dv = o.shape
    o = rmsnorm(jnp.transpose(o, (0, 2, 1, 3)), head_norm_w)
    return (o * jax.nn.silu(g.reshape(B, T, H, dv).astype(jnp.float32))).reshape(B, T, H * dv).astype(g.dtype)


def _split_cd(p):
    return p[..., :O_Q], p[..., O_Q:O_K], p[..., O_K:O_V], p[..., O_V:O_G], p[..., O_G:O_A], p[..., O_A:]


def cd_mixer(h, hc, w_in, w_out, dec_w_f, dec_b_f, dec_w_b, dec_b_b, head_norm_w, with_ctx_out):
    B, T, _ = h.shape
    f32 = jnp.float32
    flip = lambda t: jnp.flip(t, axis=2)
    f_l, q_l, k_l, v_l, g_l, a_l = _split_cd(h @ w_in)
    q_l = axial_rope(q_l.reshape(B, T, GLA_HEADS, GLA_DK)) * GLA_DK ** -0.5
    k_l = axial_rope(k_l.reshape(B, T, GLA_HEADS, GLA_DK))
    ql = jnp.transpose(q_l, (0, 2, 1, 3)).astype(f32)
    kl = jnp.transpose(k_l, (0, 2, 1, 3)).astype(f32)
    vl = _to_heads(v_l, GLA_HEADS).astype(f32)
    la_lf = _to_heads(log_decay(a_l[..., :GLA_LOW_RANK], dec_w_f, dec_b_f), GLA_HEADS)
    la_lb = _to_heads(log_decay(a_l[..., GLA_LOW_RANK:], dec_w_b, dec_b_b), GLA_HEADS)
    if with_ctx_out:
        f_c, q_c, k_c, v_c, g_c, a_c = _split_cd(hc @ w_in)
    else:
        pc = hc @ jnp.concatenate([w_in[:, O_K:O_G], w_in[:, O_A:]], axis=1)
        k_c = pc[..., :GLA_KEY_WIDTH]
        v_c = pc[..., GLA_KEY_WIDTH:GLA_KEY_WIDTH + GLA_VAL_WIDTH]
        a_c = pc[..., GLA_KEY_WIDTH + GLA_VAL_WIDTH:]
    kc = _to_heads(k_c, GLA_HEADS).astype(f32)
    vc = _to_heads(v_c, GLA_HEADS).astype(f32)
    la_cf = _to_heads(log_decay(a_c[..., :GLA_LOW_RANK], dec_w_f, dec_b_f), GLA_HEADS)
    la_cb = _to_heads(log_decay(a_c[..., GLA_LOW_RANK:], dec_w_b, dec_b_b), GLA_HEADS)
    s0 = jnp.zeros((hc.shape[0], GLA_HEADS, GLA_DK, GLA_DV), f32)
    if with_ctx_out:
        qc = (_to_heads(q_c, GLA_HEADS) * GLA_DK ** -0.5).astype(f32)
        o_cf, s_f = gla_chunked(qc, kc, vc, la_cf, s0)
        o_cb, s_b = gla_chunked(flip(qc), flip(kc), flip(vc), flip(la_cb), s0)
        o_c = o_cf + flip(o_cb)
    else:
        s_f = gla_final_state(kc, vc, la_cf, s0)
        s_b = gla_final_state(flip(kc), flip(vc), flip(la_cb), s0)
    o_lf, _ = gla_chunked(ql, kl, vl, la_lf, s_f)
    o_lb, _ = gla_chunked(flip(ql), flip(kl), flip(vl), flip(la_lb), s_b)
    o_l = o_lf + flip(o_lb)
    y = jnp.concatenate([fourier_mix(f_l), gla_output(o_l, g_l, head_norm_w)], axis=-1) @ w_out
    if not with_ctx_out:
        return y, None
    y_c = jnp.concatenate([fourier_mix(f_c), gla_output(o_c, g_c, head_norm_w)], axis=-1) @ w_out
    return y, y_c


def setup_inputs(seed: int = 0) -> dict:
    key = jax.random.key(seed)
    ks = jax.random.split(key, 26)
    D = D_MODEL

    def nrm(k, shape, s):
        return jax.random.normal(k, shape, jnp.float32) * s

    return {
        'x': nrm(ks[0], (BATCH, SEQ, D), 1.0),
        'c': nrm(ks[1], (BATCH, D), 1.0),
        'ctx': nrm(ks[2], (BATCH, CTX_LEN, D), 1.0),
        'c_ctx': nrm(ks[3], (D,), 1.0),
        'ada_w': nrm(ks[4], (DEPTH, D, 6 * D), 0.5 * D ** -0.5),
        'ada_b': nrm(ks[5], (DEPTH, 6 * D), 0.02),
        'norm_mix_w': 1.0 + nrm(ks[6], (DEPTH, D), 0.05),
        'norm_ffn_w': 1.0 + nrm(ks[7], (DEPTH, D), 0.05),
        'ffn_w_gate': nrm(ks[8], (DEPTH, D, FFN_HIDDEN), D ** -0.5),
        'ffn_w_up': nrm(ks[9], (DEPTH, D, FFN_HIDDEN), D ** -0.5),
        'ffn_w_down': nrm(ks[10], (DEPTH, FFN_HIDDEN, D), FFN_HIDDEN ** -0.5),
        'ab_w_in': nrm(ks[11], (N_EVEN, D, AB_IN_WIDTH), D ** -0.5),
        'ab_w_out': nrm(ks[12], (N_EVEN, MIX_WIDTH, D), MIX_WIDTH ** -0.5),
        'ab_sgu_norm_w': 1.0 + nrm(ks[13], (N_EVEN, A_WIDTH), 0.05),
        'ab_sgu_w': nrm(ks[14], (N_EVEN, A_GROUPS, A_CHUNK, A_CHUNK), A_CHUNK ** -0.5),
        'ab_sgu_b': 1.0 + nrm(ks[15], (N_EVEN, A_GROUPS, A_CHUNK), 0.05),
        'ab_rel_bias': nrm(ks[16], (N_EVEN, NA_HEADS, 2 * NA_WIN_ROWS - 1, 2 * NA_WIN_COLS - 1), 0.5),
        'cd_w_in': nrm(ks[17], (N_ODD, D, CD_IN_WIDTH), D ** -0.5),
        'cd_w_out': nrm(ks[18], (N_ODD, MIX_WIDTH, D), MIX_WIDTH ** -0.5),
        'cd_decay_w_fwd': nrm(ks[19], (N_ODD, GLA_LOW_RANK, GLA_KEY_WIDTH), GLA_LOW_RANK ** -0.5),
        'cd_decay_b_fwd': nrm(ks[20], (N_ODD, GLA_KEY_WIDTH), 0.1),
        'cd_decay_w_bwd': nrm(ks[21], (N_ODD, GLA_LOW_RANK, GLA_KEY_WIDTH), GLA_LOW_RANK ** -0.5),
        'cd_decay_b_bwd': nrm(ks[22], (N_ODD, GLA_KEY_WIDTH), 0.1),
        'cd_head_norm_w': 1.0 + nrm(ks[23], (N_ODD, GLA_DV), 0.05),
        'final_norm_w': 1.0 + nrm(ks[24], (D,), 0.05),
    }


def reference(x, c, ctx, c_ctx, ada_w, ada_b, norm_mix_w, norm_ffn_w, ffn_w_gate, ffn_w_up, ffn_w_down,
              ab_w_in, ab_w_out, ab_sgu_norm_w, ab_sgu_w, ab_sgu_b, ab_rel_bias,
              cd_w_in, cd_w_out, cd_decay_w_fwd, cd_decay_b_fwd, cd_decay_w_bwd, cd_decay_b_bwd, cd_head_norm_w,
              final_norm_w):
    silu_c = jax.nn.silu(c)
    silu_cc = jax.nn.silu(c_ctx)
    for i in range(DEPTH):
        last = i == DEPTH - 1
        j = i // 2
        mod = (silu_c @ ada_w[i] + ada_b[i])[:, None, :]
        mod_c = (silu_cc @ ada_w[i] + ada_b[i])[None, None, :]
        sh_m, sc_m, g_m, sh_f, sc_f, g_f = jnp.split(mod, 6, axis=-1)
        shc_m, scc_m, gc_m, shc_f, scc_f, gc_f = jnp.split(mod_c, 6, axis=-1)
        h = modulate(rmsnorm(x, norm_mix_w[i]), sh_m, sc_m)
        hc = modulate(rmsnorm(ctx, norm_mix_w[i]), shc_m, scc_m)
        if i % 2 == 0:
            y, y_c = ab_mixer(h, hc, ab_w_in[j], ab_w_out[j], ab_sgu_norm_w[j], ab_sgu_w[j], ab_sgu_b[j],
                              ab_rel_bias[j], not last)
        else:
            y, y_c = cd_mixer(h, hc, cd_w_in[j], cd_w_out[j], cd_decay_w_fwd[j], cd_decay_b_fwd[j],
                              cd_decay_w_bwd[j], cd_decay_b_bwd[j], cd_head_norm_w[j], not last)
        x = x + g_m * y
        x = x + g_f * swiglu(modulate(rmsnorm(x, norm_ffn_w[i]), sh_f, sc_f), ffn_w_gate[i], ffn_w_up[i], ffn_w_down[i])
        if not last:
            ctx = ctx + gc_m * y_c
            ctx = ctx + gc_f * swiglu(modulate(rmsnorm(ctx, norm_ffn_w[i]), shc_f, scc_f),
                                      ffn_w_gate[i], ffn_w_up[i], ffn_w_down[i])
    return rmsnorm(x, final_norm_w)
```

```python
import numpy as np
import concourse.bass as bass
import concourse.mybir as mybir
from concourse.bass_utils import run_bass_kernel_spmd
from contextlib import ExitStack

F32 = mybir.dt.float32
BF16 = mybir.dt.bfloat16
AF = mybir.ActivationFunctionType
ALU = mybir.AluOpType
AX = mybir.AxisListType

NX, NCX, NT, D, KC, FH, NJ = 4096, 256, 4352, 1024, 8, 2816, 22
EPS = 1e-6
DEBUG = False


class Res:
    __slots__ = ("w", "r")

    def __init__(self):
        self.w = {}
        self.r = {}


class Sched:
    NLANES = 12

    def __init__(self, nc, es):
        self.nc = nc
        self.keys = ("pe", "act", "dve", "pool", "sp")
        self.streams = {k: [] for k in self.keys}
        self.sems, self.count = {}, {}
        self.known = {k: {} for k in self.keys}
        for k in ("pe", "act", "dve", "pool"):
            self.sems[k] = es.enter_context(nc.semaphore("s_" + k))
            self.count[k] = 0
        self.lanes = {}
        for q in ("sp", "pool"):
            self.lanes[q] = []
            for i in range(self.NLANES):
                key = "L%s%d" % (q, i)
                self.sems[key] = es.enter_context(nc.semaphore(key))
                self.count[key] = 0
                self.lanes[q].append(key)
        self.lane_rr = {q: 0 for q in self.lanes}

    def _need(self, issuer, ckey, seq, waits):
        if not seq:
            return
        if ckey == "pe" and issuer == "pe":
            return
        if self.known[issuer].get(ckey, 0) >= seq:
            return
        self.known[issuer][ckey] = seq
        waits[ckey] = max(waits.get(ckey, 0), seq)

    def _deps(self, issuer, reads, writes):
        waits = {}
        for r in reads:
            for ck, sq in r.w.items():
                self._need(issuer, ck, sq, waits)
        for r in writes:
            for ck, sq in r.w.items():
                self._need(issuer, ck, sq, waits)
            for ck, sq in r.r.items():
                self._need(issuer, ck, sq, waits)
        return waits

    def _mark(self, ckey, seq, reads, writes):
        for r in reads:
            if r.r.get(ckey, 0) < seq:
                r.r[ckey] = seq
        for r in writes:
            r.w = {ckey: seq}
            r.r = {}

    def I(self, eng, meth, reads, writes, *a, **kw):
        waits = self._deps(eng, reads, writes)
        self.count[eng] += 1
        seq = self.count[eng]
        self._mark(eng, seq, reads, writes)
        self.streams[eng].append((lambda e: getattr(e, meth)(*a, **kw), waits, eng, 1))

    def D(self, q, out, in_, reads, writes, **kw):
        lanes = self.lanes[q]
        lane = lanes[self.lane_rr[q] % len(lanes)]
        self.lane_rr[q] += 1
        waits = self._deps(q, reads, writes)
        self._need(q, lane, self.count[lane], waits)
        self.count[lane] += 1
        seq = self.count[lane]
        self._mark(lane, seq, reads, writes)
        self.streams[q].append((lambda e: e.dma_start(out=out, in_=in_, **kw), waits, lane, 16))

    def barrier(self):
        for e in self.keys:
            waits = {}
            for ck, c in self.count.items():
                self._need(e, ck, c, waits)
            if waits:
                self.streams[e].append((None, waits, None, 0))

    def _mult(self, ck):
        return 16 if ck.startswith("L") else 1

    def flush(self):
        self.barrier()
        nc = self.nc
        streams = self.streams
        self.streams = {k: [] for k in self.keys}

        def mk(key):
            def body(e):
                for fn, waits, ckey, inc in streams[key]:
                    for wk, sq in waits.items():
                        e.wait_ge(self.sems[wk], sq * self._mult(wk))
                    if fn is not None:
                        fn(e).then_inc(self.sems[ckey], inc)
            return body
        with nc.Block() as block:
            block.tensor(mk("pe"))
            block.scalar(mk("act"))
            block.vector(mk("dve"))
            block.gpsimd(mk("pool"))
            block.sync(mk("sp"))


class TT:
    def __init__(self, t):
        self.t = t
        self.r = Res()


class Rot:
    def __init__(self, items):
        self.items = items
        self.i = 0

    def next(self):
        x = self.items[self.i % len(self.items)]
        self.i += 1
        return x


class KB:
    pass


_uid = [0]


def sbt(K, ph, shape, dt):
    _uid[0] += 1
    return TT(ph.enter_context(K.nc.sbuf_tensor("t%d" % _uid[0], list(shape), dt)))


def sbrot(K, ph, n, shape, dt):
    return Rot([sbt(K, ph, shape, dt) for _ in range(n)])


def bank(K):
    pool = getattr(K, "bank_pool", None) or list(range(8))
    i = pool[K.bank_rr % len(pool)]
    K.bank_rr += 1
    return K.PB[i // 2].t[:, (i % 2) * 512:(i % 2) * 512 + 512], K.RB[i]


def bank_fixed(K, i):
    return K.PB[i // 2].t[:, (i % 2) * 512:(i % 2) * 512 + 512], K.RB[i]


def bank2(K):
    if K.bank_rr % 2:
        K.bank_rr += 1
    i = (K.bank_rr % 8) // 2
    K.bank_rr += 2
    return K.PB[i].t[:, :], [K.RB[2 * i], K.RB[2 * i + 1]]


def norm_mod(K, S, xt, n, A, B, j, hT, tmps, sq, rstd):
    S.I("act", "activation", [xt.r], [sq.r], out=sq.t[:, :, :n], in_=xt.t[:, :, :n], func=AF.Square)
    ms, rms = bank(K)
    for kc in range(KC):
        S.I("pe", "matmul", [K.ones_bf.r, sq.r], [rms], ms[:, :n], lhsT=K.ones_bf.t[:, :], rhs=sq.t[:, kc, :n],
            start=(kc == 0), stop=(kc == KC - 1))
    S.I("act", "activation", [rms], [rstd.r], out=rstd.t[:, :n], in_=ms[:, :n], func=AF.Ln, scale=1.0 / D, bias=K.epsc.t[:, 0:1])
    S.I("act", "activation", [rstd.r], [rstd.r], out=rstd.t[:, :n], in_=rstd.t[:, :n], func=AF.Exp, scale=-0.5)
    for kc in range(KC):
        tm = tmps.next()
        S.I("dve", "tensor_tensor", [xt.r, rstd.r], [tm.r], out=tm.t[:, :n], in0=xt.t[:, kc, :n], in1=rstd.t[:, :n], op=ALU.mult)
        S.I("act", "activation", [tm.r, K.modr], [hT.r], out=hT.t[:, kc, :n], in_=tm.t[:, :n], func=AF.Identity,
            scale=A.t[:, kc, j:j + 1], bias=B.t[:, kc, j:j + 1])


def small_rstd(K, S, ss, rs, dim):
    S.I("act", "activation", [ss.r], [rs.r], out=rs.t[:, 0:1], in_=ss.t[:, 0:1], func=AF.Ln, scale=1.0 / dim, bias=K.epsc.t[:, 0:1])
    S.I("act", "activation", [rs.r], [rs.r], out=rs.t[:, 0:1], in_=rs.t[:, 0:1], func=AF.Exp, scale=-0.5)


def blkres(rl, t0, n):
    return rl[t0 // 128:(t0 + n) // 128]


def phase_consts(K, S, ph):
    nc = K.nc
    K.ident = sbt(K, ph, [128, 128], F32)
    S.D("sp", K.ident.t[:, :], K.inp["ident"], [], [K.ident.r])
    K.ident_bf = sbt(K, ph, [128, 128], BF16)
    S.D("pool", K.ident_bf.t[:, :], K.inp["ident"], [], [K.ident_bf.r])
    K.ones_bf = sbt(K, ph, [128, 128], BF16)
    S.I("pool", "memset", [], [K.ones_bf.r], K.ones_bf.t[:, :], 1.0)
    K.ones_f = sbt(K, ph, [1, 128], F32)
    S.I("pool", "memset", [], [K.ones_f.r], K.ones_f.t[:, :], 1.0)
    K.epsc = sbt(K, ph, [128, 1], F32)
    S.I("pool", "memset", [], [K.epsc.r], K.epsc.t[:, :], EPS)
    K.modr = Res()
    K.mods = []
    for l in range(2):
        K.mods.append({k: sbt(K, ph, [128, 8, 2], F32) for k in ("Am", "Bm", "Gm", "Af", "Bf", "Gf")})
        for v in K.mods[l].values():
            v.r = K.modr


def phase_mod(K, S):
    with ExitStack() as ph:
        csil = sbt(K, ph, [128, 8, 2], F32)
        S.D("sp", csil.t[:, :, :], K.inp["csT"], [], [csil.r])
        S.I("act", "activation", [csil.r], [csil.r], out=csil.t[:, :, :], in_=csil.t[:, :, :], func=AF.Silu)
        awts = sbrot(K, ph, 3, [128, 8, 512], F32)
        modrow = sbt(K, ph, [2, 6144], F32)
        mod = sbt(K, ph, [128, 48, 2], F32)
        abT = sbt(K, ph, [128, 48], F32)
        nw = sbt(K, ph, [128, 8], F32)
        for l in range(2):
            awv = K.inp["ada_w"][l].rearrange("(kc p) n -> p kc n", p=128)
            for cb in range(12):
                aw = awts.next()
                S.D("sp", aw.t[:, :, :], awv[:, :, cb * 512:(cb + 1) * 512], [], [aw.r])
                ps, rps = bank(K)
                for kc in range(KC):
                    S.I("pe", "matmul", [aw.r, csil.r], [rps], ps[0:2, 0:512], lhsT=csil.t[:, kc, :], rhs=aw.t[:, kc, :], start=(kc == 0), stop=(kc == KC - 1))
                if cb % 2:
                    S.I("act", "copy", [rps], [modrow.r], out=modrow.t[0:2, cb * 512:(cb + 1) * 512], in_=ps[0:2, 0:512])
                else:
                    S.I("dve", "tensor_copy", [rps], [modrow.r], out=modrow.t[0:2, cb * 512:(cb + 1) * 512], in_=ps[0:2, 0:512])
            mp, rmp = bank(K)
            for j in range(48):
                S.I("pe", "transpose", [modrow.r, K.ident.r], [rmp], out=mp[:, 2 * j:2 * j + 2], in_=modrow.t[0:2, j * 128:(j + 1) * 128], identity=K.ident.t[0:2, 0:2])
            S.D("sp", abT.t[:, :], K.inp["ada_bT"][l], [], [abT.r])
            mpv = mp[:, 0:96].rearrange("p (a b) -> p a b", b=2)
            for j in range(2):
                S.I("dve", "tensor_tensor", [rmp, abT.r], [mod.r], out=mod.t[:, :, j], in0=mpv[:, :, j], in1=abT.t[:, :], op=ALU.add)
            M = K.mods[l]
            for (nm, a, b, g, off) in (("nmw", "Am", "Bm", "Gm", 0), ("nfw", "Af", "Bf", "Gf", 24)):
                S.D("sp", nw.t[:, :], K.inp[nm][l], [], [nw.r])
                for j in range(2):
                    S.I("dve", "scalar_tensor_tensor", [mod.r, nw.r], [K.modr], out=M[a].t[:, :, j], in0=mod.t[:, off + 8:off + 16, j],
                        scalar=1.0, in1=nw.t[:, :], op0=ALU.add, op1=ALU.mult)
                    S.I("dve", "tensor_copy", [mod.r], [K.modr], out=M[b].t[:, :, j], in_=mod.t[:, off:off + 8, j])
                    S.I("dve", "tensor_copy", [mod.r], [K.modr], out=M[g].t[:, :, j], in_=mod.t[:, off + 16:off + 24, j])
        S.flush()


def phase_wcast(K, S):
    with ExitStack() as ph:
        fin = sbrot(K, ph, 2, [128, FH], F32)
        fout = sbrot(K, ph, 2, [128, FH], BF16)
        engs = Rot(["pool", "dve", "act"])
        for l in range(2):
            for m, nm in ((0, "ffn_wg"), (1, "ffn_wu")):
                for kc in range(KC):
                    a, b = fin.next(), fout.next()
                    S.D("sp", a.t[:, :], K.inp[nm][l][kc * 128:(kc + 1) * 128, :], [], [a.r])
                    e = engs.next()
                    if e == "act":
                        S.I("act", "copy", [a.r], [b.r], out=b.t[:, :], in_=a.t[:, :])
                    else:
                        S.I(e, "tensor_copy", [a.r], [b.r], out=b.t[:, :], in_=a.t[:, :])
                    S.D("sp", K.WGb[l][m][kc * 128:(kc + 1) * 128, :], b.t[:, :], [b.r], [K.R_WG[l][m][kc]])
        S.flush()


def phase0(K, S):
    with ExitStack() as ph:
        xin = sbrot(K, ph, 2, [128, 1024], F32)
        xo = sbrot(K, ph, 2, [128, 8, 128], F32)
        for blk in range(34):
            src = K.inp["x"][blk * 128:(blk + 1) * 128, :] if blk < 32 else K.inp["ctx"][(blk - 32) * 128:(blk - 31) * 128, :]
            a = xin.next()
            S.D("sp", a.t[:, :], src, [], [a.r])
            pt, rpt = bank2(K)
            for kc in range(KC):
                S.I("pe", "transpose", [a.r, K.ident.r], rpt, out=pt[:, kc * 128:(kc + 1) * 128], in_=a.t[:, kc * 128:(kc + 1) * 128],
                    identity=K.ident.t[:, :])
            o = xo.next()
            ov = o.t[:, :, :].rearrange("p a b -> p (a b)")
            if blk % 2:
                S.I("act", "copy", rpt, [o.r], out=ov, in_=pt)
            else:
                S.I("dve", "tensor_copy", rpt, [o.r], out=ov, in_=pt)
            S.D("sp", K.XT[:, :, blk * 128:(blk + 1) * 128], o.t[:, :, :], [o.r], [K.R_XT[blk]])
        S.flush()


def phaseA0(K, S):
    M = K.mods[0]
    with ExitStack() as ph:
        win = sbt(K, ph, [128, 8, 2560], BF16)
        wv = K.inp["ab_w_in"].rearrange("(kc p) n -> p kc n", p=128)
        for kc in range(KC):
            S.D("pool", win.t[:, kc, :], wv[:, kc, :], [], [win.r])
        sguw = sbt(K, ph, [128, 4, 128], BF16)
        S.D("pool", sguw.t[:, :, :], K.inp["sguWT"], [], [sguw.r])
        nwbc = sbt(K, ph, [128, 512], F32)
        S.D("sp", nwbc.t[:, :], K.inp["sgu_nw_bc"], [], [nwbc.r])
        sgub = sbt(K, ph, [128, 512], F32)
        S.D("sp", sgub.t[:, :], K.inp["sgub"], [], [sgub.r])
        gts = sbrot(K, ph, 2, [128, 512], F32)
        xts = sbrot(K, ph, 2, [128, 8, 512], F32)
        xins = sbrot(K, ph, 3, [128, 1024], F32)
        fins = sbrot(K, ph, 2, [128, FH], F32)
        fouts = sbrot(K, ph, 2, [128, FH], BF16)
        wcl = [(l, m, nm, kc) for l in range(2) for m, nm in ((0, "ffn_wg"), (1, "ffn_wu")) for kc in range(KC)]

        def wcast_step():
            if not wcl:
                return
            l, m, nm, kc = wcl.pop(0)
            a, b = fins.next(), fouts.next()
            S.D("sp", a.t[:, :], K.inp[nm][l][kc * 128:(kc + 1) * 128, :], [], [a.r])
            S.I("pool", "tensor_copy", [a.r], [b.r], out=b.t[:, :], in_=a.t[:, :])
            S.D("sp", K.WGb[l][m][kc * 128:(kc + 1) * 128, :], b.t[:, :], [b.r], [K.R_WG[l][m][kc]])

        sqs = sbrot(K, ph, 2, [128, 8, 512], BF16)
        rstds = sbrot(K, ph, 2, [128, 512], F32)
        tmps = sbrot(K, ph, 2, [128, 512], F32)
        hTs = sbrot(K, ph, 2, [128, 8, 512], BF16)
        uT = sbt(K, ph, [128, 4, 512], BF16)
        qos = sbrot(K, ph, 2, [128, 4, 512], BF16)
        kos = sbrot(K, ph, 2, [128, 4, 512], BF16)
        aos = sbrot(K, ph, 2, [128, 4, 512], BF16)
        vos = sbrot(K, ph, 2, [128, 8, 65], BF16)
        for v in vos.items:
            S.I("pool", "memset", [], [v.r], v.t[:, :, :], 1.0)
        gvs = sbrot(K, ph, 2, [128, 512], F32)
        junk = sbt(K, ph, [128, 512], BF16)
        vns = sbrot(K, ph, 2, [128, 512], BF16)
        sss = sbrot(K, ph, 2, [128, 1], F32)
        rss = sbrot(K, ph, 2, [128, 1], F32)
        def tile(t):
            n = 512 if t < 8 else 256
            t0 = t * 512
            j = 0 if t < 8 else 1
            xt = xts.next()
            hT, sq, rstd = hTs.next(), sqs.next(), rstds.next()
            for sbk in range(n // 128):
                blk = t * 4 + sbk
                src = K.inp["x"][blk * 128:(blk + 1) * 128, :] if blk < 32 else K.inp["ctx"][(blk - 32) * 128:(blk - 31) * 128, :]
                a = xins.next()
                S.D("sp", a.t[:, :], src, [], [a.r])
                pt, rpt = bank2(K)
                for kc in range(KC):
                    S.I("pe", "transpose", [a.r, K.ident.r], rpt, out=pt[:, kc * 128:(kc + 1) * 128], in_=a.t[:, kc * 128:(kc + 1) * 128],
                        identity=K.ident.t[:, :])
                if sbk % 2:
                    S.I("act", "copy", rpt, [xt.r], out=xt.t[:, :, sbk * 128:(sbk + 1) * 128], in_=pt.rearrange("p (k t) -> p k t", t=128))
                else:
                    S.I("dve", "tensor_copy", rpt, [xt.r], out=xt.t[:, :, sbk * 128:(sbk + 1) * 128], in_=pt.rearrange("p (k t) -> p k t", t=128))
            S.D("sp", K.XT[:, :, t0:t0 + n], xt.t[:, :, :n], [xt.r], blkres(K.R_XT, t0, n))
            norm_mod(K, S, xt, n, M["Am"], M["Bm"], j, hT, tmps, sq, rstd)
            yield
            for _ in range(4):
                wcast_step()
            qo, ko, ao = qos.next(), kos.next(), aos.next()
            for ci in list(range(4)) + list(range(8, 16)):
                ps, rps = bank(K)
                for kc in range(KC):
                    S.I("pe", "matmul", [win.r, hT.r], [rps], ps[:, :n], lhsT=win.t[:, kc, ci * 128:(ci + 1) * 128], rhs=hT.t[:, kc, :n],
                        start=(kc == 0), stop=(kc == KC - 1))
                if ci < 4:
                    S.I("act", "activation", [rps], [uT.r], out=uT.t[:, ci, :n], in_=ps[:, :n], func=AF.Gelu_apprx_tanh)
                elif ci < 12:
                    S.I("dve", "tensor_scalar", [rps], [qo.r], out=qo.t[:, ci - 8, :n], in0=ps[:, :n], scalar1=0.125, scalar2=None, op0=ALU.mult)
                else:
                    S.I("act", "copy", [rps], [ko.r], out=ko.t[:, ci - 12, :n], in_=ps[:, :n])
            S.D("sp", K.QT[:, :, t0:t0 + n], qo.t[:, :, :n], [qo.r], blkres(K.R_QT, t0, n))
            S.D("sp", K.KT[:, :, t0:t0 + n], ko.t[:, :, :n], [ko.r], blkres(K.R_KT, t0, n))
            yield
            pend = []
            for sbk in range(n // 128):
                blk = t * 4 + sbk
                tk = slice(sbk * 128, (sbk + 1) * 128)
                pv, rpv = bank(K)
                pa, rpa = bank(K)
                for kc in range(KC):
                    S.I("pe", "matmul", [win.r, hT.r], [rpv], pv, lhsT=hT.t[:, kc, tk], rhs=win.t[:, kc, 512:1024], start=(kc == 0), stop=(kc == KC - 1))
                for kc in range(KC):
                    S.I("pe", "matmul", [win.r, hT.r], [rpa], pa, lhsT=hT.t[:, kc, tk], rhs=win.t[:, kc, 2048:2560], start=(kc == 0), stop=(kc == KC - 1))
                while pend:
                    pend.pop(0)()
                vo = vos.next()
                S.I("act", "copy", [rpa], [vo.r], out=vo.t[:, :, 0:64], in_=pa.rearrange("p (h d) -> p h d", d=64))
                S.D("sp", K.VT[blk].rearrange("p (h e) -> p h e", e=65), vo.t[:, :, :], [vo.r], [K.R_VT[blk]])
                gv, ss, rs, vn = gvs.next(), sss.next(), rss.next(), vns.next()
                S.I("act", "activation", [rpv], [gv.r], out=gv.t[:, :], in_=pv, func=AF.Gelu_apprx_tanh)
                S.I("dve", "memset", [], [ss.r], ss.t[:, :], 0.0)
                S.I("act", "activation", [gv.r, ss.r], [junk.r, ss.r], out=junk.t[:, :], in_=gv.t[:, :], func=AF.Square, accum_out=ss.t[:, 0:1])
                small_rstd(K, S, ss, rs, 512)
                S.I("dve", "scalar_tensor_tensor", [gv.r, rs.r, nwbc.r], [vn.r], out=vn.t[:, :], in0=gv.t[:, :], scalar=rs.t[:, 0:1],
                    in1=nwbc.t[:, :], op0=ALU.mult, op1=ALU.mult)
                def mk(vn=vn, tk=tk):
                    def f():
                        pg, rpg = bank(K)
                        for g in range(4):
                            gs = slice(g * 128, (g + 1) * 128)
                            S.I("pe", "matmul", [vn.r, sguw.r], [rpg], pg[:, gs], lhsT=vn.t[:, gs], rhs=sguw.t[:, g, :], start=True, stop=True)
                        gt = gts.next()
                        S.I("dve", "tensor_tensor", [rpg, sgub.r], [gt.r], out=gt.t[:, :], in0=pg, in1=sgub.t[:, :], op=ALU.add)
                        S.I("dve", "tensor_tensor", [gt.r, uT.r], [ao.r], out=ao.t[:, :, tk], in0=gt.t[:, :].rearrange("p (g t) -> p g t", t=128),
                            in1=uT.t[:, :, tk], op=ALU.mult)
                    return f
                pend.append(mk())
            while pend:
                pend.pop(0)()
            S.D("sp", K.MIX[:, 0:4, t0:t0 + n], ao.t[:, :, :n], [ao.r], blkres(K.R_MIXa, t0, n))
        gens = [tile(t) for t in range(9)]

        next(gens[0])
        for t in range(9):
            next(gens[t])
            if t + 1 < 9:
                next(gens[t + 1])
            for _ in gens[t]:
                pass
        while wcl:
            wcast_step()
        S.flush()


def na_blocks():
    out = []
    for i in range(32):
        if i < 2:
            out.append((list(range(4)), 5 + 4 * i))
        elif i >= 30:
            out.append((list(range(28, 32)), 5 + 4 * (i - 28)))
        else:
            out.append((list(range(i - 2, i + 3)), 0))
    out.append(([], 0))
    out.append(([], 0))
    return out


def phaseB0(K, S):
    with ExitStack() as ph:
        vall = sbt(K, ph, [128, 34, 520], BF16)
        S.D("sp", vall.t[:, :, :], K.VT.rearrange("b p e -> p b e"), K.R_VT, [vall.r])
        vv = vall.t[:, :, :].rearrange("p b (h e) -> p b h e", e=65)
        qjs = sbrot(K, ph, 2, [128, NT], BF16)
        kjs = sbrot(K, ph, 2, [128, NT], BF16)
        nbs = sbrot(K, ph, 2, [128, 21, 128], F32)
        otok = sbt(K, ph, [128, 34, 512], BF16)
        tmps = sbrot(K, ph, 3, [128, 640], F32)
        pTs = sbrot(K, ph, 4, [128, 896], BF16)
        rcs = sbrot(K, ph, 3, [128, 1], F32)
        blocks = na_blocks()
        spairs = Rot([0, 1, 2])
        pobanks = Rot([6, 7])
        its = []
        for jp in range(4):
            for hh in range(2):
                for i in range(34):
                    its.append((jp, hh, i))
        N = len(its)
        ctxs = [None] * N
        cur = {}

        def stA(k):
            jp, hh, i = its[k]
            if hh == 0 and i == 0:
                cur["qj"], cur["kj"] = qjs.next(), kjs.next()
                S.D("sp", cur["qj"].t[:, :], K.QT[:, jp, :], K.R_QT, [cur["qj"].r])
                S.D("sp", cur["kj"].t[:, :], K.KT[:, jp, :], K.R_KT, [cur["kj"].r])
            if i == 0:
                cur["nb"] = nbs.next()
                S.D("sp", cur["nb"].t[:, :, :], K.inp["nab"][2 * jp + hh], [], [cur["nb"].r])
            qj, kj, nb = cur["qj"], cur["kj"], cur["nb"]
            hb = hh * 64
            chunks, t0 = blocks[i]
            allc = chunks + [32, 33]
            pi = spairs.next()
            Sp = K.PB[pi].t[:, :]
            rS = [K.RB[2 * pi], K.RB[2 * pi + 1]]
            for ci, m in enumerate(allc):
                S.I("pe", "matmul", [qj.r, kj.r], rS, Sp[:, ci * 128:(ci + 1) * 128], lhsT=kj.t[hb:hb + 64, m * 128:(m + 1) * 128],
                    rhs=qj.t[hb:hb + 64, i * 128:(i + 1) * 128], start=True, stop=True)
            ctxs[k] = dict(Sp=Sp, rS=rS, nb=nb, nnb=len(chunks), t0=t0, allc=allc, h=2 * jp + hh, i=i)

        def stB(k):
            c = ctxs[k]
            nnb, Sp, rS = c["nnb"], c["Sp"], c["rS"]
            pT = pTs.next()
            c["pT"] = pT
            if nnb:
                tm = tmps.next()
                S.I("dve", "tensor_tensor", rS + [c["nb"].r], [tm.r], out=tm.t[:, :nnb * 128].rearrange("p (a b) -> p a b", b=128),
                    in0=Sp[:, :nnb * 128].rearrange("p (a b) -> p a b", b=128), in1=c["nb"].t[:, c["t0"]:c["t0"] + nnb, :], op=ALU.add)
                S.I("act", "activation", [tm.r], [pT.r], out=pT.t[:, :nnb * 128], in_=tm.t[:, :nnb * 128], func=AF.Exp)
            S.I("act", "activation", rS, [pT.r], out=pT.t[:, nnb * 128:(nnb + 2) * 128], in_=Sp[:, nnb * 128:(nnb + 2) * 128], func=AF.Exp)

        def stC(k):
            c = ctxs[k]
            po, rpo = bank_fixed(K, pobanks.next())
            c["po"], c["rpo"] = po, rpo
            pT, allc = c["pT"], c["allc"]
            for ci, m in enumerate(allc):
                S.I("pe", "matmul", [pT.r, vall.r], [rpo], po[:, 0:65], lhsT=pT.t[:, ci * 128:(ci + 1) * 128], rhs=vv[:, m, c["h"], :],
                    start=(ci == 0), stop=(ci == len(allc) - 1))

        def stD(k):
            c = ctxs[k]
            po, rpo, h, i = c["po"], c["rpo"], c["h"], c["i"]
            rc = rcs.next()
            S.I("dve", "reciprocal", [rpo], [rc.r], out=rc.t[:, 0:1], in_=po[:, 64:65])
            S.I("dve", "tensor_scalar", [rpo, rc.r], [otok.r], out=otok.t[:, i, h * 64:(h + 1) * 64], in0=po[:, 0:64], scalar1=rc.t[:, 0:1],
                scalar2=None, op0=ALU.mult)
            ctxs[k] = None
        for step in range(N + 3):
            if step < N:
                stA(step)
            if 0 <= step - 3 < N:
                stD(step - 3)
            if 0 <= step - 1 < N:
                stB(step - 1)
            if 0 <= step - 2 < N:
                stC(step - 2)
        K.bank_rr = 0
        bos = sbrot(K, ph, 2, [128, 4, 128], BF16)
        for i in range(34):
            pt, rpt = bank(K)
            for fc in range(4):
                S.I("pe", "matmul", [otok.r, K.ident_bf.r], [rpt], pt[:, fc * 128:(fc + 1) * 128], lhsT=otok.t[:, i, fc * 128:(fc + 1) * 128],
                    rhs=K.ident_bf.t[:, :], start=True, stop=True)
            bo = bos.next()
            bv = bo.t[:, :, :].rearrange("p a b -> p (a b)")
            if i % 2:
                S.I("act", "copy", [rpt], [bo.r], out=bv, in_=pt)
            else:
                S.I("dve", "tensor_copy", [rpt], [bo.r], out=bv, in_=pt)
            S.D("sp", K.MIX[:, 4:8, i * 128:(i + 1) * 128], bo.t[:, :, :], [bo.r], [K.R_MIXb[i]])
        S.flush()


def phaseC(K, S, l):
    M = K.mods[l]
    ntiles = 9 if l == 0 else 8
    last = (l == 1)
    with ExitStack() as ph:
        wout = sbt(K, ph, [128, 8, 1024], BF16)
        wv = K.inp["ab_w_out" if l == 0 else "cd_w_out"].rearrange("(kc p) n -> p kc n", p=128)
        for kc in range(KC):
            S.D("pool", wout.t[:, kc, :], wv[:, kc, :], [], [wout.r])
        wdr = sbt(K, ph, [128, NJ, 1024], BF16)
        wdv = K.inp["ffn_wd"][l].rearrange("(j p) n -> p j n", p=128)
        wdres = [Res() for _ in range(NJ)]
        for jj in range(NJ):
            S.D("pool", wdr.t[:, jj, :], wdv[:, jj, :], [], [wdres[jj]])
        xts = sbrot(K, ph, 2, [128, 8, 512], F32)
        mxs = sbrot(K, ph, 2, [128, 8, 512], BF16)
        sq = sbt(K, ph, [128, 8, 512], BF16)
        rstd = sbt(K, ph, [128, 512], F32)
        tmps = sbrot(K, ph, 2, [128, 512], F32)
        h2 = sbt(K, ph, [128, 8, 512], BF16)
        aT = sbt(K, ph, [128, NJ, 512], BF16)
        wgs = sbrot(K, ph, 2, [128, 8, 256], BF16)
        wus = sbrot(K, ph, 2, [128, 8, 256], BF16)
        sgs = sbrot(K, ph, 2, [128, 512], F32)
        if last:
            fnw = sbt(K, ph, [128, 1024], F32)
            S.D("sp", fnw.t[:, :], K.inp["fnw_bc"], [], [fnw.r])
            ots = sbrot(K, ph, 2, [128, 1024], F32)
            junk = sbt(K, ph, [128, 1024], BF16)
            sss = sbrot(K, ph, 2, [128, 1], F32)
            rss = sbrot(K, ph, 2, [128, 1], F32)
        wgv = [K.WGb[l][m].rearrange("(kc p) n -> p kc n", p=128) for m in range(2)]
        def cload(t):
            n = 512 if t < 8 else 256
            t0 = t * 512
            xt, mx = xts.next(), mxs.next()
            S.D("sp", xt.t[:, :, :n], K.XT[:, :, t0:t0 + n], blkres(K.R_XT, t0, n), [xt.r])
            S.D("sp", mx.t[:, :, :n], K.MIX[:, :, t0:t0 + n], blkres(K.R_MIXa, t0, n) + blkres(K.R_MIXb, t0, n), [mx.r])
            return xt, mx
        nxt = cload(0)
        for t in range(ntiles):
            n = 512 if t < 8 else 256
            t0 = t * 512
            j = 0 if t < 8 else 1
            xt, mx = nxt
            for mo in range(8):
                ps, rps = bank(K)
                for kc in range(KC):
                    S.I("pe", "matmul", [wout.r, mx.r], [rps], ps[:, :n], lhsT=wout.t[:, kc, mo * 128:(mo + 1) * 128], rhs=mx.t[:, kc, :n],
                        start=(kc == 0), stop=(kc == KC - 1))
                S.I("dve", "scalar_tensor_tensor", [rps, xt.r, K.modr], [xt.r], out=xt.t[:, mo, :n], in0=ps[:, :n], scalar=M["Gm"].t[:, mo, j:j + 1],
                    in1=xt.t[:, mo, :n], op0=ALU.mult, op1=ALU.add)
            norm_mod(K, S, xt, n, M["Af"], M["Bf"], j, h2, tmps, sq, rstd)
            if t + 1 < ntiles:
                nxt = cload(t + 1)
            for pc in range(11):
                wg, wu = wgs.next(), wus.next()
                S.D("sp", wg.t[:, :, :], wgv[0][:, :, pc * 256:(pc + 1) * 256], K.R_WG[l][0], [wg.r])
                S.D("sp", wu.t[:, :, :], wgv[1][:, :, pc * 256:(pc + 1) * 256], K.R_WG[l][1], [wu.r])
                for q in range(2):
                    jj = pc * 2 + q
                    pg, rpg = bank(K)
                    pu, rpu = bank(K)
                    for kc in range(KC):
                        S.I("pe", "matmul", [wg.r, h2.r], [rpg], pg[:, :n], lhsT=wg.t[:, kc, q * 128:(q + 1) * 128], rhs=h2.t[:, kc, :n], start=(kc == 0), stop=(kc == KC - 1))
                    for kc in range(KC):
                        S.I("pe", "matmul", [wu.r, h2.r], [rpu], pu[:, :n], lhsT=wu.t[:, kc, q * 128:(q + 1) * 128], rhs=h2.t[:, kc, :n], start=(kc == 0), stop=(kc == KC - 1))
                    sg = sgs.next()
                    S.I("act", "activation", [rpg], [sg.r], out=sg.t[:, :n], in_=pg[:, :n], func=AF.Silu)
                    S.I("dve", "tensor_tensor", [sg.r, rpu], [aT.r], out=aT.t[:, jj, :n], in0=sg.t[:, :n], in1=pu[:, :n], op=ALU.mult)
            for mo in range(8):
                ps, rps = bank(K)
                for jj in range(NJ):
                    S.I("pe", "matmul", [wdres[jj], aT.r], [rps], ps[:, :n], lhsT=wdr.t[:, jj, mo * 128:(mo + 1) * 128], rhs=aT.t[:, jj, :n], start=(jj == 0), stop=(jj == NJ - 1))
                S.I("dve", "scalar_tensor_tensor", [rps, xt.r, K.modr], [xt.r], out=xt.t[:, mo, :n], in0=ps[:, :n], scalar=M["Gf"].t[:, mo, j:j + 1],
                    in1=xt.t[:, mo, :n], op0=ALU.mult, op1=ALU.add)
            if not last:
                S.D("sp", K.XT[:, :, t0:t0 + n], xt.t[:, :, :n], [xt.r], blkres(K.R_XT, t0, n))
            else:
                for sbk in range(4):
                    blk = t * 4 + sbk
                    pt, rpt = bank2(K)
                    for kc in range(KC):
                        S.I("pe", "transpose", [xt.r, K.ident.r], rpt, out=pt[:, kc * 128:(kc + 1) * 128], in_=xt.t[:, kc, sbk * 128:(sbk + 1) * 128], identity=K.ident.t[:, :])
                    ss, rs, ot = sss.next(), rss.next(), ots.next()
                    S.I("dve", "memset", [], [ss.r], ss.t[:, :], 0.0)
                    S.I("act", "activation", rpt + [ss.r], [junk.r, ss.r], out=junk.t[:, :], in_=pt, func=AF.Square, accum_out=ss.t[:, 0:1])
                    small_rstd(K, S, ss, rs, 1024)
                    S.I("dve", "scalar_tensor_tensor", rpt + [rs.r, fnw.r], [ot.r], out=ot.t[:, :], in0=pt, scalar=rs.t[:, 0:1], in1=fnw.t[:, :],
                        op0=ALU.mult, op1=ALU.mult)
                    S.D("sp", K.out[blk * 128:(blk + 1) * 128, :], ot.t[:, :], [ot.r], [K.R_out])
        S.flush()


def phaseD(K, S):
    with ExitStack() as ph:
        fnw = sbt(K, ph, [128, 1024], F32)
        S.D("sp", fnw.t[:, :], K.inp["fnw_bc"], [], [fnw.r])
        xbs = sbrot(K, ph, 2, [128, 8, 128], F32)
        ots = sbrot(K, ph, 2, [128, 1024], F32)
        junk = sbt(K, ph, [128, 1024], BF16)
        sss = sbrot(K, ph, 2, [128, 1], F32)
        rss = sbrot(K, ph, 2, [128, 1], F32)
        for blk in range(32):
            xb = xbs.next()
            S.D("sp", xb.t[:, :, :], K.XT[:, :, blk * 128:(blk + 1) * 128], [K.R_XT[blk]], [xb.r])
            pt, rpt = bank2(K)
            for kc in range(KC):
                S.I("pe", "transpose", [xb.r, K.ident.r], rpt, out=pt[:, kc * 128:(kc + 1) * 128], in_=xb.t[:, kc, :], identity=K.ident.t[:, :])
            ss, rs, ot = sss.next(), rss.next(), ots.next()
            S.I("dve", "memset", [], [ss.r], ss.t[:, :], 0.0)
            S.I("act", "activation", rpt + [ss.r], [junk.r, ss.r], out=junk.t[:, :], in_=pt, func=AF.Square, accum_out=ss.t[:, 0:1])
            small_rstd(K, S, ss, rs, 1024)
            S.I("dve", "scalar_tensor_tensor", rpt + [rs.r, fnw.r], [ot.r], out=ot.t[:, :], in0=pt, scalar=rs.t[:, 0:1], in1=fnw.t[:, :],
                op0=ALU.mult, op1=ALU.mult)
            S.D("sp", K.out[blk * 128:(blk + 1) * 128, :], ot.t[:, :], [ot.r], [K.R_out])
        S.flush()


def l1_decl(K, din):
    din("cd_w_in", [D, 2080]); din("cd_w_perm", [D, 512]); din("decw", [2, 17, 256]); din("hnw", [128, 1])
    din("tri", [128, 4, 128]); din("gmask", [128, 2, 128])
    din("ropeC", [128, NX]); din("ropeS", [128, NX]); din("ropeCt", [32, 128, 256]); din("ropeSt", [32, 128, 256])
    din("w128ri", [128, 256]); din("wc1", [128, 256]); din("wc2", [128, 256]); din("tw", [128, 32, 2, 128])


def l1_scratch(K, scr):
    K.HT = scr("HT", [128, 8, NX], BF16)
    K.GQ = scr("GQ", [128, 2, NT], BF16)
    K.GK = scr("GK", [128, 2, NT], BF16)
    K.KTOK = scr("KTOK", [34, 128, 256], BF16)
    K.VTOK1 = scr("VTOK1", [34, 128, 512], BF16)
    K.SG = scr("SG", [128, 4, NX], BF16)
    K.ALR = scr("ALR", [2, 16, NT], F32)
    K.QE = [scr("QE%d" % d, [128, 2, NX], BF16) for d in range(2)]
    K.KE = [scr("KE%d" % d, [128, 2, NX], BF16) for d in range(2)]
    for nm in ("HT", "GQ", "GK", "KTOK", "VTOK1", "SG", "ALR", "QE0", "QE1", "KE0", "KE1"):
        setattr(K, "R_" + nm, [Res() for _ in range(34)])


def phaseA1(K, S):
    M = K.mods[1]
    with ExitStack() as ph:
        win = sbt(K, ph, [128, 8, 2080], BF16)
        wv = K.inp["cd_w_in"].rearrange("(kc p) n -> p kc n", p=128)
        for kc in range(KC):
            S.D("pool", win.t[:, kc, :], wv[:, kc, :], [], [win.r])
        wpm = sbt(K, ph, [128, 8, 512], BF16)
        S.D("pool", wpm.t[:, :, :], K.inp["cd_w_perm"].rearrange("(kc p) n -> p kc n", p=128), [], [wpm.r])
        xts = sbrot(K, ph, 2, [128, 8, 512], F32)
        sqs = sbrot(K, ph, 2, [128, 8, 512], BF16)
        rstds = sbrot(K, ph, 2, [128, 512], F32)
        tmps = sbrot(K, ph, 2, [128, 512], F32)
        hTs = sbrot(K, ph, 2, [128, 8, 512], BF16)
        rcs = sbrot(K, ph, 2, [128, 512], F32)
        rss = sbrot(K, ph, 2, [128, 512], F32)
        t1s = sbrot(K, ph, 2, [128, 512], F32)
        t2s = sbrot(K, ph, 2, [128, 512], F32)
        qos = sbrot(K, ph, 2, [128, 2, 512], BF16)
        kos = sbrot(K, ph, 2, [128, 2, 512], BF16)
        sgs = sbrot(K, ph, 2, [128, 4, 512], BF16)
        als = sbrot(K, ph, 2, [16, 2, 512], F32)
        vos = sbrot(K, ph, 2, [128, 512], BF16)
        cts = sbrot(K, ph, 2, [128, 256], F32)
        sts = sbrot(K, ph, 2, [128, 256], F32)
        u1s = sbrot(K, ph, 2, [128, 256], F32)
        u2s = sbrot(K, ph, 2, [128, 256], F32)
        ktos = sbrot(K, ph, 2, [128, 256], BF16)
        def tile(t):
            n = 512 if t < 8 else 256
            t0 = t * 512
            isx = t < 8
            j = 0 if isx else 1
            xt = xts.next()
            hT = hTs.next()
            sq, rstd = sqs.next(), rstds.next()
            S.D("sp", xt.t[:, :, :n], K.XT[:, :, t0:t0 + n], blkres(K.R_XT, t0, n), [xt.r])
            norm_mod(K, S, xt, n, M["Am"], M["Bm"], j, hT, tmps, sq, rstd)
            yield
            if isx:
                S.D("sp", K.HT[:, :, t0:t0 + n], hT.t[:, :, :n], [hT.r], blkres(K.R_HT, t0, n))
                rc, rs = rcs.next(), rss.next()
                S.D("sp", rc.t[:, :], K.inp["ropeC"][:, t0:t0 + n], [], [rc.r])
                S.D("sp", rs.t[:, :], K.inp["ropeS"][:, t0:t0 + n], [], [rs.r])
            qo, ko = qos.next(), kos.next()
            for c in (range(4) if isx else (2, 3)):
                p1, rp1 = bank(K)
                for kc in range(KC):
                    S.I("pe", "matmul", [win.r, hT.r], [rp1], p1[:, :n], lhsT=win.t[:, kc, 512 + c * 128:512 + (c + 1) * 128], rhs=hT.t[:, kc, :n],
                        start=(kc == 0), stop=(kc == KC - 1))
                dst = qo if c < 2 else ko
                if not isx:
                    S.I("act", "copy", [rp1], [dst.r], out=dst.t[:, c % 2, :n], in_=p1[:, :n])
                    continue
                p2, rp2 = bank(K)
                for kc in range(KC):
                    S.I("pe", "matmul", [wpm.r, hT.r], [rp2], p2[:, :n], lhsT=wpm.t[:, kc, c * 128:(c + 1) * 128], rhs=hT.t[:, kc, :n],
                        start=(kc == 0), stop=(kc == KC - 1))
                sc = 0.125 if c < 2 else 1.0
                t1, t2 = t1s.next(), t2s.next()
                S.I("dve", "scalar_tensor_tensor", [rp1, rc.r], [t1.r], out=t1.t[:, :n], in0=p1[:, :n], scalar=sc, in1=rc.t[:, :n], op0=ALU.mult, op1=ALU.mult)
                S.I("dve", "scalar_tensor_tensor", [rp2, rs.r], [t2.r], out=t2.t[:, :n], in0=p2[:, :n], scalar=sc, in1=rs.t[:, :n], op0=ALU.mult, op1=ALU.mult)
                S.I("pool", "tensor_tensor", [t1.r, t2.r], [dst.r], out=dst.t[:, c % 2, :n], in0=t1.t[:, :n], in1=t2.t[:, :n], op=ALU.add)
            if isx:
                S.D("sp", K.GQ[:, :, t0:t0 + n], qo.t[:, :, :n], [qo.r], blkres(K.R_GQ, t0, n))
            S.D("sp", K.GK[:, :, t0:t0 + n], ko.t[:, :, :n], [ko.r], blkres(K.R_GK, t0, n))
            if isx:
                sg = sgs.next()
                for c in range(4):
                    p1, rp1 = bank(K)
                    for kc in range(KC):
                        S.I("pe", "matmul", [win.r, hT.r], [rp1], p1[:, :n], lhsT=win.t[:, kc, 1536 + c * 128:1536 + (c + 1) * 128], rhs=hT.t[:, kc, :n],
                            start=(kc == 0), stop=(kc == KC - 1))
                    S.I("act", "activation", [rp1], [sg.r], out=sg.t[:, c, :n], in_=p1[:, :n], func=AF.Silu)
                S.D("sp", K.SG[:, :, t0:t0 + n], sg.t[:, :, :n], [sg.r], blkres(K.R_SG, t0, n))
            al = als.next()
            for d in range(2):
                p1, rp1 = bank(K)
                for kc in range(KC):
                    S.I("pe", "matmul", [win.r, hT.r], [rp1], p1[0:16, :n], lhsT=win.t[:, kc, 2048 + d * 16:2064 + d * 16], rhs=hT.t[:, kc, :n],
                        start=(kc == 0), stop=(kc == KC - 1))
                S.I("dve", "tensor_copy", [rp1], [al.r], out=al.t[:, d, :n], in_=p1[0:16, :n])
            for d in range(2):
                S.D("sp", K.ALR[d, :, t0:t0 + n], al.t[:, d, :n], [al.r], blkres(K.R_ALR, t0, n))
            yield
            for sbk in range(n // 128):
                blk = t * 4 + sbk
                tk = slice(sbk * 128, (sbk + 1) * 128)
                pv, rpv = bank(K)
                for kc in range(KC):
                    S.I("pe", "matmul", [win.r, hT.r], [rpv], pv, lhsT=hT.t[:, kc, tk], rhs=win.t[:, kc, 1024:1536], start=(kc == 0), stop=(kc == KC - 1))
                vo = vos.next()
                S.I("act", "copy", [rpv], [vo.r], out=vo.t[:, :], in_=pv)
                S.D("sp", K.VTOK1[blk], vo.t[:, :], [vo.r], [K.R_VTOK1[blk]])
                pk, rpk = bank(K)
                for kc in range(KC):
                    S.I("pe", "matmul", [win.r, hT.r], [rpk], pk[:, 0:256], lhsT=hT.t[:, kc, tk], rhs=win.t[:, kc, 768:1024], start=(kc == 0), stop=(kc == KC - 1))
                kto = ktos.next()
                if isx:
                    for kc in range(KC):
                        S.I("pe", "matmul", [wpm.r, hT.r], [rpk], pk[:, 256:512], lhsT=hT.t[:, kc, tk], rhs=wpm.t[:, kc, 256:512], start=(kc == 0), stop=(kc == KC - 1))
                    ct, st, u1, u2 = cts.next(), sts.next(), u1s.next(), u2s.next()
                    S.D("sp", ct.t[:, :], K.inp["ropeCt"][blk], [], [ct.r])
                    S.D("sp", st.t[:, :], K.inp["ropeSt"][blk], [], [st.r])
                    S.I("dve", "tensor_tensor", [rpk, ct.r], [u1.r], out=u1.t[:, :], in0=pk[:, 0:256], in1=ct.t[:, :], op=ALU.mult)
                    S.I("dve", "tensor_tensor", [rpk, st.r], [u2.r], out=u2.t[:, :], in0=pk[:, 256:512], in1=st.t[:, :], op=ALU.mult)
                    S.I("pool", "tensor_tensor", [u1.r, u2.r], [kto.r], out=kto.t[:, :], in0=u1.t[:, :], in1=u2.t[:, :], op=ALU.add)
                else:
                    S.I("dve", "tensor_copy", [rpk], [kto.r], out=kto.t[:, :], in_=pk[:, 0:256])
                S.D("sp", K.KTOK[blk], kto.t[:, :], [kto.r], [K.R_KTOK[blk]])
        gens = [tile(t) for t in range(9)]
        next(gens[0])
        for t in range(9):
            next(gens[t])
            if t + 1 < 9:
                next(gens[t + 1])
            for _ in gens[t]:
                pass
        S.flush()


def phaseG(K, S):
    with ExitStack() as ph:
        tri = sbt(K, ph, [128, 4, 128], BF16)
        S.D("pool", tri.t[:, :, :], K.inp["tri"], [], [tri.r])
        gmask = sbt(K, ph, [128, 2, 128], F32)
        S.D("sp", gmask.t[:, :, :], K.inp["gmask"], [], [gmask.r])
        decw = [sbt(K, ph, [17, 256], F32) for _ in range(2)]
        for d in range(2):
            S.D("sp", decw[d].t[:, :], K.inp["decw"][d], [], [decw[d].r])
        hnw = sbt(K, ph, [128, 1], F32)
        S.D("sp", hnw.t[:, :], K.inp["hnw"], [], [hnw.r])
        vtk = sbt(K, ph, [128, 34, 512], BF16)
        S.D("sp", vtk.t[:, :, :], K.VTOK1.rearrange("b p e -> p b e"), K.R_VTOK1, [vtk.r])
        ktk = sbt(K, ph, [128, 34, 256], BF16)
        S.D("sp", ktk.t[:, :, :], K.KTOK.rearrange("b p e -> p b e"), K.R_KTOK, [ktk.r])
        sst = sbt(K, ph, [128, 2 * 64 * 2, 128], BF16)
        scurs = [sbt(K, ph, [128, 2, 128], F32) for _ in range(2)]
        for sc_ in scurs:
            S.I("dve", "memset", [], [sc_.r], sc_.t[:, :, :], 0.0)
        p1 = ph
        if True:
            alrs = sbrot(K, p1, 5, [17, 128], F32)
            for a in alrs.items:
                S.I("pool", "memset", [], [a.r], a.t[:, :], 1.0)
            e1s = sbrot(K, p1, 2, [128, 256], F32)
            Ls = sbrot(K, p1, 3, [128, 256], BF16)
            ebs = sbrot(K, p1, 3, [128, 2, 128], F32)
            enbs = sbrot(K, p1, 2, [128, 2, 128], F32)
            gqs = sbrot(K, p1, 6, [128, 2, 128], BF16)
            gks = sbrot(K, p1, 6, [128, 2, 128], BF16)
            qes = sbrot(K, p1, 3, [128, 2, 128], BF16)
            kes = sbrot(K, p1, 3, [128, 2, 128], BF16)
            edss = sbrot(K, p1, 2, [128, 256], F32)
            kds = sbrot(K, p1, 3, [128, 256], BF16)
            seq = []
            for d in range(2):
                order = [32, 33] + list(range(32)) if d == 0 else [33, 32] + list(range(31, -1, -1))
                for blk in order:
                    seq.append((d, blk))
            NS = len(seq)
            cx = [None] * NS
            def next_slot(d, blk, c):
                cs = (0, 1) if d == 0 else (1, 0)
                order = [32, 33] + list(range(32)) if d == 0 else [33, 32] + list(range(31, -1, -1))
                chunks = [(bk, cc) for bk in order for cc in cs]
                i = chunks.index((blk, c))
                if i + 1 >= len(chunks):
                    return None
                nb_, nc_ = chunks[i + 1]
                if nb_ >= 32:
                    return None
                return (d * 64 + nb_ * 2 + nc_) * 2

            pre = [None] * NS

            def P0(k):
                d, blk = seq[k]
                tok0 = blk * 128
                al = alrs.next()
                S.D("sp", al.t[0:16, :], K.ALR[d, :, tok0:tok0 + 128], [K.R_ALR[blk]], [al.r])
                gq = gk = None
                if blk < 32:
                    gq, gk = gqs.next(), gks.next()
                    S.D("sp", gq.t[:, :, :], K.GQ[:, :, tok0:tok0 + 128], [K.R_GQ[blk]], [gq.r])
                    S.D("sp", gk.t[:, :, :], K.GK[:, :, tok0:tok0 + 128], [K.R_GK[blk]], [gk.r])
                pre[k] = (al, gq, gk)

            def P1(k):
                d, blk = seq[k]
                tok0 = blk * 128
                al = pre[k][0]
                z, rz = bank(K)
                S.I("pe", "matmul", [al.r, decw[d].r], [rz], z[:, 0:256], lhsT=al.t[0:17, :], rhs=decw[d].t[0:17, :], start=True, stop=True)
                e1, L = e1s.next(), Ls.next()
                S.I("act", "activation", [rz], [e1.r], out=e1.t[:, :], in_=z[:, 0:256], func=AF.Exp, scale=-1.0)
                S.I("act", "activation", [e1.r], [L.r], out=L.t[:, :], in_=e1.t[:, :], func=AF.Ln, bias=1.0)
                cx[k] = dict(L=L)

            def P2(k):
                d, blk = seq[k]
                tok0 = blk * 128
                isx = blk < 32
                L = cx[k]["L"]
                bT, rbT = bank(K)
                for hp in range(2):
                    S.I("pe", "matmul", [L.r, tri.r], [rbT], bT[:, hp * 128:(hp + 1) * 128], lhsT=L.t[:, hp * 128:(hp + 1) * 128], rhs=tri.t[:, 2 * d, :],
                        start=True, stop=True)
                ds, rds = bank(K)
                S.I("pe", "matmul", [L.r, tri.r], [rds], ds[:, 0:256], lhsT=tri.t[:, 2 * d + 1, :], rhs=L.t[:, :], start=True, stop=True)
                eb = ebs.next()
                S.I("act", "activation", [rbT], [eb.r], out=eb.t[:, :, :].rearrange("p a b -> p (a b)"), in_=bT[:, 0:256], func=AF.Exp)
                eds, kd = edss.next(), kds.next()
                S.I("act", "activation", [rds], [eds.r], out=eds.t[:, :], in_=ds[:, 0:256], func=AF.Exp)
                S.I("dve", "tensor_tensor", [ktk.r, eds.r], [kd.r], out=kd.t[:, :], in0=ktk.t[:, blk, :], in1=eds.t[:, :], op=ALU.mult)
                if isx:
                    enb, qe, ke = enbs.next(), qes.next(), kes.next()
                    gq, gk = pre[k][1], pre[k][2]
                    S.I("act", "activation", [rbT], [enb.r], out=enb.t[:, :, :].rearrange("p a b -> p (a b)"), in_=bT[:, 0:256], func=AF.Exp, scale=-1.0)
                    S.I("dve", "tensor_tensor", [gq.r, eb.r], [qe.r], out=qe.t[:, :, :], in0=gq.t[:, :, :], in1=eb.t[:, :, :], op=ALU.mult)
                    S.I("pool", "tensor_tensor", [gk.r, enb.r], [ke.r], out=ke.t[:, :, :], in0=gk.t[:, :, :], in1=enb.t[:, :, :], op=ALU.mult)
                    S.D("sp", K.QE[d][:, :, tok0:tok0 + 128], qe.t[:, :, :], [qe.r], [getattr(K, "R_QE%d" % d)[blk]])
                    S.D("sp", K.KE[d][:, :, tok0:tok0 + 128], ke.t[:, :, :], [ke.r], [getattr(K, "R_KE%d" % d)[blk]])
                cx[k]["eb"] = eb
                cx[k]["kd"] = kd

            def P3(k):
                d, blk = seq[k]
                eb, kd = cx[k]["eb"], cx[k]["kd"]
                scur = scurs[d]
                for c in ((0, 1) if d == 0 else (1, 0)):
                    cs = slice(c * 64, (c + 1) * 64)
                    kv, rkv = bank(K)
                    for hp in range(2):
                        for hh in range(2):
                            h = 2 * hp + hh
                            S.I("pe", "matmul", [kd.r, vtk.r], [rkv], kv[hh * 64:(hh + 1) * 64, hp * 128:(hp + 1) * 128], lhsT=kd.t[cs, h * 64:(h + 1) * 64],
                                rhs=vtk.t[cs, blk, h * 128:(h + 1) * 128], start=True, stop=True)
                    col = c * 64 + 63 if d == 0 else c * 64
                    slot = next_slot(d, blk, c)
                    for hp in range(2):
                        if slot is not None:
                            S.I("dve", "scalar_tensor_tensor", [scur.r, eb.r, rkv], [sst.r], out=sst.t[:, slot + hp, :], in0=scur.t[:, hp, :],
                                scalar=eb.t[:, hp, col:col + 1], in1=kv[:, hp * 128:(hp + 1) * 128], op0=ALU.mult, op1=ALU.add)
                        S.I("dve", "scalar_tensor_tensor", [scur.r, eb.r, rkv], [scur.r], out=scur.t[:, hp, :], in0=scur.t[:, hp, :],
                            scalar=eb.t[:, hp, col:col + 1], in1=kv[:, hp * 128:(hp + 1) * 128], op0=ALU.mult, op1=ALU.add)
                cx[k] = None
            P0(0)
            P0(1)
            for st in range(NS + 2):
                if st + 2 < NS:
                    P0(st + 2)
                if st < NS:
                    P1(st)
                if 0 <= st - 2 < NS:
                    P3(st - 2)
                if 0 <= st - 1 < NS:
                    P2(st - 1)
        K.bank_pool = [0, 1, 2, 3, 4, 5]
        K.bank_rr = 0
        p2 = ph
        if True:
            qets = [sbrot(K, p2, 2, [128, 2, 512], BF16) for _ in range(2)]
            kets = [sbrot(K, p2, 2, [128, 2, 512], BF16) for _ in range(2)]
            sgs = sbrot(K, p2, 2, [128, 4, 512], BF16)
            gos = sbrot(K, p2, 2, [128, 4, 512], BF16)
            atms = sbrot(K, p2, 4, [128, 128], BF16)
            sqo = sbt(K, p2, [128, 512], BF16)
            rst = sbt(K, p2, [128, 512], F32)
            t1 = sbt(K, p2, [128, 512], F32)
            nacc = 0
            def g2load(T):
                t0 = T * 512
                qe = [qets[d].next() for d in range(2)]
                ke = [kets[d].next() for d in range(2)]
                for d in range(2):
                    S.D("sp", qe[d].t[:, :, :], K.QE[d][:, :, t0:t0 + 512], blkres(getattr(K, "R_QE%d" % d), t0, 512), [qe[d].r])
                    S.D("sp", ke[d].t[:, :, :], K.KE[d][:, :, t0:t0 + 512], blkres(getattr(K, "R_KE%d" % d), t0, 512), [ke[d].r])
                sg = sgs.next()
                S.D("sp", sg.t[:, :, :], K.SG[:, :, t0:t0 + 512], blkres(K.R_SG, t0, 512), [sg.r])
                return qe, ke, sg
            nxt = g2load(0)
            for T in range(8):
                t0 = T * 512
                qe, ke, sg = nxt
                if T + 1 < 8:
                    nxt = g2load(T + 1)
                go = gos.next()
                for h in range(4):
                    hp, hb = h // 2, (h % 2) * 64
                    oT, roT = bank_fixed(K, 6 + (nacc % 2))
                    nacc += 1
                    for bb in range(4):
                        blk = T * 4 + bb
                        cols = slice(bb * 128, (bb + 1) * 128)
                        atm = []
                        for d in range(2):
                            att, ratt = bank(K)
                            S.I("pe", "matmul", [ke[d].r, qe[d].r], [ratt], att[:, 0:128], lhsT=ke[d].t[hb:hb + 64, hp, cols], rhs=qe[d].t[hb:hb + 64, hp, cols],
                                start=True, stop=True)
                            am = atms.next()
                            S.I("dve", "tensor_tensor", [ratt, gmask.r], [am.r], out=am.t[:, :], in0=att[:, 0:128], in1=gmask.t[:, d, :], op=ALU.mult)
                            atm.append(am)
                        S.I("pe", "matmul", [vtk.r, atm[0].r], [roT], oT[:, cols], lhsT=vtk.t[:, blk, h * 128:(h + 1) * 128], rhs=atm[0].t[:, :], start=True, stop=False)
                        S.I("pe", "matmul", [vtk.r, atm[1].r], [roT], oT[:, cols], lhsT=vtk.t[:, blk, h * 128:(h + 1) * 128], rhs=atm[1].t[:, :], start=False, stop=False)
                        for c in range(2):
                            for d in range(2):
                                nch = blk * 2 + c
                                base = (d * 64 + nch) * 2 + hp
                                cc = slice(bb * 128 + c * 64, bb * 128 + (c + 1) * 64)
                                S.I("pe", "matmul", [sst.r, qe[d].r], [roT], oT[:, cc], lhsT=sst.t[hb:hb + 64, base, :], rhs=qe[d].t[hb:hb + 64, hp, cc],
                                    start=False, stop=(c == 1 and d == 1))
                    S.I("act", "activation", [roT], [sqo.r], out=sqo.t[:, :], in_=oT, func=AF.Square)
                    ms, rms = bank(K)
                    S.I("pe", "matmul", [K.ones_bf.r, sqo.r], [rms], ms, lhsT=K.ones_bf.t[:, :], rhs=sqo.t[:, :], start=True, stop=True)
                    S.I("act", "activation", [rms], [rst.r], out=rst.t[:, :], in_=ms, func=AF.Ln, scale=1.0 / 128, bias=K.epsc.t[:, 0:1])
                    S.I("act", "activation", [rst.r], [rst.r], out=rst.t[:, :], in_=rst.t[:, :], func=AF.Exp, scale=-0.5)
                    S.I("dve", "tensor_tensor", [roT, rst.r], [t1.r], out=t1.t[:, :], in0=oT, in1=rst.t[:, :], op=ALU.mult)
                    S.I("dve", "scalar_tensor_tensor", [t1.r, hnw.r, sg.r], [go.r], out=go.t[:, h, :], in0=t1.t[:, :], scalar=hnw.t[:, 0:1], in1=sg.t[:, h, :],
                        op0=ALU.mult, op1=ALU.mult)
                S.D("sp", K.MIX[:, 4:8, t0:t0 + 512], go.t[:, :, :], [go.r], blkres(K.R_MIXb, t0, 512))
        K.bank_pool = None
        K.bank_rr = 0
        S.flush()


def phaseF(K, S):
    with ExitStack() as ph:
        X = sbt(K, ph, [128, 32768], BF16)
        Y = sbt(K, ph, [128, 32768], BF16)
        wf = sbt(K, ph, [128, 8, 512], BF16)
        S.D("pool", wf.t[:, :, :], K.inp["cd_w_in"].rearrange("(kc p) n -> p kc n", p=128)[:, :, 0:512], [], [wf.r])
        w128 = sbt(K, ph, [128, 256], BF16)
        S.D("pool", w128.t[:, :], K.inp["w128ri"], [], [w128.r])
        wc = [sbt(K, ph, [128, 256], BF16) for _ in range(2)]
        S.D("pool", wc[0].t[:, :], K.inp["wc1"], [], [wc[0].r])
        S.D("pool", wc[1].t[:, :], K.inp["wc2"], [], [wc[1].r])
        tw = sbt(K, ph, [128, 32, 2, 128], BF16)
        S.D("pool", tw.t[:, :, :, :], K.inp["tw"], [], [tw.r])
        hTv = X.t[:, :].rearrange("p (k a b) -> p k a b", k=8, b=32)
        for kc in range(KC):
            S.D("sp", X.t[:, kc * 4096:(kc + 1) * 4096], K.HT[:, kc, :], K.R_HT[:32], [X.r])
        F1 = Y.t[:, 0:16384].rearrange("p (a b) -> p a b", b=512)
        nev = [0]

        def evac(reads, writes, out, in_):
            nev[0] += 1
            if nev[0] % 2:
                S.I("act", "copy", reads, writes, out=out, in_=in_)
            else:
                S.I("dve", "tensor_copy", reads, writes, out=out, in_=in_)
        for n2 in range(32):
            ps, rps = bank(K)
            for kc in range(KC):
                S.I("pe", "matmul", [X.r, wf.r], [rps], ps, lhsT=hTv[:, kc, :, n2], rhs=wf.t[:, kc, :], start=(kc == 0), stop=(kc == KC - 1))
            evac([rps], [Y.r], F1[:, n2, :], ps)
        A2 = X.t[:, :].rearrange("p (g r h n l) -> p g r h n l", g=4, r=2, h=32, n=32)
        for n2 in range(32):
            for g in range(4):
                ps, rps = bank(K)
                S.I("pe", "matmul", [Y.r, w128.r], [rps], ps[:, 0:256], lhsT=F1[:, n2, g * 128:(g + 1) * 128], rhs=w128.t[:, :], start=True, stop=True)
                evac([rps], [X.r], A2[:, g, :, :, n2, :], ps[:, 0:256].rearrange("p (r h l) -> p r h l", r=2, l=4))
        Z3 = Y.t[:, :].rearrange("p (g h r c) -> p g h r c", g=4, h=32, r=2)
        for g in range(4):
            for kh in range(32):
                ps, rps = bank(K)
                for ri in range(2):
                    off = ((g * 2 + ri) * 32 + kh) * 128
                    S.I("pe", "matmul", [X.r, wc[ri].r], [rps], ps[:, 0:256], lhsT=X.t[:, off:off + 128], rhs=wc[ri].t[:, :], start=(ri == 0), stop=(ri == 1))
                evac([rps], [Y.r], Z3[:, g, kh, :, :], ps[:, 0:256].rearrange("p (r c) -> p r c", r=2))
        fn = X.t[:, 0:16384].rearrange("p (g a h l) -> p g a h l", g=4, a=32, h=32)
        scale = float(1.0 / np.sqrt(4096.0 * 128.0))
        for g in range(4):
            for kh in range(32):
                ps, rps = bank(K)
                for ri in range(2):
                    S.I("pe", "matmul", [Y.r, tw.r], [rps], ps[:, 0:128], lhsT=Z3[:, g, kh, ri, :], rhs=tw.t[:, kh, ri, :], start=(ri == 0), stop=(ri == 1))
                nev[0] += 1
                src = ps[:, 0:128].rearrange("p (a l) -> p a l", l=4)
                if nev[0] % 2:
                    S.I("act", "mul", [rps], [X.r], out=fn[:, g, :, kh, :], in_=src, mul=scale)
                else:
                    S.I("dve", "tensor_scalar", [rps], [X.r], out=fn[:, g, :, kh, :], in0=src, scalar1=scale, scalar2=None, op0=ALU.mult)
        for g in range(4):
            S.D("sp", K.MIX[:, g, 0:NX], X.t[:, g * 4096:(g + 1) * 4096], [X.r], K.R_MIXa[:32])
        S.flush()


def l1_run(K, S):
    phaseA1(K, S)
    phaseG(K, S)
    phaseF(K, S)


def l1_prep(I, shared):
    f = lambda a: np.ascontiguousarray(np.asarray(a, dtype=np.float32))
    w_in = I["cd_w_in"][0]
    shared["cd_w_in"] = f(w_in)
    d = np.arange(64)
    e = d % 32
    partner = np.where(e < 16, d + 16, d - 16)
    perm = (np.arange(8)[:, None] * 64 + partner[None, :]).reshape(-1)
    shared["cd_w_perm"] = f(w_in[:, 512:1024][:, perm])
    shared["decw"] = f(np.stack([np.concatenate([I["cd_decay_w_fwd"][0], I["cd_decay_b_fwd"][0][None]], 0),
                                 np.concatenate([I["cd_decay_w_bwd"][0], I["cd_decay_b_bwd"][0][None]], 0)], 0))
    shared["hnw"] = f(I["cd_head_norm_w"][0].reshape(128, 1))
    m = np.arange(128)[:, None]
    l = np.arange(128)[None, :]
    sc = (m // 64) == (l // 64)
    tri = np.stack([sc & (m <= l), sc & (m > l), sc & (m >= l), sc & (m < l)], 1).astype(np.float32) * (-1.0 / 16.0)
    shared["tri"] = f(tri)
    shared["gmask"] = f(np.stack([sc & (m <= l), sc & (m >= l)], 1).astype(np.float32))
    t = np.arange(NX)
    fi = e % 16
    inv = 10000.0 ** (-(fi.astype(np.float64)) / 16.0)
    pos = np.where((d // 32)[:, None] == 0, (t // 64)[None, :], (t % 64)[None, :]).astype(np.float64)
    ang = pos * inv[:, None]
    C64 = np.cos(ang)
    S64 = np.sin(ang) * np.where(e < 16, -1.0, 1.0)[:, None]
    shared["ropeC"] = f(np.concatenate([C64, C64], 0))
    shared["ropeS"] = f(np.concatenate([S64, S64], 0))
    Ct = np.tile(C64.T, (1, 4)).reshape(32, 128, 256)
    St = np.tile(S64.T, (1, 4)).reshape(32, 128, 256)
    shared["ropeCt"] = f(Ct)
    shared["ropeSt"] = f(St)
    n = np.arange(128)
    a128 = 2 * np.pi * np.outer(n, n) / 128.0
    Cn, Sn = np.cos(a128), np.sin(a128)
    shared["w128ri"] = f(np.concatenate([Cn, -Sn], 1))
    shared["wc1"] = f(np.concatenate([Cn, -Sn], 1))
    shared["wc2"] = f(np.concatenate([Sn, Cn], 1))
    n2 = np.arange(32)
    k2 = np.arange(32)
    TW = np.zeros((32, 4, 32, 32, 4), np.complex128)
    for kh in range(32):
        for kl in range(4):
            k1 = 4 * kh + kl
            TW[:, kl, kh, :, kl] = np.exp(-2j * np.pi * n2 * k1 / 4096.0)[:, None] * np.exp(-2j * np.pi * np.outer(n2, k2) / 32.0)
    TW = TW.reshape(128, 32, 128)
    shared["tw"] = f(np.stack([TW.real, -TW.imag], 2))


LAYER1 = {"decl": l1_decl, "scratch": l1_scratch, "run": l1_run, "prep": l1_prep}


def _na_bias_tiles(rel_bias):
    NEG = np.float32(-30000.0)
    H = rel_bias.shape[0]
    out = np.full((H, 128, 21, 128), NEG, np.float32)
    kc = np.arange(64)
    qc = np.arange(64)
    cs = np.clip(qc - 8, 0, 48)
    colok = (kc[:, None] >= cs[None, :]) & (kc[:, None] < cs[None, :] + 16)
    cidx = np.clip(kc[:, None] - qc[None, :] + 15, 0, 30)

    def tile(i, m):
        t = np.full((H, 128, 128), NEG, np.float32)
        for a in range(2):
            kr = 2 * m + a
            for b in range(2):
                qr = 2 * i + b
                rs = min(max(qr - 4, 0), 56)
                if not (rs <= kr < rs + 8):
                    continue
                ridx = kr - qr + 7
                vals = rel_bias[:, ridx, :][:, cidx]
                vals = np.where(colok[None], vals, NEG)
                t[:, a * 64:(a + 1) * 64, b * 64:(b + 1) * 64] = vals
        return t
    for d in range(5):
        out[:, :, d, :] = tile(10, 10 + d - 2)
    for s, i in enumerate((0, 1, 30, 31)):
        ms = list(range(4)) if i < 2 else list(range(28, 32))
        for ci, m in enumerate(ms):
            out[:, :, 5 + 4 * s + ci, :] = tile(i, m)
    return out


def _consts():
    c = {}
    c["ident"] = np.eye(128, dtype=np.float32)
    return c


def build_program():
    nc = bass.Bass("TRN2", target_bir_lowering=False)
    K = KB()
    K.nc = nc
    K.bank_rr = 0
    K.inp = {}

    def din(name, shape, dt=F32):
        K.inp[name] = nc.dram_tensor(name, list(shape), dt, kind="ExternalInput").ap()
    din("x", [NX, D]); din("ctx", [NCX, D]); din("csT", [128, 8, 2]); din("ada_w", [2, D, 6 * D]); din("ada_bT", [2, 128, 48])
    din("nmw", [2, 128, 8]); din("nfw", [2, 128, 8]); din("fnw_bc", [128, D])
    din("ffn_wg", [2, D, FH]); din("ffn_wu", [2, D, FH]); din("ffn_wd", [2, FH, D])
    din("ab_w_in", [D, 2560]); din("ab_w_out", [D, D]); din("sgu_nw_bc", [128, 512]); din("sguWT", [128, 4, 128]); din("sgub", [128, 512])
    din("nab", [8, 128, 21, 128]); din("ident", [128, 128])
    din("cd_w_out", [D, D])
    if LAYER1 is not None:
        LAYER1["decl"](K, din)
    K.out = nc.dram_tensor("out", [NX, D], F32, kind="ExternalOutput").ap()
    K.R_out = Res()

    def scr(name, shape, dt):
        kind = "ExternalOutput" if (DEBUG and name in DEBUG) else "Internal"
        return nc.dram_tensor(name, list(shape), dt, kind=kind).ap()
    K.XT = scr("XT", [128, 8, NT], F32)
    K.MIX = scr("MIX", [128, 8, NT], BF16)
    K.QT = scr("QT", [128, 4, NT], BF16)
    K.KT = scr("KT", [128, 4, NT], BF16)
    K.VT = scr("VT", [34, 128, 520], BF16)
    K.WGb = [[scr("WG%d_%d" % (l, m), [D, FH], BF16) for m in range(2)] for l in range(2)]
    K.R_XT = [Res() for _ in range(34)]
    K.R_MIXa = [Res() for _ in range(34)]
    K.R_MIXb = [Res() for _ in range(34)]
    K.R_QT = [Res() for _ in range(34)]
    K.R_KT = [Res() for _ in range(34)]
    K.R_VT = [Res() for _ in range(34)]
    K.R_WG = [[[Res() for _ in range(KC)] for _ in range(2)] for _ in range(2)]
    if LAYER1 is not None:
        LAYER1["scratch"](K, scr)
    with ExitStack() as es:
        S = Sched(nc, es)
        K.PB = [TT(es.enter_context(nc.psum_tensor("pb%d" % i, [128, 1024], F32))) for i in range(4)]
        K.RB = [Res() for _ in range(8)]
        phase_consts(K, S, es)
        phase_mod(K, S)
        phaseA0(K, S)
        phaseB0(K, S)
        phaseC(K, S, 0)
        if LAYER1 is not None:
            LAYER1["run"](K, S)
            phaseC(K, S, 1)
    return nc


def prep_inputs(inputs):
    f = lambda a: np.ascontiguousarray(np.asarray(a, dtype=np.float32))
    I = {k: np.asarray(v) for k, v in inputs.items()}
    shared = {}
    shared["ada_w"] = f(I["ada_w"])
    shared["ada_bT"] = f(I["ada_b"].reshape(2, 48, 128).transpose(0, 2, 1))
    shared["nmw"] = f(I["norm_mix_w"].reshape(2, 8, 128).transpose(0, 2, 1))
    shared["nfw"] = f(I["norm_ffn_w"].reshape(2, 8, 128).transpose(0, 2, 1))
    shared["fnw_bc"] = f(np.broadcast_to(I["final_norm_w"][None, :], (128, D)))
    shared["ffn_wg"] = f(I["ffn_w_gate"]); shared["ffn_wu"] = f(I["ffn_w_up"]); shared["ffn_wd"] = f(I["ffn_w_down"])
    shared["ab_w_in"] = f(I["ab_w_in"][0]); shared["ab_w_out"] = f(I["ab_w_out"][0])
    shared["sgu_nw_bc"] = f(np.broadcast_to(I["ab_sgu_norm_w"][0][None, :], (128, 512)))
    shared["sguWT"] = f(I["ab_sgu_w"][0].transpose(2, 0, 1))
    shared["sgub"] = f(np.broadcast_to(I["ab_sgu_b"][0].reshape(1, 512), (128, 512)))
    shared["nab"] = _na_bias_tiles(f(I["ab_rel_bias"][0]))
    shared["cd_w_out"] = f(I["cd_w_out"][0])
    shared.update(_consts())
    if LAYER1 is not None:
        LAYER1["prep"](I, shared)
    in_maps = []
    for b in range(8):
        m = dict(shared)
        m["x"] = f(I["x"][b]); m["ctx"] = f(I["ctx"][b])
        cs = np.stack([I["c"][b], I["c_ctx"]], axis=-1)
        m["csT"] = f(cs.reshape(8, 128, 2).transpose(1, 0, 2))
        in_maps.append(m)
    return in_maps


_NC_CACHE = {}


def kernel(**inputs):
    in_maps = prep_inputs(inputs)
    if "nc" not in _NC_CACHE:
        _NC_CACHE["nc"] = build_program()
    res = run_bass_kernel_spmd(_NC_CACHE["nc"], in_maps, core_ids=list(range(8)))
    return np.stack([np.asarray(r["out"], dtype=np.float32) for r in res.results], axis=0)
```

```python
import numpy as np
import concourse.bass as bass
import concourse.mybir as mybir
from concourse.bass_utils import run_bass_kernel_spmd
from contextlib import ExitStack

F32 = mybir.dt.float32
BF16 = mybir.dt.bfloat16
AF = mybir.ActivationFunctionType
ALU = mybir.AluOpType
AX = mybir.AxisListType

NX, NCX, NT, D, KC, FH, NJ = 4096, 256, 4352, 1024, 8, 2816, 22
EPS = 1e-6
DEBUG = False


class Res:
    __slots__ = ("w", "r")

    def __init__(self):
        self.w = {}
        self.r = {}


class Sched:
    NLANES = 12

    def __init__(self, nc, es):
        self.nc = nc
        self.keys = ("pe", "act", "dve", "pool", "sp")
        self.streams = {k: [] for k in self.keys}
        self.sems, self.count = {}, {}
        self.known = {k: {} for k in self.keys}
        for k in ("pe", "act", "dve", "pool"):
            self.sems[k] = es.enter_context(nc.semaphore("s_" + k))
            self.count[k] = 0
        self.lanes = {}
        for q in ("sp", "pool"):
            self.lanes[q] = []
            for i in range(self.NLANES):
                key = "L%s%d" % (q, i)
                self.sems[key] = es.enter_context(nc.semaphore(key))
                self.count[key] = 0
                self.lanes[q].append(key)
        self.lane_rr = {q: 0 for q in self.lanes}

    def _need(self, issuer, ckey, seq, waits):
        if not seq:
            return
        if ckey == "pe" and issuer == "pe":
            return
        if self.known[issuer].get(ckey, 0) >= seq:
            return
        self.known[issuer][ckey] = seq
        waits[ckey] = max(waits.get(ckey, 0), seq)

    def _deps(self, issuer, reads, writes):
        waits = {}
        for r in reads:
            for ck, sq in r.w.items():
                self._need(issuer, ck, sq, waits)
        for r in writes:
            for ck, sq in r.w.items():
                self._need(issuer, ck, sq, waits)
            for ck, sq in r.r.items():
                self._need(issuer, ck, sq, waits)
        return waits

    def _mark(self, ckey, seq, reads, writes):
        for r in reads:
            if r.r.get(ckey, 0) < seq:
                r.r[ckey] = seq
        for r in writes:
            r.w = {ckey: seq}
            r.r = {}

    def I(self, eng, meth, reads, writes, *a, **kw):
        waits = self._deps(eng, reads, writes)
        self.count[eng] += 1
        seq = self.count[eng]
        self._mark(eng, seq, reads, writes)
        self.streams[eng].append((lambda e: getattr(e, meth)(*a, **kw), waits, eng, 1))

    def D(self, q, out, in_, reads, writes, **kw):
        lanes = self.lanes[q]
        lane = lanes[self.lane_rr[q] % len(lanes)]
        self.lane_rr[q] += 1
        waits = self._deps(q, reads, writes)
        self._need(q, lane, self.count[lane], waits)
        self.count[lane] += 1
        seq = self.count[lane]
        self._mark(lane, seq, reads, writes)
        self.streams[q].append((lambda e: e.dma_start(out=out, in_=in_, **kw), waits, lane, 16))

    def barrier(self):
        for e in self.keys:
            waits = {}
            for ck, c in self.count.items():
                self._need(e, ck, c, waits)
            if waits:
                self.streams[e].append((None, waits, None, 0))

    def _mult(self, ck):
        return 16 if ck.startswith("L") else 1

    def flush(self):
        self.barrier()
        nc = self.nc
        streams = self.streams
        self.streams = {k: [] for k in self.keys}

        def mk(key):
            def body(e):
                for fn, waits, ckey, inc in streams[key]:
                    for wk, sq in waits.items():
                        e.wait_ge(self.sems[wk], sq * self._mult(wk))
                    if fn is not None:
                        fn(e).then_inc(self.sems[ckey], inc)
            return body
        with nc.Block() as block:
            block.tensor(mk("pe"))
            block.scalar(mk("act"))
            block.vector(mk("dve"))
            block.gpsimd(mk("pool"))
            block.sync(mk("sp"))


class TT:
    def __init__(self, t):
        self.t = t
        self.r = Res()


class Rot:
    def __init__(self, items):
        self.items = items
        self.i = 0

    def next(self):
        x = self.items[self.i % len(self.items)]
        self.i += 1
        return x


class KB:
    pass


_uid = [0]


def sbt(K, ph, shape, dt):
    _uid[0] += 1
    return TT(ph.enter_context(K.nc.sbuf_tensor("t%d" % _uid[0], list(shape), dt)))


def sbrot(K, ph, n, shape, dt):
    return Rot([sbt(K, ph, shape, dt) for _ in range(n)])


def bank(K):
    pool = getattr(K, "bank_pool", None) or list(range(8))
    i = pool[K.bank_rr % len(pool)]
    K.bank_rr += 1
    return K.PB[i // 2].t[:, (i % 2) * 512:(i % 2) * 512 + 512], K.RB[i]


def bank_fixed(K, i):
    return K.PB[i // 2].t[:, (i % 2) * 512:(i % 2) * 512 + 512], K.RB[i]


def bank2(K):
    if K.bank_rr % 2:
        K.bank_rr += 1
    i = (K.bank_rr % 8) // 2
    K.bank_rr += 2
    return K.PB[i].t[:, :], [K.RB[2 * i], K.RB[2 * i + 1]]


def norm_mod(K, S, xt, n, A, B, j, hT, tmps, sq, rstd):
    S.I("act", "activation", [xt.r], [sq.r], out=sq.t[:, :, :n], in_=xt.t[:, :, :n], func=AF.Square)
    ms, rms = bank(K)
    for kc in range(KC):
        S.I("pe", "matmul", [K.ones_bf.r, sq.r], [rms], ms[:, :n], lhsT=K.ones_bf.t[:, :], rhs=sq.t[:, kc, :n],
            start=(kc == 0), stop=(kc == KC - 1))
    S.I("act", "activation", [rms], [rstd.r], out=rstd.t[:, :n], in_=ms[:, :n], func=AF.Ln, scale=1.0 / D, bias=K.epsc.t[:, 0:1])
    S.I("act", "activation", [rstd.r], [rstd.r], out=rstd.t[:, :n], in_=rstd.t[:, :n], func=AF.Exp, scale=-0.5)
    for kc in range(KC):
        tm = tmps.next()
        S.I("dve", "tensor_tensor", [xt.r, rstd.r], [tm.r], out=tm.t[:, :n], in0=xt.t[:, kc, :n], in1=rstd.t[:, :n], op=ALU.mult)
        S.I("act", "activation", [tm.r, K.modr], [hT.r], out=hT.t[:, kc, :n], in_=tm.t[:, :n], func=AF.Identity,
            scale=A.t[:, kc, j:j + 1], bias=B.t[:, kc, j:j + 1])


def small_rstd(K, S, ss, rs, dim):
    S.I("act", "activation", [ss.r], [rs.r], out=rs.t[:, 0:1], in_=ss.t[:, 0:1], func=AF.Ln, scale=1.0 / dim, bias=K.epsc.t[:, 0:1])
    S.I("act", "activation", [rs.r], [rs.r], out=rs.t[:, 0:1], in_=rs.t[:, 0:1], func=AF.Exp, scale=-0.5)


def blkres(rl, t0, n):
    return rl[t0 // 128:(t0 + n) // 128]


def phase_consts(K, S, ph):
    nc = K.nc
    K.ident = sbt(K, ph, [128, 128], F32)
    S.D("sp", K.ident.t[:, :], K.inp["ident"], [], [K.ident.r])
    K.ident_bf = sbt(K, ph, [128, 128], BF16)
    S.D("pool", K.ident_bf.t[:, :], K.inp["ident"], [], [K.ident_bf.r])
    K.ones_bf = sbt(K, ph, [128, 128], BF16)
    S.I("pool", "memset", [], [K.ones_bf.r], K.ones_bf.t[:, :], 1.0)
    K.ones_f = sbt(K, ph, [1, 128], F32)
    S.I("pool", "memset", [], [K.ones_f.r], K.ones_f.t[:, :], 1.0)
    K.epsc = sbt(K, ph, [128, 1], F32)
    S.I("pool", "memset", [], [K.epsc.r], K.epsc.t[:, :], EPS)
    K.modr = Res()
    K.mods = []
    for l in range(2):
        K.mods.append({k: sbt(K, ph, [128, 8, 2], F32) for k in ("Am", "Bm", "Gm", "Af", "Bf", "Gf")})
        for v in K.mods[l].values():
            v.r = K.modr


def phase_mod(K, S):
    with ExitStack() as ph:
        csil = sbt(K, ph, [128, 8, 2], F32)
        S.D("sp", csil.t[:, :, :], K.inp["csT"], [], [csil.r])
        S.I("act", "activation", [csil.r], [csil.r], out=csil.t[:, :, :], in_=csil.t[:, :, :], func=AF.Silu)
        awts = sbrot(K, ph, 3, [128, 8, 512], F32)
        modrow = sbt(K, ph, [2, 6144], F32)
        mod = sbt(K, ph, [128, 48, 2], F32)
        abT = sbt(K, ph, [128, 48], F32)
        nw = sbt(K, ph, [128, 8], F32)
        fins = sbrot(K, ph, 3, [128, FH], F32)
        fouts = sbrot(K, ph, 3, [128, FH], BF16)
        engs = Rot(["pool", "dve", "pool", "act"])
        wcl = [(l, m, nm, kc) for l in range(2) for m, nm in ((0, "ffn_wg"), (1, "ffn_wu")) for kc in range(KC)]
        wst = {}

        def w_load(i):
            l, m, nm, kc = wcl[i]
            a = fins.next()
            S.D("sp", a.t[:, :], K.inp[nm][l][kc * 128:(kc + 1) * 128, :], [], [a.r])
            wst[i] = a

        def w_cast_store(i):
            l, m, nm, kc = wcl[i]
            a = wst.pop(i)
            b = fouts.next()
            e = engs.next()
            if e == "act":
                S.I("act", "copy", [a.r], [b.r], out=b.t[:, :], in_=a.t[:, :])
            else:
                S.I(e, "tensor_copy", [a.r], [b.r], out=b.t[:, :], in_=a.t[:, :])
            S.D("sp", K.WGb[l][m][kc * 128:(kc + 1) * 128, :], b.t[:, :], [b.r], [K.R_WG[l][m][kc]])

        def m_block(i):
            l, cb = divmod(i, 12)
            awv = K.inp["ada_w"][l].rearrange("(kc p) n -> p kc n", p=128)
            aw = awts.next()
            S.D("sp", aw.t[:, :, :], awv[:, :, cb * 512:(cb + 1) * 512], [], [aw.r])
            ps, rps = bank(K)
            for kc in range(KC):
                S.I("pe", "matmul", [aw.r, csil.r], [rps], ps[0:2, 0:512], lhsT=csil.t[:, kc, :], rhs=aw.t[:, kc, :], start=(kc == 0), stop=(kc == KC - 1))
            if cb % 2:
                S.I("act", "copy", [rps], [modrow.r], out=modrow.t[0:2, cb * 512:(cb + 1) * 512], in_=ps[0:2, 0:512])
            else:
                S.I("dve", "tensor_copy", [rps], [modrow.r], out=modrow.t[0:2, cb * 512:(cb + 1) * 512], in_=ps[0:2, 0:512])
            if cb == 11:
                m_final(l)

        def m_final(l):
            mp, rmp = bank(K)
            for j in range(48):
                S.I("pe", "transpose", [modrow.r, K.ident.r], [rmp], out=mp[:, 2 * j:2 * j + 2], in_=modrow.t[0:2, j * 128:(j + 1) * 128], identity=K.ident.t[0:2, 0:2])
            S.D("sp", abT.t[:, :], K.inp["ada_bT"][l], [], [abT.r])
            mpv = mp[:, 0:96].rearrange("p (a b) -> p a b", b=2)
            for j in range(2):
                S.I("dve", "tensor_tensor", [rmp, abT.r], [mod.r], out=mod.t[:, :, j], in0=mpv[:, :, j], in1=abT.t[:, :], op=ALU.add)
            M = K.mods[l]
            for (nm, a, b, g, off) in (("nmw", "Am", "Bm", "Gm", 0), ("nfw", "Af", "Bf", "Gf", 24)):
                S.D("sp", nw.t[:, :], K.inp[nm][l], [], [nw.r])
                for j in range(2):
                    S.I("dve", "scalar_tensor_tensor", [mod.r, nw.r], [K.modr], out=M[a].t[:, :, j], in0=mod.t[:, off + 8:off + 16, j],
                        scalar=1.0, in1=nw.t[:, :], op0=ALU.add, op1=ALU.mult)
                    S.I("dve", "tensor_copy", [mod.r], [K.modr], out=M[b].t[:, :, j], in_=mod.t[:, off:off + 8, j])
                    S.I("dve", "tensor_copy", [mod.r], [K.modr], out=M[g].t[:, :, j], in_=mod.t[:, off + 16:off + 24, j])
        NW = len(wcl)
        w_load(0)
        for step in range(NW):
            if step + 1 < NW:
                w_load(step + 1)
            if step < 24:
                m_block(step)
            w_cast_store(step)
        S.flush()


def phase_wcast(K, S):
    with ExitStack() as ph:
        fin = sbrot(K, ph, 2, [128, FH], F32)
        fout = sbrot(K, ph, 2, [128, FH], BF16)
        engs = Rot(["pool", "dve", "act"])
        for l in range(2):
            for m, nm in ((0, "ffn_wg"), (1, "ffn_wu")):
                for kc in range(KC):
                    a, b = fin.next(), fout.next()
                    S.D("sp", a.t[:, :], K.inp[nm][l][kc * 128:(kc + 1) * 128, :], [], [a.r])
                    e = engs.next()
                    if e == "act":
                        S.I("act", "copy", [a.r], [b.r], out=b.t[:, :], in_=a.t[:, :])
                    else:
                        S.I(e, "tensor_copy", [a.r], [b.r], out=b.t[:, :], in_=a.t[:, :])
                    S.D("sp", K.WGb[l][m][kc * 128:(kc + 1) * 128, :], b.t[:, :], [b.r], [K.R_WG[l][m][kc]])
        S.flush()


def phase0(K, S):
    with ExitStack() as ph:
        xin = sbrot(K, ph, 2, [128, 1024], F32)
        xo = sbrot(K, ph, 2, [128, 8, 128], F32)
        for blk in range(34):
            src = K.inp["x"][blk * 128:(blk + 1) * 128, :] if blk < 32 else K.inp["ctx"][(blk - 32) * 128:(blk - 31) * 128, :]
            a = xin.next()
            S.D("sp", a.t[:, :], src, [], [a.r])
            pt, rpt = bank2(K)
            for kc in range(KC):
                S.I("pe", "transpose", [a.r, K.ident.r], rpt, out=pt[:, kc * 128:(kc + 1) * 128], in_=a.t[:, kc * 128:(kc + 1) * 128],
                    identity=K.ident.t[:, :])
            o = xo.next()
            ov = o.t[:, :, :].rearrange("p a b -> p (a b)")
            if blk % 2:
                S.I("act", "copy", rpt, [o.r], out=ov, in_=pt)
            else:
                S.I("dve", "tensor_copy", rpt, [o.r], out=ov, in_=pt)
            S.D("sp", K.XT[:, :, blk * 128:(blk + 1) * 128], o.t[:, :, :], [o.r], [K.R_XT[blk]])
        S.flush()


def phaseA0(K, S):
    M = K.mods[0]
    with ExitStack() as ph:
        win = sbt(K, ph, [128, 8, 2560], BF16)
        wv = K.inp["ab_w_in"].rearrange("(kc p) n -> p kc n", p=128)
        for kc in range(KC):
            S.D("pool", win.t[:, kc, :], wv[:, kc, :], [], [win.r])
        sguw = sbt(K, ph, [128, 4, 128], BF16)
        S.D("pool", sguw.t[:, :, :], K.inp["sguWT"], [], [sguw.r])
        nwbc = sbt(K, ph, [128, 512], F32)
        S.D("sp", nwbc.t[:, :], K.inp["sgu_nw_bc"], [], [nwbc.r])
        sgub = sbt(K, ph, [128, 512], F32)
        S.D("sp", sgub.t[:, :], K.inp["sgub"], [], [sgub.r])
        gts = sbrot(K, ph, 2, [128, 512], F32)
        xts = sbrot(K, ph, 2, [128, 8, 512], F32)
        xins = sbrot(K, ph, 3, [128, 1024], F32)
        sqs = sbrot(K, ph, 2, [128, 8, 512], BF16)
        rstds = sbrot(K, ph, 2, [128, 512], F32)
        tmps = sbrot(K, ph, 2, [128, 512], F32)
        hTs = sbrot(K, ph, 2, [128, 8, 512], BF16)
        uT = sbt(K, ph, [128, 4, 512], BF16)
        qos = sbrot(K, ph, 2, [128, 4, 512], BF16)
        kos = sbrot(K, ph, 2, [128, 4, 512], BF16)
        aos = sbrot(K, ph, 2, [128, 4, 512], BF16)
        vos = sbrot(K, ph, 2, [128, 8, 65], BF16)
        for v in vos.items:
            S.I("pool", "memset", [], [v.r], v.t[:, :, :], 1.0)
        gvs = sbrot(K, ph, 2, [128, 512], F32)
        junk = sbt(K, ph, [128, 512], BF16)
        vns = sbrot(K, ph, 2, [128, 512], BF16)
        sss = sbrot(K, ph, 2, [128, 1], F32)
        rss = sbrot(K, ph, 2, [128, 1], F32)
        def tile(t):
            n = 512 if t < 8 else 256
            t0 = t * 512
            j = 0 if t < 8 else 1
            xt = xts.next()
            hT, sq, rstd = hTs.next(), sqs.next(), rstds.next()
            for sbk in range(n // 128):
                blk = t * 4 + sbk
                src = K.inp["x"][blk * 128:(blk + 1) * 128, :] if blk < 32 else K.inp["ctx"][(blk - 32) * 128:(blk - 31) * 128, :]
                a = xins.next()
                S.D("sp", a.t[:, :], src, [], [a.r])
                pt, rpt = bank2(K)
                for kc in range(KC):
                    S.I("pe", "transpose", [a.r, K.ident.r], rpt, out=pt[:, kc * 128:(kc + 1) * 128], in_=a.t[:, kc * 128:(kc + 1) * 128],
                        identity=K.ident.t[:, :])
                if sbk % 2:
                    S.I("act", "copy", rpt, [xt.r], out=xt.t[:, :, sbk * 128:(sbk + 1) * 128], in_=pt.rearrange("p (k t) -> p k t", t=128))
                else:
                    S.I("dve", "tensor_copy", rpt, [xt.r], out=xt.t[:, :, sbk * 128:(sbk + 1) * 128], in_=pt.rearrange("p (k t) -> p k t", t=128))
            S.D("sp", K.XT[:, :, t0:t0 + n], xt.t[:, :, :n], [xt.r], blkres(K.R_XT, t0, n))
            norm_mod(K, S, xt, n, M["Am"], M["Bm"], j, hT, tmps, sq, rstd)
            yield
            qo, ko, ao = qos.next(), kos.next(), aos.next()
            for ci in list(range(4)) + list(range(8, 16)):
                ps, rps = bank(K)
                for kc in range(KC):
                    S.I("pe", "matmul", [win.r, hT.r], [rps], ps[:, :n], lhsT=win.t[:, kc, ci * 128:(ci + 1) * 128], rhs=hT.t[:, kc, :n],
                        start=(kc == 0), stop=(kc == KC - 1))
                if ci < 4:
                    S.I("act", "activation", [rps], [uT.r], out=uT.t[:, ci, :n], in_=ps[:, :n], func=AF.Gelu_apprx_tanh)
                elif ci < 12:
                    S.I("dve", "tensor_scalar", [rps], [qo.r], out=qo.t[:, ci - 8, :n], in0=ps[:, :n], scalar1=0.125, scalar2=None, op0=ALU.mult)
                else:
                    S.I("act", "copy", [rps], [ko.r], out=ko.t[:, ci - 12, :n], in_=ps[:, :n])
            S.D("sp", K.QT[:, :, t0:t0 + n], qo.t[:, :, :n], [qo.r], blkres(K.R_QT, t0, n))
            S.D("sp", K.KT[:, :, t0:t0 + n], ko.t[:, :, :n], [ko.r], blkres(K.R_KT, t0, n))
            yield
            pend = []
            for sbk in range(n // 128):
                blk = t * 4 + sbk
                tk = slice(sbk * 128, (sbk + 1) * 128)
                pv, rpv = bank(K)
                pa, rpa = bank(K)
                for kc in range(KC):
                    S.I("pe", "matmul", [win.r, hT.r], [rpv], pv, lhsT=hT.t[:, kc, tk], rhs=win.t[:, kc, 512:1024], start=(kc == 0), stop=(kc == KC - 1))
                for kc in range(KC):
                    S.I("pe", "matmul", [win.r, hT.r], [rpa], pa, lhsT=hT.t[:, kc, tk], rhs=win.t[:, kc, 2048:2560], start=(kc == 0), stop=(kc == KC - 1))
                while pend:
                    pend.pop(0)()
                vo = vos.next()
                S.I("act", "copy", [rpa], [vo.r], out=vo.t[:, :, 0:64], in_=pa.rearrange("p (h d) -> p h d", d=64))
                S.D("sp", K.VT[blk].rearrange("p (h e) -> p h e", e=65), vo.t[:, :, :], [vo.r], [K.R_VT[blk]])
                gv, ss, rs, vn = gvs.next(), sss.next(), rss.next(), vns.next()
                S.I("act", "activation", [rpv], [gv.r], out=gv.t[:, :], in_=pv, func=AF.Gelu_apprx_tanh)
                S.I("dve", "memset", [], [ss.r], ss.t[:, :], 0.0)
                S.I("act", "activation", [gv.r, ss.r], [junk.r, ss.r], out=junk.t[:, :], in_=gv.t[:, :], func=AF.Square, accum_out=ss.t[:, 0:1])
                small_rstd(K, S, ss, rs, 512)
                S.I("dve", "scalar_tensor_tensor", [gv.r, rs.r, nwbc.r], [vn.r], out=vn.t[:, :], in0=gv.t[:, :], scalar=rs.t[:, 0:1],
                    in1=nwbc.t[:, :], op0=ALU.mult, op1=ALU.mult)
                def mk(vn=vn, tk=tk):
                    def f():
                        pg, rpg = bank(K)
                        for g in range(4):
                            gs = slice(g * 128, (g + 1) * 128)
                            S.I("pe", "matmul", [vn.r, sguw.r], [rpg], pg[:, gs], lhsT=vn.t[:, gs], rhs=sguw.t[:, g, :], start=True, stop=True)
                        gt = gts.next()
                        S.I("dve", "tensor_tensor", [rpg, sgub.r], [gt.r], out=gt.t[:, :], in0=pg, in1=sgub.t[:, :], op=ALU.add)
                        S.I("dve", "tensor_tensor", [gt.r, uT.r], [ao.r], out=ao.t[:, :, tk], in0=gt.t[:, :].rearrange("p (g t) -> p g t", t=128),
                            in1=uT.t[:, :, tk], op=ALU.mult)
                    return f
                pend.append(mk())
            while pend:
                pend.pop(0)()
            S.D("sp", K.MIX[:, 0:4, t0:t0 + n], ao.t[:, :, :n], [ao.r], blkres(K.R_MIXa, t0, n))
        gens = [tile(t) for t in range(9)]
        next(gens[0])
        for t in range(9):
            next(gens[t])
            if t + 1 < 9:
                next(gens[t + 1])
            for _ in gens[t]:
                pass
        S.flush()


def na_blocks():
    out = []
    for i in range(32):
        if i < 2:
            out.append((list(range(4)), 5 + 4 * i))
        elif i >= 30:
            out.append((list(range(28, 32)), 5 + 4 * (i - 28)))
        else:
            out.append((list(range(i - 2, i + 3)), 0))
    out.append(([], 0))
    out.append(([], 0))
    return out


def phaseB0(K, S):
    with ExitStack() as ph:
        vall = sbt(K, ph, [128, 34, 520], BF16)
        S.D("sp", vall.t[:, :, :], K.VT.rearrange("b p e -> p b e"), K.R_VT, [vall.r])
        vv = vall.t[:, :, :].rearrange("p b (h e) -> p b h e", e=65)
        qjs = sbrot(K, ph, 2, [128, NT], BF16)
        kjs = sbrot(K, ph, 2, [128, NT], BF16)
        nbs = sbrot(K, ph, 2, [128, 21, 128], F32)
        otok = sbt(K, ph, [128, 34, 512], BF16)
        tmps = sbrot(K, ph, 3, [128, 640], F32)
        pTs = sbrot(K, ph, 4, [128, 896], BF16)
        rcs = sbrot(K, ph, 3, [128, 1], F32)
        blocks = na_blocks()
        spairs = Rot([0, 1, 2])
        pobanks = Rot([6, 7])
        its = []
        for jp in range(4):
            for hh in range(2):
                for i in range(34):
                    its.append((jp, hh, i))
        N = len(its)
        ctxs = [None] * N
        cur = {}

        def stA(k):
            jp, hh, i = its[k]
            if hh == 0 and i == 0:
                cur["qj"], cur["kj"] = qjs.next(), kjs.next()
                S.D("sp", cur["qj"].t[:, :], K.QT[:, jp, :], K.R_QT, [cur["qj"].r])
                S.D("sp", cur["kj"].t[:, :], K.KT[:, jp, :], K.R_KT, [cur["kj"].r])
            if i == 0:
                cur["nb"] = nbs.next()
                S.D("sp", cur["nb"].t[:, :, :], K.inp["nab"][2 * jp + hh], [], [cur["nb"].r])
            qj, kj, nb = cur["qj"], cur["kj"], cur["nb"]
            hb = hh * 64
            chunks, t0 = blocks[i]
            allc = chunks + [32, 33]
            pi = spairs.next()
            Sp = K.PB[pi].t[:, :]
            rS = [K.RB[2 * pi], K.RB[2 * pi + 1]]
            for ci, m in enumerate(allc):
                S.I("pe", "matmul", [qj.r, kj.r], rS, Sp[:, ci * 128:(ci + 1) * 128], lhsT=kj.t[hb:hb + 64, m * 128:(m + 1) * 128],
                    rhs=qj.t[hb:hb + 64, i * 128:(i + 1) * 128], start=True, stop=True)
            ctxs[k] = dict(Sp=Sp, rS=rS, nb=nb, nnb=len(chunks), t0=t0, allc=allc, h=2 * jp + hh, i=i)

        def stB(k):
            c = ctxs[k]
            nnb, Sp, rS = c["nnb"], c["Sp"], c["rS"]
            pT = pTs.next()
            c["pT"] = pT
            if nnb:
                tm = tmps.next()
                S.I("dve", "tensor_tensor", rS + [c["nb"].r], [tm.r], out=tm.t[:, :nnb * 128].rearrange("p (a b) -> p a b", b=128),
                    in0=Sp[:, :nnb * 128].rearrange("p (a b) -> p a b", b=128), in1=c["nb"].t[:, c["t0"]:c["t0"] + nnb, :], op=ALU.add)
                S.I("act", "activation", [tm.r], [pT.r], out=pT.t[:, :nnb * 128], in_=tm.t[:, :nnb * 128], func=AF.Exp)
            S.I("act", "activation", rS, [pT.r], out=pT.t[:, nnb * 128:(nnb + 2) * 128], in_=Sp[:, nnb * 128:(nnb + 2) * 128], func=AF.Exp)

        def stC(k):
            c = ctxs[k]
            po, rpo = bank_fixed(K, pobanks.next())
            c["po"], c["rpo"] = po, rpo
            pT, allc = c["pT"], c["allc"]
            for ci, m in enumerate(allc):
                S.I("pe", "matmul", [pT.r, vall.r], [rpo], po[:, 0:65], lhsT=pT.t[:, ci * 128:(ci + 1) * 128], rhs=vv[:, m, c["h"], :],
                    start=(ci == 0), stop=(ci == len(allc) - 1))

        def stD(k):
            c = ctxs[k]
            po, rpo, h, i = c["po"], c["rpo"], c["h"], c["i"]
            rc = rcs.next()
            S.I("dve", "reciprocal", [rpo], [rc.r], out=rc.t[:, 0:1], in_=po[:, 64:65])
            S.I("dve", "tensor_scalar", [rpo, rc.r], [otok.r], out=otok.t[:, i, h * 64:(h + 1) * 64], in0=po[:, 0:64], scalar1=rc.t[:, 0:1],
                scalar2=None, op0=ALU.mult)
            ctxs[k] = None
        for step in range(N + 3):
            if step < N:
                stA(step)
            if 0 <= step - 3 < N:
                stD(step - 3)
            if 0 <= step - 1 < N:
                stB(step - 1)
            if 0 <= step - 2 < N:
                stC(step - 2)
        K.bank_rr = 0
        bos = sbrot(K, ph, 2, [128, 4, 128], BF16)
        for i in range(34):
            pt, rpt = bank(K)
            for fc in range(4):
                S.I("pe", "matmul", [otok.r, K.ident_bf.r], [rpt], pt[:, fc * 128:(fc + 1) * 128], lhsT=otok.t[:, i, fc * 128:(fc + 1) * 128],
                    rhs=K.ident_bf.t[:, :], start=True, stop=True)
            bo = bos.next()
            bv = bo.t[:, :, :].rearrange("p a b -> p (a b)")
            if i % 2:
                S.I("act", "copy", [rpt], [bo.r], out=bv, in_=pt)
            else:
                S.I("dve", "tensor_copy", [rpt], [bo.r], out=bv, in_=pt)
            S.D("sp", K.MIX[:, 4:8, i * 128:(i + 1) * 128], bo.t[:, :, :], [bo.r], [K.R_MIXb[i]])
        S.flush()


def phaseC(K, S, l):
    M = K.mods[l]
    ntiles = 9 if l == 0 else 8
    last = (l == 1)
    with ExitStack() as ph:
        wout = sbt(K, ph, [128, 8, 1024], BF16)
        wv = K.inp["ab_w_out" if l == 0 else "cd_w_out"].rearrange("(kc p) n -> p kc n", p=128)
        for kc in range(KC):
            S.D("pool", wout.t[:, kc, :], wv[:, kc, :], [], [wout.r])
        wdr = sbt(K, ph, [128, NJ, 1024], BF16)
        wdv = K.inp["ffn_wd"][l].rearrange("(j p) n -> p j n", p=128)
        wdres = [Res() for _ in range(NJ)]
        for jj in range(NJ):
            S.D("pool", wdr.t[:, jj, :], wdv[:, jj, :], [], [wdres[jj]])
        xts = sbrot(K, ph, 2, [128, 8, 512], F32)
        mxs = sbrot(K, ph, 2, [128, 8, 512], BF16)
        sq = sbt(K, ph, [128, 8, 512], BF16)
        rstd = sbt(K, ph, [128, 512], F32)
        tmps = sbrot(K, ph, 2, [128, 512], F32)
        h2 = sbt(K, ph, [128, 8, 512], BF16)
        aT = sbt(K, ph, [128, NJ, 512], BF16)
        wgs = sbrot(K, ph, 2, [128, 8, 256], BF16)
        wus = sbrot(K, ph, 2, [128, 8, 256], BF16)
        sgs = sbrot(K, ph, 2, [128, 512], F32)
        if last:
            fnw = sbt(K, ph, [128, 1024], F32)
            S.D("sp", fnw.t[:, :], K.inp["fnw_bc"], [], [fnw.r])
            ots = sbrot(K, ph, 2, [128, 1024], F32)
            junk = sbt(K, ph, [128, 1024], BF16)
            sss = sbrot(K, ph, 2, [128, 1], F32)
            rss = sbrot(K, ph, 2, [128, 1], F32)
        wgv = [K.WGb[l][m].rearrange("(kc p) n -> p kc n", p=128) for m in range(2)]
        def cload(t):
            n = 512 if t < 8 else 256
            t0 = t * 512
            xt, mx = xts.next(), mxs.next()
            S.D("sp", xt.t[:, :, :n], K.XT[:, :, t0:t0 + n], blkres(K.R_XT, t0, n), [xt.r])
            S.D("sp", mx.t[:, :, :n], K.MIX[:, :, t0:t0 + n], blkres(K.R_MIXa, t0, n) + blkres(K.R_MIXb, t0, n), [mx.r])
            return xt, mx
        nxt = cload(0)
        for t in range(ntiles):
            n = 512 if t < 8 else 256
            t0 = t * 512
            j = 0 if t < 8 else 1
            xt, mx = nxt
            for mo in range(8):
                ps, rps = bank(K)
                for kc in range(KC):
                    S.I("pe", "matmul", [wout.r, mx.r], [rps], ps[:, :n], lhsT=wout.t[:, kc, mo * 128:(mo + 1) * 128], rhs=mx.t[:, kc, :n],
                        start=(kc == 0), stop=(kc == KC - 1))
                S.I("dve", "scalar_tensor_tensor", [rps, xt.r, K.modr], [xt.r], out=xt.t[:, mo, :n], in0=ps[:, :n], scalar=M["Gm"].t[:, mo, j:j + 1],
                    in1=xt.t[:, mo, :n], op0=ALU.mult, op1=ALU.add)
            norm_mod(K, S, xt, n, M["Af"], M["Bf"], j, h2, tmps, sq, rstd)
            if t + 1 < ntiles:
                nxt = cload(t + 1)
            for pc in range(11):
                wg, wu = wgs.next(), wus.next()
                S.D("sp", wg.t[:, :, :], wgv[0][:, :, pc * 256:(pc + 1) * 256], K.R_WG[l][0], [wg.r])
                S.D("sp", wu.t[:, :, :], wgv[1][:, :, pc * 256:(pc + 1) * 256], K.R_WG[l][1], [wu.r])
                for q in range(2):
                    jj = pc * 2 + q
                    pg, rpg = bank(K)
                    pu, rpu = bank(K)
                    for kc in range(KC):
                        S.I("pe", "matmul", [wg.r, h2.r], [rpg], pg[:, :n], lhsT=wg.t[:, kc, q * 128:(q + 1) * 128], rhs=h2.t[:, kc, :n], start=(kc == 0), stop=(kc == KC - 1))
                    for kc in range(KC):
                        S.I("pe", "matmul", [wu.r, h2.r], [rpu], pu[:, :n], lhsT=wu.t[:, kc, q * 128:(q + 1) * 128], rhs=h2.t[:, kc, :n], start=(kc == 0), stop=(kc == KC - 1))
                    sg = sgs.next()
                    S.I("act", "activation", [rpg], [sg.r], out=sg.t[:, :n], in_=pg[:, :n], func=AF.Silu)
                    S.I("dve", "tensor_tensor", [sg.r, rpu], [aT.r], out=aT.t[:, jj, :n], in0=sg.t[:, :n], in1=pu[:, :n], op=ALU.mult)
            for mo in range(8):
                ps, rps = bank(K)
                for jj in range(NJ):
                    S.I("pe", "matmul", [wdres[jj], aT.r], [rps], ps[:, :n], lhsT=wdr.t[:, jj, mo * 128:(mo + 1) * 128], rhs=aT.t[:, jj, :n], start=(jj == 0), stop=(jj == NJ - 1))
                S.I("dve", "scalar_tensor_tensor", [rps, xt.r, K.modr], [xt.r], out=xt.t[:, mo, :n], in0=ps[:, :n], scalar=M["Gf"].t[:, mo, j:j + 1],
                    in1=xt.t[:, mo, :n], op0=ALU.mult, op1=ALU.add)
            if not last:
                S.D("sp", K.XT[:, :, t0:t0 + n], xt.t[:, :, :n], [xt.r], blkres(K.R_XT, t0, n))
            else:
                for sbk in range(4):
                    blk = t * 4 + sbk
                    pt, rpt = bank2(K)
                    for kc in range(KC):
                        S.I("pe", "transpose", [xt.r, K.ident.r], rpt, out=pt[:, kc * 128:(kc + 1) * 128], in_=xt.t[:, kc, sbk * 128:(sbk + 1) * 128], identity=K.ident.t[:, :])
                    ss, rs, ot = sss.next(), rss.next(), ots.next()
                    S.I("dve", "memset", [], [ss.r], ss.t[:, :], 0.0)
                    S.I("act", "activation", rpt + [ss.r], [junk.r, ss.r], out=junk.t[:, :], in_=pt, func=AF.Square, accum_out=ss.t[:, 0:1])
                    small_rstd(K, S, ss, rs, 1024)
                    S.I("dve", "scalar_tensor_tensor", rpt + [rs.r, fnw.r], [ot.r], out=ot.t[:, :], in0=pt, scalar=rs.t[:, 0:1], in1=fnw.t[:, :],
                        op0=ALU.mult, op1=ALU.mult)
                    S.D("sp", K.out[blk * 128:(blk + 1) * 128, :], ot.t[:, :], [ot.r], [K.R_out])
        S.flush()


def phaseD(K, S):
    with ExitStack() as ph:
        fnw = sbt(K, ph, [128, 1024], F32)
        S.D("sp", fnw.t[:, :], K.inp["fnw_bc"], [], [fnw.r])
        xbs = sbrot(K, ph, 2, [128, 8, 128], F32)
        ots = sbrot(K, ph, 2, [128, 1024], F32)
        junk = sbt(K, ph, [128, 1024], BF16)
        sss = sbrot(K, ph, 2, [128, 1], F32)
        rss = sbrot(K, ph, 2, [128, 1], F32)
        for blk in range(32):
            xb = xbs.next()
            S.D("sp", xb.t[:, :, :], K.XT[:, :, blk * 128:(blk + 1) * 128], [K.R_XT[blk]], [xb.r])
            pt, rpt = bank2(K)
            for kc in range(KC):
                S.I("pe", "transpose", [xb.r, K.ident.r], rpt, out=pt[:, kc * 128:(kc + 1) * 128], in_=xb.t[:, kc, :], identity=K.ident.t[:, :])
            ss, rs, ot = sss.next(), rss.next(), ots.next()
            S.I("dve", "memset", [], [ss.r], ss.t[:, :], 0.0)
            S.I("act", "activation", rpt + [ss.r], [junk.r, ss.r], out=junk.t[:, :], in_=pt, func=AF.Square, accum_out=ss.t[:, 0:1])
            small_rstd(K, S, ss, rs, 1024)
            S.I("dve", "scalar_tensor_tensor", rpt + [rs.r, fnw.r], [ot.r], out=ot.t[:, :], in0=pt, scalar=rs.t[:, 0:1], in1=fnw.t[:, :],
                op0=ALU.mult, op1=ALU.mult)
            S.D("sp", K.out[blk * 128:(blk + 1) * 128, :], ot.t[:, :], [ot.r], [K.R_out])
        S.flush()


def l1_decl(K, din):
    din("cd_w_in", [D, 2080]); din("cd_w_perm", [D, 512]); din("decw", [2, 17, 256]); din("hnw", [128, 1])
    din("tri", [128, 4, 128]); din("gmask", [128, 2, 128])
    din("ropeC", [128, NX]); din("ropeS", [128, NX]); din("ropeCt", [32, 128, 256]); din("ropeSt", [32, 128, 256])
    din("w128ri", [128, 256]); din("wc1", [128, 256]); din("wc2", [128, 256]); din("tw", [128, 32, 2, 128])


def l1_scratch(K, scr):
    K.HT = scr("HT", [128, 8, NX], BF16)
    K.GQ = scr("GQ", [128, 2, NT], BF16)
    K.GK = scr("GK", [128, 2, NT], BF16)
    K.KTOK = scr("KTOK", [34, 128, 256], BF16)
    K.VTOK1 = scr("VTOK1", [34, 128, 512], BF16)
    K.SG = scr("SG", [128, 4, NX], BF16)
    K.ALR = scr("ALR", [2, 16, NT], F32)
    K.QE = [scr("QE%d" % d, [128, 2, NX], BF16) for d in range(2)]
    K.KE = [scr("KE%d" % d, [128, 2, NX], BF16) for d in range(2)]
    for nm in ("HT", "GQ", "GK", "KTOK", "VTOK1", "SG", "ALR", "QE0", "QE1", "KE0", "KE1"):
        setattr(K, "R_" + nm, [Res() for _ in range(34)])


def phaseA1(K, S):
    M = K.mods[1]
    with ExitStack() as ph:
        win = sbt(K, ph, [128, 8, 2080], BF16)
        wv = K.inp["cd_w_in"].rearrange("(kc p) n -> p kc n", p=128)
        for kc in range(KC):
            S.D("pool", win.t[:, kc, :], wv[:, kc, :], [], [win.r])
        wpm = sbt(K, ph, [128, 8, 512], BF16)
        S.D("pool", wpm.t[:, :, :], K.inp["cd_w_perm"].rearrange("(kc p) n -> p kc n", p=128), [], [wpm.r])
        xts = sbrot(K, ph, 2, [128, 8, 512], F32)
        sqs = sbrot(K, ph, 2, [128, 8, 512], BF16)
        rstds = sbrot(K, ph, 2, [128, 512], F32)
        tmps = sbrot(K, ph, 2, [128, 512], F32)
        hTs = sbrot(K, ph, 2, [128, 8, 512], BF16)
        rcs = sbrot(K, ph, 2, [128, 512], F32)
        rss = sbrot(K, ph, 2, [128, 512], F32)
        t1s = sbrot(K, ph, 2, [128, 512], F32)
        t2s = sbrot(K, ph, 2, [128, 512], F32)
        qos = sbrot(K, ph, 2, [128, 2, 512], BF16)
        kos = sbrot(K, ph, 2, [128, 2, 512], BF16)
        sgs = sbrot(K, ph, 2, [128, 4, 512], BF16)
        als = sbrot(K, ph, 2, [16, 2, 512], F32)
        vos = sbrot(K, ph, 2, [128, 512], BF16)
        cts = sbrot(K, ph, 2, [128, 256], F32)
        sts = sbrot(K, ph, 2, [128, 256], F32)
        u1s = sbrot(K, ph, 2, [128, 256], F32)
        u2s = sbrot(K, ph, 2, [128, 256], F32)
        ktos = sbrot(K, ph, 2, [128, 256], BF16)
        def tile(t):
            n = 512 if t < 8 else 256
            t0 = t * 512
            isx = t < 8
            j = 0 if isx else 1
            xt = xts.next()
            hT = hTs.next()
            sq, rstd = sqs.next(), rstds.next()
            S.D("sp", xt.t[:, :, :n], K.XT[:, :, t0:t0 + n], blkres(K.R_XT, t0, n), [xt.r])
            norm_mod(K, S, xt, n, M["Am"], M["Bm"], j, hT, tmps, sq, rstd)
            yield
            if isx:
                S.D("sp", K.HT[:, :, t0:t0 + n], hT.t[:, :, :n], [hT.r], blkres(K.R_HT, t0, n))
                rc, rs = rcs.next(), rss.next()
                S.D("sp", rc.t[:, :], K.inp["ropeC"][:, t0:t0 + n], [], [rc.r])
                S.D("sp", rs.t[:, :], K.inp["ropeS"][:, t0:t0 + n], [], [rs.r])
            qo, ko = qos.next(), kos.next()
            for c in (range(4) if isx else (2, 3)):
                p1, rp1 = bank(K)
                for kc in range(KC):
                    S.I("pe", "matmul", [win.r, hT.r], [rp1], p1[:, :n], lhsT=win.t[:, kc, 512 + c * 128:512 + (c + 1) * 128], rhs=hT.t[:, kc, :n],
                        start=(kc == 0), stop=(kc == KC - 1))
                dst = qo if c < 2 else ko
                if not isx:
                    S.I("act", "copy", [rp1], [dst.r], out=dst.t[:, c % 2, :n], in_=p1[:, :n])
                    continue
                p2, rp2 = bank(K)
                for kc in range(KC):
                    S.I("pe", "matmul", [wpm.r, hT.r], [rp2], p2[:, :n], lhsT=wpm.t[:, kc, c * 128:(c + 1) * 128], rhs=hT.t[:, kc, :n],
                        start=(kc == 0), stop=(kc == KC - 1))
                sc = 0.125 if c < 2 else 1.0
                t1, t2 = t1s.next(), t2s.next()
                S.I("dve", "scalar_tensor_tensor", [rp1, rc.r], [t1.r], out=t1.t[:, :n], in0=p1[:, :n], scalar=sc, in1=rc.t[:, :n], op0=ALU.mult, op1=ALU.mult)
                S.I("dve", "scalar_tensor_tensor", [rp2, rs.r], [t2.r], out=t2.t[:, :n], in0=p2[:, :n], scalar=sc, in1=rs.t[:, :n], op0=ALU.mult, op1=ALU.mult)
                S.I("pool", "tensor_tensor", [t1.r, t2.r], [dst.r], out=dst.t[:, c % 2, :n], in0=t1.t[:, :n], in1=t2.t[:, :n], op=ALU.add)
            if isx:
                S.D("sp", K.GQ[:, :, t0:t0 + n], qo.t[:, :, :n], [qo.r], blkres(K.R_GQ, t0, n))
            S.D("sp", K.GK[:, :, t0:t0 + n], ko.t[:, :, :n], [ko.r], blkres(K.R_GK, t0, n))
            if isx:
                sg = sgs.next()
                for c in range(4):
                    p1, rp1 = bank(K)
                    for kc in range(KC):
                        S.I("pe", "matmul", [win.r, hT.r], [rp1], p1[:, :n], lhsT=win.t[:, kc, 1536 + c * 128:1536 + (c + 1) * 128], rhs=hT.t[:, kc, :n],
                            start=(kc == 0), stop=(kc == KC - 1))
                    S.I("act", "activation", [rp1], [sg.r], out=sg.t[:, c, :n], in_=p1[:, :n], func=AF.Silu)
                S.D("sp", K.SG[:, :, t0:t0 + n], sg.t[:, :, :n], [sg.r], blkres(K.R_SG, t0, n))
            al = als.next()
            for d in range(2):
                p1, rp1 = bank(K)
                for kc in range(KC):
                    S.I("pe", "matmul", [win.r, hT.r], [rp1], p1[0:16, :n], lhsT=win.t[:, kc, 2048 + d * 16:2064 + d * 16], rhs=hT.t[:, kc, :n],
                        start=(kc == 0), stop=(kc == KC - 1))
                S.I("dve", "tensor_copy", [rp1], [al.r], out=al.t[:, d, :n], in_=p1[0:16, :n])
            for d in range(2):
                S.D("sp", K.ALR[d, :, t0:t0 + n], al.t[:, d, :n], [al.r], blkres(K.R_ALR, t0, n))
            yield
            for sbk in range(n // 128):
                blk = t * 4 + sbk
                tk = slice(sbk * 128, (sbk + 1) * 128)
                pv, rpv = bank(K)
                for kc in range(KC):
                    S.I("pe", "matmul", [win.r, hT.r], [rpv], pv, lhsT=hT.t[:, kc, tk], rhs=win.t[:, kc, 1024:1536], start=(kc == 0), stop=(kc == KC - 1))
                vo = vos.next()
                S.I("act", "copy", [rpv], [vo.r], out=vo.t[:, :], in_=pv)
                S.D("sp", K.VTOK1[blk], vo.t[:, :], [vo.r], [K.R_VTOK1[blk]])
                pk, rpk = bank(K)
                for kc in range(KC):
                    S.I("pe", "matmul", [win.r, hT.r], [rpk], pk[:, 0:256], lhsT=hT.t[:, kc, tk], rhs=win.t[:, kc, 768:1024], start=(kc == 0), stop=(kc == KC - 1))
                kto = ktos.next()
                if isx:
                    for kc in range(KC):
                        S.I("pe", "matmul", [wpm.r, hT.r], [rpk], pk[:, 256:512], lhsT=hT.t[:, kc, tk], rhs=wpm.t[:, kc, 256:512], start=(kc == 0), stop=(kc == KC - 1))
                    ct, st, u1, u2 = cts.next(), sts.next(), u1s.next(), u2s.next()
                    S.D("sp", ct.t[:, :], K.inp["ropeCt"][blk], [], [ct.r])
                    S.D("sp", st.t[:, :], K.inp["ropeSt"][blk], [], [st.r])
                    S.I("dve", "tensor_tensor", [rpk, ct.r], [u1.r], out=u1.t[:, :], in0=pk[:, 0:256], in1=ct.t[:, :], op=ALU.mult)
                    S.I("dve", "tensor_tensor", [rpk, st.r], [u2.r], out=u2.t[:, :], in0=pk[:, 256:512], in1=st.t[:, :], op=ALU.mult)
                    S.I("pool", "tensor_tensor", [u1.r, u2.r], [kto.r], out=kto.t[:, :], in0=u1.t[:, :], in1=u2.t[:, :], op=ALU.add)
                else:
                    S.I("dve", "tensor_copy", [rpk], [kto.r], out=kto.t[:, :], in_=pk[:, 0:256])
                S.D("sp", K.KTOK[blk], kto.t[:, :], [kto.r], [K.R_KTOK[blk]])
        gens = [tile(t) for t in range(9)]
        next(gens[0])
        for t in range(9):
            next(gens[t])
            if t + 1 < 9:
                next(gens[t + 1])
            for _ in gens[t]:
                pass
        S.flush()


def phaseG(K, S):
    with ExitStack() as ph:
        tri = sbt(K, ph, [128, 4, 128], BF16)
        S.D("pool", tri.t[:, :, :], K.inp["tri"], [], [tri.r])
        gmask = sbt(K, ph, [128, 2, 128], F32)
        S.D("sp", gmask.t[:, :, :], K.inp["gmask"], [], [gmask.r])
        decw = [sbt(K, ph, [17, 256], F32) for _ in range(2)]
        for d in range(2):
            S.D("sp", decw[d].t[:, :], K.inp["decw"][d], [], [decw[d].r])
        hnw = sbt(K, ph, [128, 1], F32)
        S.D("sp", hnw.t[:, :], K.inp["hnw"], [], [hnw.r])
        vtk = sbt(K, ph, [128, 34, 512], BF16)
        S.D("sp", vtk.t[:, :, :], K.VTOK1.rearrange("b p e -> p b e"), K.R_VTOK1, [vtk.r])
        ktk = sbt(K, ph, [128, 34, 256], BF16)
        S.D("sp", ktk.t[:, :, :], K.KTOK.rearrange("b p e -> p b e"), K.R_KTOK, [ktk.r])
        sst = sbt(K, ph, [128, 2 * 64 * 2, 128], BF16)
        scurs = [sbt(K, ph, [128, 2, 128], F32) for _ in range(2)]
        for sc_ in scurs:
            S.I("dve", "memset", [], [sc_.r], sc_.t[:, :, :], 0.0)
        p1 = ph
        if True:
            alrs = sbrot(K, p1, 5, [17, 128], F32)
            for a in alrs.items:
                S.I("pool", "memset", [], [a.r], a.t[:, :], 1.0)
            e1s = sbrot(K, p1, 2, [128, 256], F32)
            Ls = sbrot(K, p1, 3, [128, 256], BF16)
            ebs = sbrot(K, p1, 3, [128, 2, 128], F32)
            enbs = sbrot(K, p1, 2, [128, 2, 128], F32)
            gqs = sbrot(K, p1, 6, [128, 2, 128], BF16)
            gks = sbrot(K, p1, 6, [128, 2, 128], BF16)
            qes = sbrot(K, p1, 3, [128, 2, 128], BF16)
            kes = sbrot(K, p1, 3, [128, 2, 128], BF16)
            edss = sbrot(K, p1, 2, [128, 256], F32)
            kds = sbrot(K, p1, 3, [128, 256], BF16)
            seq = []
            for d in range(2):
                order = [32, 33] + list(range(32)) if d == 0 else [33, 32] + list(range(31, -1, -1))
                for blk in order:
                    seq.append((d, blk))
            NS = len(seq)
            cx = [None] * NS
            def next_slot(d, blk, c):
                cs = (0, 1) if d == 0 else (1, 0)
                order = [32, 33] + list(range(32)) if d == 0 else [33, 32] + list(range(31, -1, -1))
                chunks = [(bk, cc) for bk in order for cc in cs]
                i = chunks.index((blk, c))
                if i + 1 >= len(chunks):
                    return None
                nb_, nc_ = chunks[i + 1]
                if nb_ >= 32:
                    return None
                return (d * 64 + nb_ * 2 + nc_) * 2

            pre = [None] * NS

            def P0(k):
                d, blk = seq[k]
                tok0 = blk * 128
                al = alrs.next()
                S.D("sp", al.t[0:16, :], K.ALR[d, :, tok0:tok0 + 128], [K.R_ALR[blk]], [al.r])
                gq = gk = None
                if blk < 32:
                    gq, gk = gqs.next(), gks.next()
                    S.D("sp", gq.t[:, :, :], K.GQ[:, :, tok0:tok0 + 128], [K.R_GQ[blk]], [gq.r])
                    S.D("sp", gk.t[:, :, :], K.GK[:, :, tok0:tok0 + 128], [K.R_GK[blk]], [gk.r])
                pre[k] = (al, gq, gk)

            def P1(k):
                d, blk = seq[k]
                tok0 = blk * 128
                al = pre[k][0]
                z, rz = bank(K)
                S.I("pe", "matmul", [al.r, decw[d].r], [rz], z[:, 0:256], lhsT=al.t[0:17, :], rhs=decw[d].t[0:17, :], start=True, stop=True)
                e1, L = e1s.next(), Ls.next()
                S.I("act", "activation", [rz], [e1.r], out=e1.t[:, :], in_=z[:, 0:256], func=AF.Exp, scale=-1.0)
                S.I("act", "activation", [e1.r], [L.r], out=L.t[:, :], in_=e1.t[:, :], func=AF.Ln, bias=1.0)
                cx[k] = dict(L=L)

            def P2(k):
                d, blk = seq[k]
                tok0 = blk * 128
                isx = blk < 32
                L = cx[k]["L"]
                bT, rbT = bank(K)
                for hp in range(2):
                    S.I("pe", "matmul", [L.r, tri.r], [rbT], bT[:, hp * 128:(hp + 1) * 128], lhsT=L.t[:, hp * 128:(hp + 1) * 128], rhs=tri.t[:, 2 * d, :],
                        start=True, stop=True)
                ds, rds = bank(K)
                S.I("pe", "matmul", [L.r, tri.r], [rds], ds[:, 0:256], lhsT=tri.t[:, 2 * d + 1, :], rhs=L.t[:, :], start=True, stop=True)
                eb = ebs.next()
                S.I("act", "activation", [rbT], [eb.r], out=eb.t[:, :, :].rearrange("p a b -> p (a b)"), in_=bT[:, 0:256], func=AF.Exp)
                eds, kd = edss.next(), kds.next()
                S.I("act", "activation", [rds], [eds.r], out=eds.t[:, :], in_=ds[:, 0:256], func=AF.Exp)
                S.I("dve", "tensor_tensor", [ktk.r, eds.r], [kd.r], out=kd.t[:, :], in0=ktk.t[:, blk, :], in1=eds.t[:, :], op=ALU.mult)
                if isx:
                    enb, qe, ke = enbs.next(), qes.next(), kes.next()
                    gq, gk = pre[k][1], pre[k][2]
                    S.I("act", "activation", [rbT], [enb.r], out=enb.t[:, :, :].rearrange("p a b -> p (a b)"), in_=bT[:, 0:256], func=AF.Exp, scale=-1.0)
                    S.I("dve", "tensor_tensor", [gq.r, eb.r], [qe.r], out=qe.t[:, :, :], in0=gq.t[:, :, :], in1=eb.t[:, :, :], op=ALU.mult)
                    S.I("pool", "tensor_tensor", [gk.r, enb.r], [ke.r], out=ke.t[:, :, :], in0=gk.t[:, :, :], in1=enb.t[:, :, :], op=ALU.mult)
                    S.D("sp", K.QE[d][:, :, tok0:tok0 + 128], qe.t[:, :, :], [qe.r], [getattr(K, "R_QE%d" % d)[blk]])
                    S.D("sp", K.KE[d][:, :, tok0:tok0 + 128], ke.t[:, :, :], [ke.r], [getattr(K, "R_KE%d" % d)[blk]])
                cx[k]["eb"] = eb
                cx[k]["kd"] = kd

            def P3(k):
                d, blk = seq[k]
                eb, kd = cx[k]["eb"], cx[k]["kd"]
                scur = scurs[d]
                for c in ((0, 1) if d == 0 else (1, 0)):
                    cs = slice(c * 64, (c + 1) * 64)
                    kv, rkv = bank(K)
                    for hp in range(2):
                        for hh in range(2):
                            h = 2 * hp + hh
                            S.I("pe", "matmul", [kd.r, vtk.r], [rkv], kv[hh * 64:(hh + 1) * 64, hp * 128:(hp + 1) * 128], lhsT=kd.t[cs, h * 64:(h + 1) * 64],
                                rhs=vtk.t[cs, blk, h * 128:(h + 1) * 128], start=True, stop=True)
                    col = c * 64 + 63 if d == 0 else c * 64
                    slot = next_slot(d, blk, c)
                    for hp in range(2):
                        if slot is not None:
                            S.I("dve", "scalar_tensor_tensor", [scur.r, eb.r, rkv], [sst.r], out=sst.t[:, slot + hp, :], in0=scur.t[:, hp, :],
                                scalar=eb.t[:, hp, col:col + 1], in1=kv[:, hp * 128:(hp + 1) * 128], op0=ALU.mult, op1=ALU.add)
                        S.I("dve", "scalar_tensor_tensor", [scur.r, eb.r, rkv], [scur.r], out=scur.t[:, hp, :], in0=scur.t[:, hp, :],
                            scalar=eb.t[:, hp, col:col + 1], in1=kv[:, hp * 128:(hp + 1) * 128], op0=ALU.mult, op1=ALU.add)
                cx[k] = None
            P0(0)
            P0(1)
            for st in range(NS + 2):
                if st + 2 < NS:
                    P0(st + 2)
                if st < NS:
                    P1(st)
                if 0 <= st - 2 < NS:
                    P3(st - 2)
                if 0 <= st - 1 < NS:
                    P2(st - 1)
        K.bank_pool = [0, 1, 2, 3, 4, 5]
        K.bank_rr = 0
        p2 = ph
        if True:
            qets = [sbrot(K, p2, 2, [128, 2, 512], BF16) for _ in range(2)]
            kets = [sbrot(K, p2, 2, [128, 2, 512], BF16) for _ in range(2)]
            sgs = sbrot(K, p2, 2, [128, 4, 512], BF16)
            gos = sbrot(K, p2, 2, [128, 4, 512], BF16)
            atms = sbrot(K, p2, 4, [128, 128], BF16)
            sqo = sbt(K, p2, [128, 512], BF16)
            rst = sbt(K, p2, [128, 512], F32)
            t1 = sbt(K, p2, [128, 512], F32)
            nacc = 0
            def g2load(T):
                t0 = T * 512
                qe = [qets[d].next() for d in range(2)]
                ke = [kets[d].next() for d in range(2)]
                for d in range(2):
                    S.D("sp", qe[d].t[:, :, :], K.QE[d][:, :, t0:t0 + 512], blkres(getattr(K, "R_QE%d" % d), t0, 512), [qe[d].r])
                    S.D("sp", ke[d].t[:, :, :], K.KE[d][:, :, t0:t0 + 512], blkres(getattr(K, "R_KE%d" % d), t0, 512), [ke[d].r])
                sg = sgs.next()
                S.D("sp", sg.t[:, :, :], K.SG[:, :, t0:t0 + 512], blkres(K.R_SG, t0, 512), [sg.r])
                return qe, ke, sg
            nxt = g2load(0)
            for T in range(8):
                t0 = T * 512
                qe, ke, sg = nxt
                if T + 1 < 8:
                    nxt = g2load(T + 1)
                go = gos.next()
                for h in range(4):
                    hp, hb = h // 2, (h % 2) * 64
                    oT, roT = bank_fixed(K, 6 + (nacc % 2))
                    nacc += 1
                    for bb in range(4):
                        blk = T * 4 + bb
                        cols = slice(bb * 128, (bb + 1) * 128)
                        atm = []
                        for d in range(2):
                            att, ratt = bank(K)
                            S.I("pe", "matmul", [ke[d].r, qe[d].r], [ratt], att[:, 0:128], lhsT=ke[d].t[hb:hb + 64, hp, cols], rhs=qe[d].t[hb:hb + 64, hp, cols],
                                start=True, stop=True)
                            am = atms.next()
                            S.I("dve", "tensor_tensor", [ratt, gmask.r], [am.r], out=am.t[:, :], in0=att[:, 0:128], in1=gmask.t[:, d, :], op=ALU.mult)
                            atm.append(am)
                        S.I("pe", "matmul", [vtk.r, atm[0].r], [roT], oT[:, cols], lhsT=vtk.t[:, blk, h * 128:(h + 1) * 128], rhs=atm[0].t[:, :], start=True, stop=False)
                        S.I("pe", "matmul", [vtk.r, atm[1].r], [roT], oT[:, cols], lhsT=vtk.t[:, blk, h * 128:(h + 1) * 128], rhs=atm[1].t[:, :], start=False, stop=False)
                        for c in range(2):
                            for d in range(2):
                                nch = blk * 2 + c
                                base = (d * 64 + nch) * 2 + hp
                                cc = slice(bb * 128 + c * 64, bb * 128 + (c + 1) * 64)
                                S.I("pe", "matmul", [sst.r, qe[d].r], [roT], oT[:, cc], lhsT=sst.t[hb:hb + 64, base, :], rhs=qe[d].t[hb:hb + 64, hp, cc],
                                    start=False, stop=(c == 1 and d == 1))
                    S.I("act", "activation", [roT], [sqo.r], out=sqo.t[:, :], in_=oT, func=AF.Square)
                    ms, rms = bank(K)
                    S.I("pe", "matmul", [K.ones_bf.r, sqo.r], [rms], ms, lhsT=K.ones_bf.t[:, :], rhs=sqo.t[:, :], start=True, stop=True)
                    S.I("act", "activation", [rms], [rst.r], out=rst.t[:, :], in_=ms, func=AF.Ln, scale=1.0 / 128, bias=K.epsc.t[:, 0:1])
                    S.I("act", "activation", [rst.r], [rst.r], out=rst.t[:, :], in_=rst.t[:, :], func=AF.Exp, scale=-0.5)
                    S.I("dve", "tensor_tensor", [roT, rst.r], [t1.r], out=t1.t[:, :], in0=oT, in1=rst.t[:, :], op=ALU.mult)
                    S.I("dve", "scalar_tensor_tensor", [t1.r, hnw.r, sg.r], [go.r], out=go.t[:, h, :], in0=t1.t[:, :], scalar=hnw.t[:, 0:1], in1=sg.t[:, h, :],
                        op0=ALU.mult, op1=ALU.mult)
                S.D("sp", K.MIX[:, 4:8, t0:t0 + 512], go.t[:, :, :], [go.r], blkres(K.R_MIXb, t0, 512))
        K.bank_pool = None
        K.bank_rr = 0
        S.flush()


def phaseF(K, S):
    with ExitStack() as ph:
        X = sbt(K, ph, [128, 32768], BF16)
        Y = sbt(K, ph, [128, 32768], BF16)
        wf = sbt(K, ph, [128, 8, 512], BF16)
        S.D("pool", wf.t[:, :, :], K.inp["cd_w_in"].rearrange("(kc p) n -> p kc n", p=128)[:, :, 0:512], [], [wf.r])
        w128 = sbt(K, ph, [128, 256], BF16)
        S.D("pool", w128.t[:, :], K.inp["w128ri"], [], [w128.r])
        wc = [sbt(K, ph, [128, 256], BF16) for _ in range(2)]
        S.D("pool", wc[0].t[:, :], K.inp["wc1"], [], [wc[0].r])
        S.D("pool", wc[1].t[:, :], K.inp["wc2"], [], [wc[1].r])
        tw = sbt(K, ph, [128, 32, 2, 128], BF16)
        S.D("pool", tw.t[:, :, :, :], K.inp["tw"], [], [tw.r])
        hTv = X.t[:, :].rearrange("p (k a b) -> p k a b", k=8, b=32)
        for kc in range(KC):
            S.D("sp", X.t[:, kc * 4096:(kc + 1) * 4096], K.HT[:, kc, :], K.R_HT[:32], [X.r])
        F1 = Y.t[:, 0:16384].rearrange("p (a b) -> p a b", b=512)
        nev = [0]

        def evac(reads, writes, out, in_):
            nev[0] += 1
            if nev[0] % 2:
                S.I("act", "copy", reads, writes, out=out, in_=in_)
            else:
                S.I("dve", "tensor_copy", reads, writes, out=out, in_=in_)
        for n2 in range(32):
            ps, rps = bank(K)
            for kc in range(KC):
                S.I("pe", "matmul", [X.r, wf.r], [rps], ps, lhsT=hTv[:, kc, :, n2], rhs=wf.t[:, kc, :], start=(kc == 0), stop=(kc == KC - 1))
            evac([rps], [Y.r], F1[:, n2, :], ps)
        A2 = X.t[:, :].rearrange("p (g r h n l) -> p g r h n l", g=4, r=2, h=32, n=32)
        for n2 in range(32):
            for g in range(4):
                ps, rps = bank(K)
                S.I("pe", "matmul", [Y.r, w128.r], [rps], ps[:, 0:256], lhsT=F1[:, n2, g * 128:(g + 1) * 128], rhs=w128.t[:, :], start=True, stop=True)
                evac([rps], [X.r], A2[:, g, :, :, n2, :], ps[:, 0:256].rearrange("p (r h l) -> p r h l", r=2, l=4))
        Z3 = Y.t[:, :].rearrange("p (g h r c) -> p g h r c", g=4, h=32, r=2)
        for g in range(4):
            for kh in range(32):
                ps, rps = bank(K)
                for ri in range(2):
                    off = ((g * 2 + ri) * 32 + kh) * 128
                    S.I("pe", "matmul", [X.r, wc[ri].r], [rps], ps[:, 0:256], lhsT=X.t[:, off:off + 128], rhs=wc[ri].t[:, :], start=(ri == 0), stop=(ri == 1))
                evac([rps], [Y.r], Z3[:, g, kh, :, :], ps[:, 0:256].rearrange("p (r c) -> p r c", r=2))
        fn = X.t[:, 0:16384].rearrange("p (g a h l) -> p g a h l", g=4, a=32, h=32)
        scale = float(1.0 / np.sqrt(4096.0 * 128.0))
        for g in range(4):
            for kh in range(32):
                ps, rps = bank(K)
                for ri in range(2):
                    S.I("pe", "matmul", [Y.r, tw.r], [rps], ps[:, 0:128], lhsT=Z3[:, g, kh, ri, :], rhs=tw.t[:, kh, ri, :], start=(ri == 0), stop=(ri == 1))
                nev[0] += 1
                src = ps[:, 0:128].rearrange("p (a l) -> p a l", l=4)
                if nev[0] % 2:
                    S.I("act", "mul", [rps], [X.r], out=fn[:, g, :, kh, :], in_=src, mul=scale)
                else:
                    S.I("dve", "tensor_scalar", [rps], [X.r], out=fn[:, g, :, kh, :], in0=src, scalar1=scale, scalar2=None, op0=ALU.mult)
        for g in range(4):
            S.D("sp", K.MIX[:, g, 0:NX], X.t[:, g * 4096:(g + 1) * 4096], [X.r], K.R_MIXa[:32])
        S.flush()


def l1_run(K, S):
    phaseA1(K, S)
    phaseG(K, S)
    phaseF(K, S)


def l1_prep(I, shared):
    f = lambda a: np.ascontiguousarray(np.asarray(a, dtype=np.float32))
    w_in = I["cd_w_in"][0]
    shared["cd_w_in"] = f(w_in)
    d = np.arange(64)
    e = d % 32
    partner = np.where(e < 16, d + 16, d - 16)
    perm = (np.arange(8)[:, None] * 64 + partner[None, :]).reshape(-1)
    shared["cd_w_perm"] = f(w_in[:, 512:1024][:, perm])
    shared["decw"] = f(np.stack([np.concatenate([I["cd_decay_w_fwd"][0], I["cd_decay_b_fwd"][0][None]], 0),
                                 np.concatenate([I["cd_decay_w_bwd"][0], I["cd_decay_b_bwd"][0][None]], 0)], 0))
    shared["hnw"] = f(I["cd_head_norm_w"][0].reshape(128, 1))
    m = np.arange(128)[:, None]
    l = np.arange(128)[None, :]
    sc = (m // 64) == (l // 64)
    tri = np.stack([sc & (m <= l), sc & (m > l), sc & (m >= l), sc & (m < l)], 1).astype(np.float32) * (-1.0 / 16.0)
    shared["tri"] = f(tri)
    shared["gmask"] = f(np.stack([sc & (m <= l), sc & (m >= l)], 1).astype(np.float32))
    t = np.arange(NX)
    fi = e % 16
    inv = 10000.0 ** (-(fi.astype(np.float64)) / 16.0)
    pos = np.where((d // 32)[:, None] == 0, (t // 64)[None, :], (t % 64)[None, :]).astype(np.float64)
    ang = pos * inv[:, None]
    C64 = np.cos(ang)
    S64 = np.sin(ang) * np.where(e < 16, -1.0, 1.0)[:, None]
    shared["ropeC"] = f(np.concatenate([C64, C64], 0))
    shared["ropeS"] = f(np.concatenate([S64, S64], 0))
    Ct = np.tile(C64.T, (1, 4)).reshape(32, 128, 256)
    St = np.tile(S64.T, (1, 4)).reshape(32, 128, 256)
    shared["ropeCt"] = f(Ct)
    shared["ropeSt"] = f(St)
    n = np.arange(128)
    a128 = 2 * np.pi * np.outer(n, n) / 128.0
    Cn, Sn = np.cos(a128), np.sin(a128)
    shared["w128ri"] = f(np.concatenate([Cn, -Sn], 1))
    shared["wc1"] = f(np.concatenate([Cn, -Sn], 1))
    shared["wc2"] = f(np.concatenate([Sn, Cn], 1))
    n2 = np.arange(32)
    k2 = np.arange(32)
    TW = np.zeros((32, 4, 32, 32, 4), np.complex128)
    for kh in range(32):
        for kl in range(4):
            k1 = 4 * kh + kl
            TW[:, kl, kh, :, kl] = np.exp(-2j * np.pi * n2 * k1 / 4096.0)[:, None] * np.exp(-2j * np.pi * np.outer(n2, k2) / 32.0)
    TW = TW.reshape(128, 32, 128)
    shared["tw"] = f(np.stack([TW.real, -TW.imag], 2))


LAYER1 = {"decl": l1_decl, "scratch": l1_scratch, "run": l1_run, "prep": l1_prep}


def _na_bias_tiles(rel_bias):
    NEG = np.float32(-30000.0)
    H = rel_bias.shape[0]
    out = np.full((H, 128, 21, 128), NEG, np.float32)
    kc = np.arange(64)
    qc = np.arange(64)
    cs = np.clip(qc - 8, 0, 48)
    colok = (kc[:, None] >= cs[None, :]) & (kc[:, None] < cs[None, :] + 16)
    cidx = np.clip(kc[:, None] - qc[None, :] + 15, 0, 30)

    def tile(i, m):
        t = np.full((H, 128, 128), NEG, np.float32)
        for a in range(2):
            kr = 2 * m + a
            for b in range(2):
                qr = 2 * i + b
                rs = min(max(qr - 4, 0), 56)
                if not (rs <= kr < rs + 8):
                    continue
                ridx = kr - qr + 7
                vals = rel_bias[:, ridx, :][:, cidx]
                vals = np.where(colok[None], vals, NEG)
                t[:, a * 64:(a + 1) * 64, b * 64:(b + 1) * 64] = vals
        return t
    for d in range(5):
        out[:, :, d, :] = tile(10, 10 + d - 2)
    for s, i in enumerate((0, 1, 30, 31)):
        ms = list(range(4)) if i < 2 else list(range(28, 32))
        for ci, m in enumerate(ms):
            out[:, :, 5 + 4 * s + ci, :] = tile(i, m)
    return out


def _consts():
    c = {}
    c["ident"] = np.eye(128, dtype=np.float32)
    return c


def build_program():
    nc = bass.Bass("TRN2", target_bir_lowering=False)
    K = KB()
    K.nc = nc
    K.bank_rr = 0
    K.inp = {}

    def din(name, shape, dt=F32):
        K.inp[name] = nc.dram_tensor(name, list(shape), dt, kind="ExternalInput").ap()
    din("x", [NX, D]); din("ctx", [NCX, D]); din("csT", [128, 8, 2]); din("ada_w", [2, D, 6 * D]); din("ada_bT", [2, 128, 48])
    din("nmw", [2, 128, 8]); din("nfw", [2, 128, 8]); din("fnw_bc", [128, D])
    din("ffn_wg", [2, D, FH]); din("ffn_wu", [2, D, FH]); din("ffn_wd", [2, FH, D])
    din("ab_w_in", [D, 2560]); din("ab_w_out", [D, D]); din("sgu_nw_bc", [128, 512]); din("sguWT", [128, 4, 128]); din("sgub", [128, 512])
    din("nab", [8, 128, 21, 128]); din("ident", [128, 128])
    din("cd_w_out", [D, D])
    if LAYER1 is not None:
        LAYER1["decl"](K, din)
    K.out = nc.dram_tensor("out", [NX, D], F32, kind="ExternalOutput").ap()
    K.R_out = Res()

    def scr(name, shape, dt):
        kind = "ExternalOutput" if (DEBUG and name in DEBUG) else "Internal"
        return nc.dram_tensor(name, list(shape), dt, kind=kind).ap()
    K.XT = scr("XT", [128, 8, NT], F32)
    K.MIX = scr("MIX", [128, 8, NT], BF16)
    K.QT = scr("QT", [128, 4, NT], BF16)
    K.KT = scr("KT", [128, 4, NT], BF16)
    K.VT = scr("VT", [34, 128, 520], BF16)
    K.WGb = [[scr("WG%d_%d" % (l, m), [D, FH], BF16) for m in range(2)] for l in range(2)]
    K.R_XT = [Res() for _ in range(34)]
    K.R_MIXa = [Res() for _ in range(34)]
    K.R_MIXb = [Res() for _ in range(34)]
    K.R_QT = [Res() for _ in range(34)]
    K.R_KT = [Res() for _ in range(34)]
    K.R_VT = [Res() for _ in range(34)]
    K.R_WG = [[[Res() for _ in range(KC)] for _ in range(2)] for _ in range(2)]
    if LAYER1 is not None:
        LAYER1["scratch"](K, scr)
    with ExitStack() as es:
        S = Sched(nc, es)
        K.PB = [TT(es.enter_context(nc.psum_tensor("pb%d" % i, [128, 1024], F32))) for i in range(4)]
        K.RB = [Res() for _ in range(8)]
        phase_consts(K, S, es)
        phase_mod(K, S)
        phaseA0(K, S)
        phaseB0(K, S)
        phaseC(K, S, 0)
        if LAYER1 is not None:
            LAYER1["run"](K, S)
            phaseC(K, S, 1)
    return nc


def prep_inputs(inputs):
    f = lambda a: np.ascontiguousarray(np.asarray(a, dtype=np.float32))
    I = {k: np.asarray(v) for k, v in inputs.items()}
    shared = {}
    shared["ada_w"] = f(I["ada_w"])
    shared["ada_bT"] = f(I["ada_b"].reshape(2, 48, 128).transpose(0, 2, 1))
    shared["nmw"] = f(I["norm_mix_w"].reshape(2, 8, 128).transpose(0, 2, 1))
    shared["nfw"] = f(I["norm_ffn_w"].reshape(2, 8, 128).transpose(0, 2, 1))
    shared["fnw_bc"] = f(np.broadcast_to(I["final_norm_w"][None, :], (128, D)))
    shared["ffn_wg"] = f(I["ffn_w_gate"]); shared["ffn_wu"] = f(I["ffn_w_up"]); shared["ffn_wd"] = f(I["ffn_w_down"])
    shared["ab_w_in"] = f(I["ab_w_in"][0]); shared["ab_w_out"] = f(I["ab_w_out"][0])
    shared["sgu_nw_bc"] = f(np.broadcast_to(I["ab_sgu_norm_w"][0][None, :], (128, 512)))
    shared["sguWT"] = f(I["ab_sgu_w"][0].transpose(2, 0, 1))
    shared["sgub"] = f(np.broadcast_to(I["ab_sgu_b"][0].reshape(1, 512), (128, 512)))
    shared["nab"] = _na_bias_tiles(f(I["ab_rel_bias"][0]))
    shared["cd_w_out"] = f(I["cd_w_out"][0])
    shared.update(_consts())
    if LAYER1 is not None:
        LAYER1["prep"](I, shared)
    in_maps = []
    for b in range(8):
        m = dict(shared)
        m["x"] = f(I["x"][b]); m["ctx"] = f(I["ctx"][b])
        cs = np.stack([I["c"][b], I["c_ctx"]], axis=-1)
        m["csT"] = f(cs.reshape(8, 128, 2).transpose(1, 0, 2))
        in_maps.append(m)
    return in_maps


_NC_CACHE = {}


def kernel(**inputs):
    in_maps = prep_inputs(inputs)
    if "nc" not in _NC_CACHE:
        _NC_CACHE["nc"] = build_program()
    res = run_bass_kernel_spmd(_NC_CACHE["nc"], in_maps, core_ids=list(range(8)))
    return np.stack([np.asarray(r["out"], dtype=np.float32) for r in res.results], axis=0)
```

```python
import numpy as np
import concourse.bass as bass
import concourse.mybir as mybir
from concourse.bass_utils import run_bass_kernel_spmd
from contextlib import ExitStack

F32 = mybir.dt.float32
BF16 = mybir.dt.bfloat16
AF = mybir.ActivationFunctionType
ALU = mybir.AluOpType
AX = mybir.AxisListType

NX, NCX, NT, D, KC, FH, NJ = 4096, 256, 4352, 1024, 8, 2816, 22
EPS = 1e-6
DEBUG = False


class Res:
    __slots__ = ("w", "r")

    def __init__(self):
        self.w = {}
        self.r = {}


class Sched:
    NLANES = 12

    def __init__(self, nc, es):
        self.nc = nc
        self.keys = ("pe", "act", "dve", "pool", "sp")
        self.streams = {k: [] for k in self.keys}
        self.sems, self.count = {}, {}
        self.known = {k: {} for k in self.keys}
        for k in ("pe", "act", "dve", "pool"):
            self.sems[k] = es.enter_context(nc.semaphore("s_" + k))
            self.count[k] = 0
        self.lanes = {}
        for q in ("sp", "pool"):
            self.lanes[q] = []
            for i in range(self.NLANES):
                key = "L%s%d" % (q, i)
                self.sems[key] = es.enter_context(nc.semaphore(key))
                self.count[key] = 0
                self.lanes[q].append(key)
        self.lane_rr = {q: 0 for q in self.lanes}

    def _need(self, issuer, ckey, seq, waits):
        if not seq:
            return
        if ckey == "pe" and issuer == "pe":
            return
        if self.known[issuer].get(ckey, 0) >= seq:
            return
        self.known[issuer][ckey] = seq
        waits[ckey] = max(waits.get(ckey, 0), seq)

    def _deps(self, issuer, reads, writes):
        waits = {}
        for r in reads:
            for ck, sq in r.w.items():
                self._need(issuer, ck, sq, waits)
        for r in writes:
            for ck, sq in r.w.items():
                self._need(issuer, ck, sq, waits)
            for ck, sq in r.r.items():
                self._need(issuer, ck, sq, waits)
        return waits

    def _mark(self, ckey, seq, reads, writes):
        for r in reads:
            if r.r.get(ckey, 0) < seq:
                r.r[ckey] = seq
        for r in writes:
            r.w = {ckey: seq}
            r.r = {}

    def I(self, eng, meth, reads, writes, *a, **kw):
        waits = self._deps(eng, reads, writes)
        self.count[eng] += 1
        seq = self.count[eng]
        self._mark(eng, seq, reads, writes)
        self.streams[eng].append((lambda e: getattr(e, meth)(*a, **kw), waits, eng, 1))

    def D(self, q, out, in_, reads, writes, **kw):
        lanes = self.lanes[q]
        lane = lanes[self.lane_rr[q] % len(lanes)]
        self.lane_rr[q] += 1
        waits = self._deps(q, reads, writes)
        self._need(q, lane, self.count[lane], waits)
        self.count[lane] += 1
        seq = self.count[lane]
        self._mark(lane, seq, reads, writes)
        self.streams[q].append((lambda e: e.dma_start(out=out, in_=in_, **kw), waits, lane, 16))

    def barrier(self):
        for e in self.keys:
            waits = {}
            for ck, c in self.count.items():
                self._need(e, ck, c, waits)
            if waits:
                self.streams[e].append((None, waits, None, 0))

    def _mult(self, ck):
        return 16 if ck.startswith("L") else 1

    def flush(self):
        self.barrier()
        nc = self.nc
        streams = self.streams
        self.streams = {k: [] for k in self.keys}

        def mk(key):
            def body(e):
                for fn, waits, ckey, inc in streams[key]:
                    for wk, sq in waits.items():
                        e.wait_ge(self.sems[wk], sq * self._mult(wk))
                    if fn is not None:
                        fn(e).then_inc(self.sems[ckey], inc)
            return body
        with nc.Block() as block:
            block.tensor(mk("pe"))
            block.scalar(mk("act"))
            block.vector(mk("dve"))
            block.gpsimd(mk("pool"))
            block.sync(mk("sp"))


class TT:
    def __init__(self, t):
        self.t = t
        self.r = Res()


class Rot:
    def __init__(self, items):
        self.items = items
        self.i = 0

    def next(self):
        x = self.items[self.i % len(self.items)]
        self.i += 1
        return x


class KB:
    pass


_uid = [0]


def sbt(K, ph, shape, dt):
    _uid[0] += 1
    return TT(ph.enter_context(K.nc.sbuf_tensor("t%d" % _uid[0], list(shape), dt)))


def sbrot(K, ph, n, shape, dt):
    return Rot([sbt(K, ph, shape, dt) for _ in range(n)])


def bank(K):
    pool = getattr(K, "bank_pool", None) or list(range(8))
    i = pool[K.bank_rr % len(pool)]
    K.bank_rr += 1
    return K.PB[i // 2].t[:, (i % 2) * 512:(i % 2) * 512 + 512], K.RB[i]


def bank_fixed(K, i):
    return K.PB[i // 2].t[:, (i % 2) * 512:(i % 2) * 512 + 512], K.RB[i]


def bank2(K):
    if K.bank_rr % 2:
        K.bank_rr += 1
    i = (K.bank_rr % 8) // 2
    K.bank_rr += 2
    return K.PB[i].t[:, :], [K.RB[2 * i], K.RB[2 * i + 1]]


def norm_mod(K, S, xt, n, A, B, j, hT, tmps, sq, rstd):
    S.I("act", "activation", [xt.r], [sq.r], out=sq.t[:, :, :n], in_=xt.t[:, :, :n], func=AF.Square)
    ms, rms = bank(K)
    for kc in range(KC):
        S.I("pe", "matmul", [K.ones_bf.r, sq.r], [rms], ms[:, :n], lhsT=K.ones_bf.t[:, :], rhs=sq.t[:, kc, :n],
            start=(kc == 0), stop=(kc == KC - 1))
    S.I("act", "activation", [rms], [rstd.r], out=rstd.t[:, :n], in_=ms[:, :n], func=AF.Ln, scale=1.0 / D, bias=K.epsc.t[:, 0:1])
    S.I("act", "activation", [rstd.r], [rstd.r], out=rstd.t[:, :n], in_=rstd.t[:, :n], func=AF.Exp, scale=-0.5)
    for kc in range(KC):
        tm = tmps.next()
        S.I("dve", "tensor_tensor", [xt.r, rstd.r], [tm.r], out=tm.t[:, :n], in0=xt.t[:, kc, :n], in1=rstd.t[:, :n], op=ALU.mult)
        S.I("act", "activation", [tm.r, K.modr], [hT.r], out=hT.t[:, kc, :n], in_=tm.t[:, :n], func=AF.Identity,
            scale=A.t[:, kc, j:j + 1], bias=B.t[:, kc, j:j + 1])


def small_rstd(K, S, ss, rs, dim):
    S.I("act", "activation", [ss.r], [rs.r], out=rs.t[:, 0:1], in_=ss.t[:, 0:1], func=AF.Ln, scale=1.0 / dim, bias=K.epsc.t[:, 0:1])
    S.I("act", "activation", [rs.r], [rs.r], out=rs.t[:, 0:1], in_=rs.t[:, 0:1], func=AF.Exp, scale=-0.5)


def blkres(rl, t0, n):
    return rl[t0 // 128:(t0 + n) // 128]


def phase_consts(K, S, ph):
    nc = K.nc
    K.ident = sbt(K, ph, [128, 128], F32)
    S.D("sp", K.ident.t[:, :], K.inp["ident"], [], [K.ident.r])
    K.ident_bf = sbt(K, ph, [128, 128], BF16)
    S.D("pool", K.ident_bf.t[:, :], K.inp["ident"], [], [K.ident_bf.r])
    K.ones_bf = sbt(K, ph, [128, 128], BF16)
    S.I("pool", "memset", [], [K.ones_bf.r], K.ones_bf.t[:, :], 1.0)
    K.ones_f = sbt(K, ph, [1, 128], F32)
    S.I("pool", "memset", [], [K.ones_f.r], K.ones_f.t[:, :], 1.0)
    K.epsc = sbt(K, ph, [128, 1], F32)
    S.I("pool", "memset", [], [K.epsc.r], K.epsc.t[:, :], EPS)
    K.modr = Res()
    K.mods = []
    for l in range(2):
        K.mods.append({k: sbt(K, ph, [128, 8, 2], F32) for k in ("Am", "Bm", "Gm", "Af", "Bf", "Gf")})
        for v in K.mods[l].values():
            v.r = K.modr


def phase_mod(K, S):
    with ExitStack() as ph:
        csil = sbt(K, ph, [128, 8, 2], F32)
        S.D("sp", csil.t[:, :, :], K.inp["csT"], [], [csil.r])
        S.I("act", "activation", [csil.r], [csil.r], out=csil.t[:, :, :], in_=csil.t[:, :, :], func=AF.Silu)
        awts = sbrot(K, ph, 3, [128, 8, 512], F32)
        modrow = sbt(K, ph, [2, 6144], F32)
        mod = sbt(K, ph, [128, 48, 2], F32)
        abT = sbt(K, ph, [128, 48], F32)
        nw = sbt(K, ph, [128, 8], F32)
        fins = sbrot(K, ph, 3, [128, FH], F32)
        fouts = sbrot(K, ph, 3, [128, FH], BF16)
        engs = Rot(["pool", "dve", "pool", "act"])
        wcl = [(l, m, nm, kc) for l in range(2) for m, nm in ((0, "ffn_wg"), (1, "ffn_wu")) for kc in range(KC)]
        wst = {}

        def w_load(i):
            l, m, nm, kc = wcl[i]
            a = fins.next()
            S.D("sp", a.t[:, :], K.inp[nm][l][kc * 128:(kc + 1) * 128, :], [], [a.r])
            wst[i] = a

        def w_cast_store(i):
            l, m, nm, kc = wcl[i]
            a = wst.pop(i)
            b = fouts.next()
            e = engs.next()
            if e == "act":
                S.I("act", "copy", [a.r], [b.r], out=b.t[:, :], in_=a.t[:, :])
            else:
                S.I(e, "tensor_copy", [a.r], [b.r], out=b.t[:, :], in_=a.t[:, :])
            S.D("sp", K.WGb[l][m][kc * 128:(kc + 1) * 128, :], b.t[:, :], [b.r], [K.R_WG[l][m][kc]])

        def m_block(i):
            l, cb = divmod(i, 12)
            awv = K.inp["ada_w"][l].rearrange("(kc p) n -> p kc n", p=128)
            aw = awts.next()
            S.D("sp", aw.t[:, :, :], awv[:, :, cb * 512:(cb + 1) * 512], [], [aw.r])
            ps, rps = bank(K)
            for kc in range(KC):
                S.I("pe", "matmul", [aw.r, csil.r], [rps], ps[0:2, 0:512], lhsT=csil.t[:, kc, :], rhs=aw.t[:, kc, :], start=(kc == 0), stop=(kc == KC - 1))
            if cb % 2:
                S.I("act", "copy", [rps], [modrow.r], out=modrow.t[0:2, cb * 512:(cb + 1) * 512], in_=ps[0:2, 0:512])
            else:
                S.I("dve", "tensor_copy", [rps], [modrow.r], out=modrow.t[0:2, cb * 512:(cb + 1) * 512], in_=ps[0:2, 0:512])
            if cb == 11:
                m_final(l)

        def m_final(l):
            mp, rmp = bank(K)
            for j in range(48):
                S.I("pe", "transpose", [modrow.r, K.ident.r], [rmp], out=mp[:, 2 * j:2 * j + 2], in_=modrow.t[0:2, j * 128:(j + 1) * 128], identity=K.ident.t[0:2, 0:2])
            S.D("sp", abT.t[:, :], K.inp["ada_bT"][l], [], [abT.r])
            mpv = mp[:, 0:96].rearrange("p (a b) -> p a b", b=2)
            for j in range(2):
                S.I("dve", "tensor_tensor", [rmp, abT.r], [mod.r], out=mod.t[:, :, j], in0=mpv[:, :, j], in1=abT.t[:, :], op=ALU.add)
            M = K.mods[l]
            for (nm, a, b, g, off) in (("nmw", "Am", "Bm", "Gm", 0), ("nfw", "Af", "Bf", "Gf", 24)):
                S.D("sp", nw.t[:, :], K.inp[nm][l], [], [nw.r])
                for j in range(2):
                    S.I("dve", "scalar_tensor_tensor", [mod.r, nw.r], [K.modr], out=M[a].t[:, :, j], in0=mod.t[:, off + 8:off + 16, j],
                        scalar=1.0, in1=nw.t[:, :], op0=ALU.add, op1=ALU.mult)
                    S.I("dve", "tensor_copy", [mod.r], [K.modr], out=M[b].t[:, :, j], in_=mod.t[:, off:off + 8, j])
                    S.I("dve", "tensor_copy", [mod.r], [K.modr], out=M[g].t[:, :, j], in_=mod.t[:, off + 16:off + 24, j])
        NW = len(wcl)
        w_load(0)
        for step in range(NW):
            if step + 1 < NW:
                w_load(step + 1)
            if step < 24:
                m_block(step)
            w_cast_store(step)
        S.flush()


def phase_wcast(K, S):
    with ExitStack() as ph:
        fin = sbrot(K, ph, 2, [128, FH], F32)
        fout = sbrot(K, ph, 2, [128, FH], BF16)
        engs = Rot(["pool", "dve", "act"])
        for l in range(2):
            for m, nm in ((0, "ffn_wg"), (1, "ffn_wu")):
                for kc in range(KC):
                    a, b = fin.next(), fout.next()
                    S.D("sp", a.t[:, :], K.inp[nm][l][kc * 128:(kc + 1) * 128, :], [], [a.r])
                    e = engs.next()
                    if e == "act":
                        S.I("act", "copy", [a.r], [b.r], out=b.t[:, :], in_=a.t[:, :])
                    else:
                        S.I(e, "tensor_copy", [a.r], [b.r], out=b.t[:, :], in_=a.t[:, :])
                    S.D("sp", K.WGb[l][m][kc * 128:(kc + 1) * 128, :], b.t[:, :], [b.r], [K.R_WG[l][m][kc]])
        S.flush()


def phase0(K, S):
    with ExitStack() as ph:
        xin = sbrot(K, ph, 2, [128, 1024], F32)
        xo = sbrot(K, ph, 2, [128, 8, 128], F32)
        for blk in range(34):
            src = K.inp["x"][blk * 128:(blk + 1) * 128, :] if blk < 32 else K.inp["ctx"][(blk - 32) * 128:(blk - 31) * 128, :]
            a = xin.next()
            S.D("sp", a.t[:, :], src, [], [a.r])
            pt, rpt = bank2(K)
            for kc in range(KC):
                S.I("pe", "transpose", [a.r, K.ident.r], rpt, out=pt[:, kc * 128:(kc + 1) * 128], in_=a.t[:, kc * 128:(kc + 1) * 128],
                    identity=K.ident.t[:, :])
            o = xo.next()
            ov = o.t[:, :, :].rearrange("p a b -> p (a b)")
            if blk % 2:
                S.I("act", "copy", rpt, [o.r], out=ov, in_=pt)
            else:
                S.I("dve", "tensor_copy", rpt, [o.r], out=ov, in_=pt)
            S.D("sp", K.XT[:, :, blk * 128:(blk + 1) * 128], o.t[:, :, :], [o.r], [K.R_XT[blk]])
        S.flush()


def phaseA0(K, S):
    M = K.mods[0]
    with ExitStack() as ph:
        win = sbt(K, ph, [128, 8, 2560], BF16)
        wv = K.inp["ab_w_in"].rearrange("(kc p) n -> p kc n", p=128)
        for kc in range(KC):
            S.D("pool", win.t[:, kc, :], wv[:, kc, :], [], [win.r])
        sguw = sbt(K, ph, [128, 4, 128], BF16)
        S.D("pool", sguw.t[:, :, :], K.inp["sguWT"], [], [sguw.r])
        nwbc = sbt(K, ph, [128, 512], F32)
        S.D("sp", nwbc.t[:, :], K.inp["sgu_nw_bc"], [], [nwbc.r])
        sgub = sbt(K, ph, [128, 512], F32)
        S.D("sp", sgub.t[:, :], K.inp["sgub"], [], [sgub.r])
        gts = sbrot(K, ph, 2, [128, 512], F32)
        xts = sbrot(K, ph, 2, [128, 8, 512], F32)
        xins = sbrot(K, ph, 8, [128, 1024], F32)
        prex = {}

        def xload(t):
            n = 512 if t < 8 else 256
            lst = []
            for sbk in range(n // 128):
                blk = t * 4 + sbk
                src = K.inp["x"][blk * 128:(blk + 1) * 128, :] if blk < 32 else K.inp["ctx"][(blk - 32) * 128:(blk - 31) * 128, :]
                a = xins.next()
                S.D("sp", a.t[:, :], src, [], [a.r])
                lst.append(a)
            prex[t] = lst
        sqs = sbrot(K, ph, 2, [128, 8, 512], BF16)
        rstds = sbrot(K, ph, 2, [128, 512], F32)
        tmps = sbrot(K, ph, 2, [128, 512], F32)
        hTs = sbrot(K, ph, 2, [128, 8, 512], BF16)
        uT = sbt(K, ph, [128, 4, 512], BF16)
        qos = sbrot(K, ph, 2, [128, 4, 512], BF16)
        kos = sbrot(K, ph, 2, [128, 4, 512], BF16)
        aos = sbrot(K, ph, 2, [128, 4, 512], BF16)
        vos = sbrot(K, ph, 2, [128, 8, 65], BF16)
        for v in vos.items:
            S.I("pool", "memset", [], [v.r], v.t[:, :, :], 1.0)
        gvs = sbrot(K, ph, 2, [128, 512], F32)
        junk = sbt(K, ph, [128, 512], BF16)
        vns = sbrot(K, ph, 2, [128, 512], BF16)
        sss = sbrot(K, ph, 2, [128, 1], F32)
        rss = sbrot(K, ph, 2, [128, 1], F32)
        def tile(t):
            n = 512 if t < 8 else 256
            t0 = t * 512
            j = 0 if t < 8 else 1
            xt = xts.next()
            hT, sq, rstd = hTs.next(), sqs.next(), rstds.next()
            for sbk in range(n // 128):
                blk = t * 4 + sbk
                a = prex[t][sbk]
                pt, rpt = bank2(K)
                for kc in range(KC):
                    S.I("pe", "transpose", [a.r, K.ident.r], rpt, out=pt[:, kc * 128:(kc + 1) * 128], in_=a.t[:, kc * 128:(kc + 1) * 128],
                        identity=K.ident.t[:, :])
                if sbk % 2:
                    S.I("act", "copy", rpt, [xt.r], out=xt.t[:, :, sbk * 128:(sbk + 1) * 128], in_=pt.rearrange("p (k t) -> p k t", t=128))
                else:
                    S.I("dve", "tensor_copy", rpt, [xt.r], out=xt.t[:, :, sbk * 128:(sbk + 1) * 128], in_=pt.rearrange("p (k t) -> p k t", t=128))
            S.D("sp", K.XT[:, :, t0:t0 + n], xt.t[:, :, :n], [xt.r], blkres(K.R_XT, t0, n))
            norm_mod(K, S, xt, n, M["Am"], M["Bm"], j, hT, tmps, sq, rstd)
            yield
            if t + 1 < 9:
                xload(t + 1)
            qo, ko, ao = qos.next(), kos.next(), aos.next()
            for ci in list(range(4)) + list(range(8, 16)):
                ps, rps = bank(K)
                for kc in range(KC):
                    S.I("pe", "matmul", [win.r, hT.r], [rps], ps[:, :n], lhsT=win.t[:, kc, ci * 128:(ci + 1) * 128], rhs=hT.t[:, kc, :n],
                        start=(kc == 0), stop=(kc == KC - 1))
                if ci < 4:
                    S.I("act", "activation", [rps], [uT.r], out=uT.t[:, ci, :n], in_=ps[:, :n], func=AF.Gelu_apprx_tanh)
                elif ci < 12:
                    S.I("dve", "tensor_scalar", [rps], [qo.r], out=qo.t[:, ci - 8, :n], in0=ps[:, :n], scalar1=0.125, scalar2=None, op0=ALU.mult)
                else:
                    S.I("act", "copy", [rps], [ko.r], out=ko.t[:, ci - 12, :n], in_=ps[:, :n])
            S.D("sp", K.QT[:, :, t0:t0 + n], qo.t[:, :, :n], [qo.r], blkres(K.R_QT, t0, n))
            S.D("sp", K.KT[:, :, t0:t0 + n], ko.t[:, :, :n], [ko.r], blkres(K.R_KT, t0, n))
            yield
            pend = []
            for sbk in range(n // 128):
                blk = t * 4 + sbk
                tk = slice(sbk * 128, (sbk + 1) * 128)
                pv, rpv = bank(K)
                pa, rpa = bank(K)
                for kc in range(KC):
                    S.I("pe", "matmul", [win.r, hT.r], [rpv], pv, lhsT=hT.t[:, kc, tk], rhs=win.t[:, kc, 512:1024], start=(kc == 0), stop=(kc == KC - 1))
                for kc in range(KC):
                    S.I("pe", "matmul", [win.r, hT.r], [rpa], pa, lhsT=hT.t[:, kc, tk], rhs=win.t[:, kc, 2048:2560], start=(kc == 0), stop=(kc == KC - 1))
                while pend:
                    pend.pop(0)()
                vo = vos.next()
                S.I("act", "copy", [rpa], [vo.r], out=vo.t[:, :, 0:64], in_=pa.rearrange("p (h d) -> p h d", d=64))
                S.D("sp", K.VT[blk].rearrange("p (h e) -> p h e", e=65), vo.t[:, :, :], [vo.r], [K.R_VT[blk]])
                gv, ss, rs, vn = gvs.next(), sss.next(), rss.next(), vns.next()
                S.I("act", "activation", [rpv], [gv.r], out=gv.t[:, :], in_=pv, func=AF.Gelu_apprx_tanh)
                S.I("dve", "memset", [], [ss.r], ss.t[:, :], 0.0)
                S.I("act", "activation", [gv.r, ss.r], [junk.r, ss.r], out=junk.t[:, :], in_=gv.t[:, :], func=AF.Square, accum_out=ss.t[:, 0:1])
                small_rstd(K, S, ss, rs, 512)
                S.I("dve", "scalar_tensor_tensor", [gv.r, rs.r, nwbc.r], [vn.r], out=vn.t[:, :], in0=gv.t[:, :], scalar=rs.t[:, 0:1],
                    in1=nwbc.t[:, :], op0=ALU.mult, op1=ALU.mult)
                def mk(vn=vn, tk=tk):
                    def f():
                        pg, rpg = bank(K)
                        for g in range(4):
                            gs = slice(g * 128, (g + 1) * 128)
                            S.I("pe", "matmul", [vn.r, sguw.r], [rpg], pg[:, gs], lhsT=vn.t[:, gs], rhs=sguw.t[:, g, :], start=True, stop=True)
                        gt = gts.next()
                        S.I("dve", "tensor_tensor", [rpg, sgub.r], [gt.r], out=gt.t[:, :], in0=pg, in1=sgub.t[:, :], op=ALU.add)
                        S.I("dve", "tensor_tensor", [gt.r, uT.r], [ao.r], out=ao.t[:, :, tk], in0=gt.t[:, :].rearrange("p (g t) -> p g t", t=128),
                            in1=uT.t[:, :, tk], op=ALU.mult)
                    return f
                pend.append(mk())
            while pend:
                pend.pop(0)()
            S.D("sp", K.MIX[:, 0:4, t0:t0 + n], ao.t[:, :, :n], [ao.r], blkres(K.R_MIXa, t0, n))
        xload(0)
        gens = [tile(t) for t in range(9)]
        next(gens[0])
        for t in range(9):
            next(gens[t])
            if t + 1 < 9:
                next(gens[t + 1])
            for _ in gens[t]:
                pass
        S.flush()


def na_blocks():
    out = []
    for i in range(32):
        if i < 2:
            out.append((list(range(4)), 5 + 4 * i))
        elif i >= 30:
            out.append((list(range(28, 32)), 5 + 4 * (i - 28)))
        else:
            out.append((list(range(i - 2, i + 3)), 0))
    out.append(([], 0))
    out.append(([], 0))
    return out


def phaseB0(K, S):
    with ExitStack() as ph:
        vall = sbt(K, ph, [128, 34, 520], BF16)
        S.D("sp", vall.t[:, :, :], K.VT.rearrange("b p e -> p b e"), K.R_VT, [vall.r])
        vv = vall.t[:, :, :].rearrange("p b (h e) -> p b h e", e=65)
        qjs = sbrot(K, ph, 2, [128, NT], BF16)
        kjs = sbrot(K, ph, 2, [128, NT], BF16)
        nbs = sbrot(K, ph, 2, [128, 21, 128], F32)
        otok = sbt(K, ph, [128, 34, 512], BF16)
        tmps = sbrot(K, ph, 3, [128, 640], F32)
        pTs = sbrot(K, ph, 4, [128, 896], BF16)
        rcs = sbrot(K, ph, 3, [128, 1], F32)
        blocks = na_blocks()
        spairs = Rot([0, 1, 2])
        pobanks = Rot([6, 7])
        its = []
        for jp in range(4):
            for hh in range(2):
                for i in range(34):
                    its.append((jp, hh, i))
        N = len(its)
        ctxs = [None] * N
        cur = {}

        def stA(k):
            jp, hh, i = its[k]
            if hh == 0 and i == 0:
                cur["qj"], cur["kj"] = qjs.next(), kjs.next()
                S.D("sp", cur["qj"].t[:, :], K.QT[:, jp, :], K.R_QT, [cur["qj"].r])
                S.D("sp", cur["kj"].t[:, :], K.KT[:, jp, :], K.R_KT, [cur["kj"].r])
            if i == 0:
                cur["nb"] = nbs.next()
                S.D("sp", cur["nb"].t[:, :, :], K.inp["nab"][2 * jp + hh], [], [cur["nb"].r])
            qj, kj, nb = cur["qj"], cur["kj"], cur["nb"]
            hb = hh * 64
            chunks, t0 = blocks[i]
            allc = chunks + [32, 33]
            pi = spairs.next()
            Sp = K.PB[pi].t[:, :]
            rS = [K.RB[2 * pi], K.RB[2 * pi + 1]]
            for ci, m in enumerate(allc):
                S.I("pe", "matmul", [qj.r, kj.r], rS, Sp[:, ci * 128:(ci + 1) * 128], lhsT=kj.t[hb:hb + 64, m * 128:(m + 1) * 128],
                    rhs=qj.t[hb:hb + 64, i * 128:(i + 1) * 128], start=True, stop=True)
            ctxs[k] = dict(Sp=Sp, rS=rS, nb=nb, nnb=len(chunks), t0=t0, allc=allc, h=2 * jp + hh, i=i)

        def stB(k):
            c = ctxs[k]
            nnb, Sp, rS = c["nnb"], c["Sp"], c["rS"]
            pT = pTs.next()
            c["pT"] = pT
            if nnb:
                tm = tmps.next()
                S.I("dve", "tensor_tensor", rS + [c["nb"].r], [tm.r], out=tm.t[:, :nnb * 128].rearrange("p (a b) -> p a b", b=128),
                    in0=Sp[:, :nnb * 128].rearrange("p (a b) -> p a b", b=128), in1=c["nb"].t[:, c["t0"]:c["t0"] + nnb, :], op=ALU.add)
                S.I("act", "activation", [tm.r], [pT.r], out=pT.t[:, :nnb * 128], in_=tm.t[:, :nnb * 128], func=AF.Exp)
            S.I("act", "activation", rS, [pT.r], out=pT.t[:, nnb * 128:(nnb + 2) * 128], in_=Sp[:, nnb * 128:(nnb + 2) * 128], func=AF.Exp)

        def stC(k):
            c = ctxs[k]
            po, rpo = bank_fixed(K, pobanks.next())
            c["po"], c["rpo"] = po, rpo
            pT, allc = c["pT"], c["allc"]
            for ci, m in enumerate(allc):
                S.I("pe", "matmul", [pT.r, vall.r], [rpo], po[:, 0:65], lhsT=pT.t[:, ci * 128:(ci + 1) * 128], rhs=vv[:, m, c["h"], :],
                    start=(ci == 0), stop=(ci == len(allc) - 1))

        def stD(k):
            c = ctxs[k]
            po, rpo, h, i = c["po"], c["rpo"], c["h"], c["i"]
            rc = rcs.next()
            S.I("dve", "reciprocal", [rpo], [rc.r], out=rc.t[:, 0:1], in_=po[:, 64:65])
            S.I("dve", "tensor_scalar", [rpo, rc.r], [otok.r], out=otok.t[:, i, h * 64:(h + 1) * 64], in0=po[:, 0:64], scalar1=rc.t[:, 0:1],
                scalar2=None, op0=ALU.mult)
            ctxs[k] = None
        for step in range(N + 3):
            if step < N:
                stA(step)
            if 0 <= step - 3 < N:
                stD(step - 3)
            if 0 <= step - 1 < N:
                stB(step - 1)
            if 0 <= step - 2 < N:
                stC(step - 2)
        K.bank_rr = 0
        bos = sbrot(K, ph, 2, [128, 4, 128], BF16)
        for i in range(34):
            pt, rpt = bank(K)
            for fc in range(4):
                S.I("pe", "matmul", [otok.r, K.ident_bf.r], [rpt], pt[:, fc * 128:(fc + 1) * 128], lhsT=otok.t[:, i, fc * 128:(fc + 1) * 128],
                    rhs=K.ident_bf.t[:, :], start=True, stop=True)
            bo = bos.next()
            bv = bo.t[:, :, :].rearrange("p a b -> p (a b)")
            if i % 2:
                S.I("act", "copy", [rpt], [bo.r], out=bv, in_=pt)
            else:
                S.I("dve", "tensor_copy", [rpt], [bo.r], out=bv, in_=pt)
            S.D("sp", K.MIX[:, 4:8, i * 128:(i + 1) * 128], bo.t[:, :, :], [bo.r], [K.R_MIXb[i]])
        S.flush()


def phaseC(K, S, l):
    M = K.mods[l]
    ntiles = 9 if l == 0 else 8
    last = (l == 1)
    with ExitStack() as ph:
        wout = sbt(K, ph, [128, 8, 1024], BF16)
        wv = K.inp["ab_w_out" if l == 0 else "cd_w_out"].rearrange("(kc p) n -> p kc n", p=128)
        for kc in range(KC):
            S.D("pool", wout.t[:, kc, :], wv[:, kc, :], [], [wout.r])
        wdr = sbt(K, ph, [128, NJ, 1024], BF16)
        wdv = K.inp["ffn_wd"][l].rearrange("(j p) n -> p j n", p=128)
        wdres = [Res() for _ in range(NJ)]
        for jj in range(NJ):
            S.D("pool", wdr.t[:, jj, :], wdv[:, jj, :], [], [wdres[jj]])
        xts = sbrot(K, ph, 2, [128, 8, 512], F32)
        mxs = sbrot(K, ph, 2, [128, 8, 512], BF16)
        sq = sbt(K, ph, [128, 8, 512], BF16)
        rstd = sbt(K, ph, [128, 512], F32)
        tmps = sbrot(K, ph, 2, [128, 512], F32)
        h2 = sbt(K, ph, [128, 8, 512], BF16)
        aT = sbt(K, ph, [128, NJ, 512], BF16)
        wgs = sbrot(K, ph, 2, [128, 8, 256], BF16)
        wus = sbrot(K, ph, 2, [128, 8, 256], BF16)
        sgs = sbrot(K, ph, 2, [128, 512], F32)
        if last:
            fnw = sbt(K, ph, [128, 1024], F32)
            S.D("sp", fnw.t[:, :], K.inp["fnw_bc"], [], [fnw.r])
            ots = sbrot(K, ph, 2, [128, 1024], F32)
            junk = sbt(K, ph, [128, 1024], BF16)
            sss = sbrot(K, ph, 2, [128, 1], F32)
            rss = sbrot(K, ph, 2, [128, 1], F32)
        wgv = [K.WGb[l][m].rearrange("(kc p) n -> p kc n", p=128) for m in range(2)]
        def cload(t):
            n = 512 if t < 8 else 256
            t0 = t * 512
            xt, mx = xts.next(), mxs.next()
            S.D("sp", xt.t[:, :, :n], K.XT[:, :, t0:t0 + n], blkres(K.R_XT, t0, n), [xt.r])
            S.D("sp", mx.t[:, :, :n], K.MIX[:, :, t0:t0 + n], blkres(K.R_MIXa, t0, n) + blkres(K.R_MIXb, t0, n), [mx.r])
            return xt, mx
        nxt = cload(0)
        for t in range(ntiles):
            n = 512 if t < 8 else 256
            t0 = t * 512
            j = 0 if t < 8 else 1
            xt, mx = nxt
            for mo in range(8):
                ps, rps = bank(K)
                for kc in range(KC):
                    S.I("pe", "matmul", [wout.r, mx.r], [rps], ps[:, :n], lhsT=wout.t[:, kc, mo * 128:(mo + 1) * 128], rhs=mx.t[:, kc, :n],
                        start=(kc == 0), stop=(kc == KC - 1))
                S.I("dve", "scalar_tensor_tensor", [rps, xt.r, K.modr], [xt.r], out=xt.t[:, mo, :n], in0=ps[:, :n], scalar=M["Gm"].t[:, mo, j:j + 1],
                    in1=xt.t[:, mo, :n], op0=ALU.mult, op1=ALU.add)
            norm_mod(K, S, xt, n, M["Af"], M["Bf"], j, h2, tmps, sq, rstd)
            if t + 1 < ntiles:
                nxt = cload(t + 1)
            for pc in range(11):
                wg, wu = wgs.next(), wus.next()
                S.D("sp", wg.t[:, :, :], wgv[0][:, :, pc * 256:(pc + 1) * 256], K.R_WG[l][0], [wg.r])
                S.D("sp", wu.t[:, :, :], wgv[1][:, :, pc * 256:(pc + 1) * 256], K.R_WG[l][1], [wu.r])
                for q in range(2):
                    jj = pc * 2 + q
                    pg, rpg = bank(K)
                    pu, rpu = bank(K)
                    for kc in range(KC):
                        S.I("pe", "matmul", [wg.r, h2.r], [rpg], pg[:, :n], lhsT=wg.t[:, kc, q * 128:(q + 1) * 128], rhs=h2.t[:, kc, :n], start=(kc == 0), stop=(kc == KC - 1))
                    for kc in range(KC):
                        S.I("pe", "matmul", [wu.r, h2.r], [rpu], pu[:, :n], lhsT=wu.t[:, kc, q * 128:(q + 1) * 128], rhs=h2.t[:, kc, :n], start=(kc == 0), stop=(kc == KC - 1))
                    sg = sgs.next()
                    S.I("act", "activation", [rpg], [sg.r], out=sg.t[:, :n], in_=pg[:, :n], func=AF.Silu)
                    S.I("dve", "tensor_tensor", [sg.r, rpu], [aT.r], out=aT.t[:, jj, :n], in0=sg.t[:, :n], in1=pu[:, :n], op=ALU.mult)
            for mo in range(8):
                ps, rps = bank(K)
                for jj in range(NJ):
                    S.I("pe", "matmul", [wdres[jj], aT.r], [rps], ps[:, :n], lhsT=wdr.t[:, jj, mo * 128:(mo + 1) * 128], rhs=aT.t[:, jj, :n], start=(jj == 0), stop=(jj == NJ - 1))
                S.I("dve", "scalar_tensor_tensor", [rps, xt.r, K.modr], [xt.r], out=xt.t[:, mo, :n], in0=ps[:, :n], scalar=M["Gf"].t[:, mo, j:j + 1],
                    in1=xt.t[:, mo, :n], op0=ALU.mult, op1=ALU.add)
            if not last:
                S.D("sp", K.XT[:, :, t0:t0 + n], xt.t[:, :, :n], [xt.r], blkres(K.R_XT, t0, n))
            else:
                for sbk in range(4):
                    blk = t * 4 + sbk
                    pt, rpt = bank2(K)
                    for kc in range(KC):
                        S.I("pe", "transpose", [xt.r, K.ident.r], rpt, out=pt[:, kc * 128:(kc + 1) * 128], in_=xt.t[:, kc, sbk * 128:(sbk + 1) * 128], identity=K.ident.t[:, :])
                    ss, rs, ot = sss.next(), rss.next(), ots.next()
                    S.I("dve", "memset", [], [ss.r], ss.t[:, :], 0.0)
                    S.I("act", "activation", rpt + [ss.r], [junk.r, ss.r], out=junk.t[:, :], in_=pt, func=AF.Square, accum_out=ss.t[:, 0:1])
                    small_rstd(K, S, ss, rs, 1024)
                    S.I("dve", "scalar_tensor_tensor", rpt + [rs.r, fnw.r], [ot.r], out=ot.t[:, :], in0=pt, scalar=rs.t[:, 0:1], in1=fnw.t[:, :],
                        op0=ALU.mult, op1=ALU.mult)
                    S.D("sp", K.out[blk * 128:(blk + 1) * 128, :], ot.t[:, :], [ot.r], [K.R_out])
        S.flush()


def phaseD(K, S):
    with ExitStack() as ph:
        fnw = sbt(K, ph, [128, 1024], F32)
        S.D("sp", fnw.t[:, :], K.inp["fnw_bc"], [], [fnw.r])
        xbs = sbrot(K, ph, 2, [128, 8, 128], F32)
        ots = sbrot(K, ph, 2, [128, 1024], F32)
        junk = sbt(K, ph, [128, 1024], BF16)
        sss = sbrot(K, ph, 2, [128, 1], F32)
        rss = sbrot(K, ph, 2, [128, 1], F32)
        for blk in range(32):
            xb = xbs.next()
            S.D("sp", xb.t[:, :, :], K.XT[:, :, blk * 128:(blk + 1) * 128], [K.R_XT[blk]], [xb.r])
            pt, rpt = bank2(K)
            for kc in range(KC):
                S.I("pe", "transpose", [xb.r, K.ident.r], rpt, out=pt[:, kc * 128:(kc + 1) * 128], in_=xb.t[:, kc, :], identity=K.ident.t[:, :])
            ss, rs, ot = sss.next(), rss.next(), ots.next()
            S.I("dve", "memset", [], [ss.r], ss.t[:, :], 0.0)
            S.I("act", "activation", rpt + [ss.r], [junk.r, ss.r], out=junk.t[:, :], in_=pt, func=AF.Square, accum_out=ss.t[:, 0:1])
            small_rstd(K, S, ss, rs, 1024)
            S.I("dve", "scalar_tensor_tensor", rpt + [rs.r, fnw.r], [ot.r], out=ot.t[:, :], in0=pt, scalar=rs.t[:, 0:1], in1=fnw.t[:, :],
                op0=ALU.mult, op1=ALU.mult)
            S.D("sp", K.out[blk * 128:(blk + 1) * 128, :], ot.t[:, :], [ot.r], [K.R_out])
        S.flush()


def l1_decl(K, din):
    din("cd_w_in", [D, 2080]); din("cd_w_perm", [D, 512]); din("decw", [2, 17, 256]); din("hnw", [128, 1])
    din("tri", [128, 4, 128]); din("gmask", [128, 2, 128])
    din("ropeC", [128, NX]); din("ropeS", [128, NX]); din("ropeCt", [32, 128, 256]); din("ropeSt", [32, 128, 256])
    din("w128ri", [128, 256]); din("wc1", [128, 256]); din("wc2", [128, 256]); din("tw", [128, 32, 2, 128])


def l1_scratch(K, scr):
    K.HT = scr("HT", [128, 8, NX], BF16)
    K.GQ = scr("GQ", [128, 2, NT], BF16)
    K.GK = scr("GK", [128, 2, NT], BF16)
    K.KTOK = scr("KTOK", [34, 128, 256], BF16)
    K.VTOK1 = scr("VTOK1", [34, 128, 512], BF16)
    K.SG = scr("SG", [128, 4, NX], BF16)
    K.ALR = scr("ALR", [2, 16, NT], F32)
    K.QE = [scr("QE%d" % d, [128, 2, NX], BF16) for d in range(2)]
    K.KE = [scr("KE%d" % d, [128, 2, NX], BF16) for d in range(2)]
    for nm in ("HT", "GQ", "GK", "KTOK", "VTOK1", "SG", "ALR", "QE0", "QE1", "KE0", "KE1"):
        setattr(K, "R_" + nm, [Res() for _ in range(34)])


def phaseA1(K, S):
    M = K.mods[1]
    with ExitStack() as ph:
        win = sbt(K, ph, [128, 8, 2080], BF16)
        wv = K.inp["cd_w_in"].rearrange("(kc p) n -> p kc n", p=128)
        for kc in range(KC):
            S.D("pool", win.t[:, kc, :], wv[:, kc, :], [], [win.r])
        wpm = sbt(K, ph, [128, 8, 512], BF16)
        S.D("pool", wpm.t[:, :, :], K.inp["cd_w_perm"].rearrange("(kc p) n -> p kc n", p=128), [], [wpm.r])
        xts = sbrot(K, ph, 2, [128, 8, 512], F32)
        sqs = sbrot(K, ph, 2, [128, 8, 512], BF16)
        rstds = sbrot(K, ph, 2, [128, 512], F32)
        tmps = sbrot(K, ph, 2, [128, 512], F32)
        hTs = sbrot(K, ph, 2, [128, 8, 512], BF16)
        rcs = sbrot(K, ph, 2, [128, 512], F32)
        rss = sbrot(K, ph, 2, [128, 512], F32)
        t1s = sbrot(K, ph, 2, [128, 512], F32)
        t2s = sbrot(K, ph, 2, [128, 512], F32)
        qos = sbrot(K, ph, 2, [128, 2, 512], BF16)
        kos = sbrot(K, ph, 2, [128, 2, 512], BF16)
        sgs = sbrot(K, ph, 2, [128, 4, 512], BF16)
        als = sbrot(K, ph, 2, [16, 2, 512], F32)
        vos = sbrot(K, ph, 2, [128, 512], BF16)
        cts = sbrot(K, ph, 2, [128, 256], F32)
        sts = sbrot(K, ph, 2, [128, 256], F32)
        u1s = sbrot(K, ph, 2, [128, 256], F32)
        u2s = sbrot(K, ph, 2, [128, 256], F32)
        ktos = sbrot(K, ph, 2, [128, 256], BF16)
        prex = {}

        def xload(t):
            n = 512 if t < 8 else 256
            t0 = t * 512
            xt = xts.next()
            S.D("sp", xt.t[:, :, :n], K.XT[:, :, t0:t0 + n], blkres(K.R_XT, t0, n), [xt.r])
            prex[t] = xt

        def tile(t):
            n = 512 if t < 8 else 256
            t0 = t * 512
            isx = t < 8
            j = 0 if isx else 1
            xt = prex[t]
            hT = hTs.next()
            sq, rstd = sqs.next(), rstds.next()
            norm_mod(K, S, xt, n, M["Am"], M["Bm"], j, hT, tmps, sq, rstd)
            yield
            if t + 1 < 9:
                xload(t + 1)
            if isx:
                S.D("sp", K.HT[:, :, t0:t0 + n], hT.t[:, :, :n], [hT.r], blkres(K.R_HT, t0, n))
                rc, rs = rcs.next(), rss.next()
                S.D("sp", rc.t[:, :], K.inp["ropeC"][:, t0:t0 + n], [], [rc.r])
                S.D("sp", rs.t[:, :], K.inp["ropeS"][:, t0:t0 + n], [], [rs.r])
            qo, ko = qos.next(), kos.next()
            for c in (range(4) if isx else (2, 3)):
                p1, rp1 = bank(K)
                for kc in range(KC):
                    S.I("pe", "matmul", [win.r, hT.r], [rp1], p1[:, :n], lhsT=win.t[:, kc, 512 + c * 128:512 + (c + 1) * 128], rhs=hT.t[:, kc, :n],
                        start=(kc == 0), stop=(kc == KC - 1))
                dst = qo if c < 2 else ko
                if not isx:
                    S.I("act", "copy", [rp1], [dst.r], out=dst.t[:, c % 2, :n], in_=p1[:, :n])
                    continue
                p2, rp2 = bank(K)
                for kc in range(KC):
                    S.I("pe", "matmul", [wpm.r, hT.r], [rp2], p2[:, :n], lhsT=wpm.t[:, kc, c * 128:(c + 1) * 128], rhs=hT.t[:, kc, :n],
                        start=(kc == 0), stop=(kc == KC - 1))
                sc = 0.125 if c < 2 else 1.0
                t1, t2 = t1s.next(), t2s.next()
                S.I("dve", "scalar_tensor_tensor", [rp1, rc.r], [t1.r], out=t1.t[:, :n], in0=p1[:, :n], scalar=sc, in1=rc.t[:, :n], op0=ALU.mult, op1=ALU.mult)
                S.I("dve", "scalar_tensor_tensor", [rp2, rs.r], [t2.r], out=t2.t[:, :n], in0=p2[:, :n], scalar=sc, in1=rs.t[:, :n], op0=ALU.mult, op1=ALU.mult)
                S.I("pool", "tensor_tensor", [t1.r, t2.r], [dst.r], out=dst.t[:, c % 2, :n], in0=t1.t[:, :n], in1=t2.t[:, :n], op=ALU.add)
            if isx:
                S.D("sp", K.GQ[:, :, t0:t0 + n], qo.t[:, :, :n], [qo.r], blkres(K.R_GQ, t0, n))
            S.D("sp", K.GK[:, :, t0:t0 + n], ko.t[:, :, :n], [ko.r], blkres(K.R_GK, t0, n))
            if isx:
                sg = sgs.next()
                for c in range(4):
                    p1, rp1 = bank(K)
                    for kc in range(KC):
                        S.I("pe", "matmul", [win.r, hT.r], [rp1], p1[:, :n], lhsT=win.t[:, kc, 1536 + c * 128:1536 + (c + 1) * 128], rhs=hT.t[:, kc, :n],
                            start=(kc == 0), stop=(kc == KC - 1))
                    S.I("act", "activation", [rp1], [sg.r], out=sg.t[:, c, :n], in_=p1[:, :n], func=AF.Silu)
                S.D("sp", K.SG[:, :, t0:t0 + n], sg.t[:, :, :n], [sg.r], blkres(K.R_SG, t0, n))
            al = als.next()
            for d in range(2):
                p1, rp1 = bank(K)
                for kc in range(KC):
                    S.I("pe", "matmul", [win.r, hT.r], [rp1], p1[0:16, :n], lhsT=win.t[:, kc, 2048 + d * 16:2064 + d * 16], rhs=hT.t[:, kc, :n],
                        start=(kc == 0), stop=(kc == KC - 1))
                S.I("dve", "tensor_copy", [rp1], [al.r], out=al.t[:, d, :n], in_=p1[0:16, :n])
            for d in range(2):
                S.D("sp", K.ALR[d, :, t0:t0 + n], al.t[:, d, :n], [al.r], blkres(K.R_ALR, t0, n))
            yield
            for sbk in range(n // 128):
                blk = t * 4 + sbk
                tk = slice(sbk * 128, (sbk + 1) * 128)
                pv, rpv = bank(K)
                for kc in range(KC):
                    S.I("pe", "matmul", [win.r, hT.r], [rpv], pv, lhsT=hT.t[:, kc, tk], rhs=win.t[:, kc, 1024:1536], start=(kc == 0), stop=(kc == KC - 1))
                vo = vos.next()
                S.I("act", "copy", [rpv], [vo.r], out=vo.t[:, :], in_=pv)
                S.D("sp", K.VTOK1[blk], vo.t[:, :], [vo.r], [K.R_VTOK1[blk]])
                pk, rpk = bank(K)
                for kc in range(KC):
                    S.I("pe", "matmul", [win.r, hT.r], [rpk], pk[:, 0:256], lhsT=hT.t[:, kc, tk], rhs=win.t[:, kc, 768:1024], start=(kc == 0), stop=(kc == KC - 1))
                kto = ktos.next()
                if isx:
                    for kc in range(KC):
                        S.I("pe", "matmul", [wpm.r, hT.r], [rpk], pk[:, 256:512], lhsT=hT.t[:, kc, tk], rhs=wpm.t[:, kc, 256:512], start=(kc == 0), stop=(kc == KC - 1))
                    ct, st, u1, u2 = cts.next(), sts.next(), u1s.next(), u2s.next()
                    S.D("sp", ct.t[:, :], K.inp["ropeCt"][blk], [], [ct.r])
                    S.D("sp", st.t[:, :], K.inp["ropeSt"][blk], [], [st.r])
                    S.I("dve", "tensor_tensor", [rpk, ct.r], [u1.r], out=u1.t[:, :], in0=pk[:, 0:256], in1=ct.t[:, :], op=ALU.mult)
                    S.I("dve", "tensor_tensor", [rpk, st.r], [u2.r], out=u2.t[:, :], in0=pk[:, 256:512], in1=st.t[:, :], op=ALU.mult)
                    S.I("pool", "tensor_tensor", [u1.r, u2.r], [kto.r], out=kto.t[:, :], in0=u1.t[:, :], in1=u2.t[:, :], op=ALU.add)
                else:
                    S.I("dve", "tensor_copy", [rpk], [kto.r], out=kto.t[:, :], in_=pk[:, 0:256])
                S.D("sp", K.KTOK[blk], kto.t[:, :], [kto.r], [K.R_KTOK[blk]])
        xload(0)
        gens = [tile(t) for t in range(9)]
        next(gens[0])
        for t in range(9):
            next(gens[t])
            if t + 1 < 9:
                next(gens[t + 1])
            for _ in gens[t]:
                pass
        S.flush()


def phaseG(K, S):
    with ExitStack() as ph:
        tri = sbt(K, ph, [128, 4, 128], BF16)
        S.D("pool", tri.t[:, :, :], K.inp["tri"], [], [tri.r])
        gmask = sbt(K, ph, [128, 2, 128], F32)
        S.D("sp", gmask.t[:, :, :], K.inp["gmask"], [], [gmask.r])
        decw = [sbt(K, ph, [17, 256], F32) for _ in range(2)]
        for d in range(2):
            S.D("sp", decw[d].t[:, :], K.inp["decw"][d], [], [decw[d].r])
        hnw = sbt(K, ph, [128, 1], F32)
        S.D("sp", hnw.t[:, :], K.inp["hnw"], [], [hnw.r])
        vtk = sbt(K, ph, [128, 34, 512], BF16)
        S.D("sp", vtk.t[:, :, :], K.VTOK1.rearrange("b p e -> p b e"), K.R_VTOK1, [vtk.r])
        ktk = sbt(K, ph, [128, 34, 256], BF16)
        S.D("sp", ktk.t[:, :, :], K.KTOK.rearrange("b p e -> p b e"), K.R_KTOK, [ktk.r])
        sst = sbt(K, ph, [128, 2 * 64 * 2, 128], BF16)
        scurs = [sbt(K, ph, [128, 2, 128], F32) for _ in range(2)]
        for sc_ in scurs:
            S.I("dve", "memset", [], [sc_.r], sc_.t[:, :, :], 0.0)
        p1 = ph
        if True:
            alrs = sbrot(K, p1, 5, [17, 128], F32)
            for a in alrs.items:
                S.I("pool", "memset", [], [a.r], a.t[:, :], 1.0)
            e1s = sbrot(K, p1, 2, [128, 256], F32)
            Ls = sbrot(K, p1, 3, [128, 256], BF16)
            ebs = sbrot(K, p1, 3, [128, 2, 128], F32)
            enbs = sbrot(K, p1, 2, [128, 2, 128], F32)
            gqs = sbrot(K, p1, 6, [128, 2, 128], BF16)
            gks = sbrot(K, p1, 6, [128, 2, 128], BF16)
            qes = sbrot(K, p1, 3, [128, 2, 128], BF16)
            kes = sbrot(K, p1, 3, [128, 2, 128], BF16)
            edss = sbrot(K, p1, 2, [128, 256], F32)
            kds = sbrot(K, p1, 3, [128, 256], BF16)
            seq = []
            for d in range(2):
                order = [32, 33] + list(range(32)) if d == 0 else [33, 32] + list(range(31, -1, -1))
                for blk in order:
                    seq.append((d, blk))
            NS = len(seq)
            cx = [None] * NS
            def next_slot(d, blk, c):
                cs = (0, 1) if d == 0 else (1, 0)
                order = [32, 33] + list(range(32)) if d == 0 else [33, 32] + list(range(31, -1, -1))
                chunks = [(bk, cc) for bk in order for cc in cs]
                i = chunks.index((blk, c))
                if i + 1 >= len(chunks):
                    return None
                nb_, nc_ = chunks[i + 1]
                if nb_ >= 32:
                    return None
                return (d * 64 + nb_ * 2 + nc_) * 2

            pre = [None] * NS

            def P0(k):
                d, blk = seq[k]
                tok0 = blk * 128
                al = alrs.next()
                S.D("sp", al.t[0:16, :], K.ALR[d, :, tok0:tok0 + 128], [K.R_ALR[blk]], [al.r])
                gq = gk = None
                if blk < 32:
                    gq, gk = gqs.next(), gks.next()
                    S.D("sp", gq.t[:, :, :], K.GQ[:, :, tok0:tok0 + 128], [K.R_GQ[blk]], [gq.r])
                    S.D("sp", gk.t[:, :, :], K.GK[:, :, tok0:tok0 + 128], [K.R_GK[blk]], [gk.r])
                pre[k] = (al, gq, gk)

            def P1(k):
                d, blk = seq[k]
                tok0 = blk * 128
                al = pre[k][0]
                z, rz = bank(K)
                S.I("pe", "matmul", [al.r, decw[d].r], [rz], z[:, 0:256], lhsT=al.t[0:17, :], rhs=decw[d].t[0:17, :], start=True, stop=True)
                e1, L = e1s.next(), Ls.next()
                S.I("act", "activation", [rz], [e1.r], out=e1.t[:, :], in_=z[:, 0:256], func=AF.Exp, scale=-1.0)
                S.I("act", "activation", [e1.r], [L.r], out=L.t[:, :], in_=e1.t[:, :], func=AF.Ln, bias=1.0)
                cx[k] = dict(L=L)

            def P2(k):
                d, blk = seq[k]
                tok0 = blk * 128
                isx = blk < 32
                L = cx[k]["L"]
                bT, rbT = bank(K)
                for hp in range(2):
                    S.I("pe", "matmul", [L.r, tri.r], [rbT], bT[:, hp * 128:(hp + 1) * 128], lhsT=L.t[:, hp * 128:(hp + 1) * 128], rhs=tri.t[:, 2 * d, :],
                        start=True, stop=True)
                ds, rds = bank(K)
                S.I("pe", "matmul", [L.r, tri.r], [rds], ds[:, 0:256], lhsT=tri.t[:, 2 * d + 1, :], rhs=L.t[:, :], start=True, stop=True)
                eb = ebs.next()
                S.I("act", "activation", [rbT], [eb.r], out=eb.t[:, :, :].rearrange("p a b -> p (a b)"), in_=bT[:, 0:256], func=AF.Exp)
                eds, kd = edss.next(), kds.next()
                S.I("act", "activation", [rds], [eds.r], out=eds.t[:, :], in_=ds[:, 0:256], func=AF.Exp)
                S.I("dve", "tensor_tensor", [ktk.r, eds.r], [kd.r], out=kd.t[:, :], in0=ktk.t[:, blk, :], in1=eds.t[:, :], op=ALU.mult)
                if isx:
                    enb, qe, ke = enbs.next(), qes.next(), kes.next()
                    gq, gk = pre[k][1], pre[k][2]
                    S.I("act", "activation", [rbT], [enb.r], out=enb.t[:, :, :].rearrange("p a b -> p (a b)"), in_=bT[:, 0:256], func=AF.Exp, scale=-1.0)
                    S.I("dve", "tensor_tensor", [gq.r, eb.r], [qe.r], out=qe.t[:, :, :], in0=gq.t[:, :, :], in1=eb.t[:, :, :], op=ALU.mult)
                    S.I("pool", "tensor_tensor", [gk.r, enb.r], [ke.r], out=ke.t[:, :, :], in0=gk.t[:, :, :], in1=enb.t[:, :, :], op=ALU.mult)
                    S.D("sp", K.QE[d][:, :, tok0:tok0 + 128], qe.t[:, :, :], [qe.r], [getattr(K, "R_QE%d" % d)[blk]])
                    S.D("sp", K.KE[d][:, :, tok0:tok0 + 128], ke.t[:, :, :], [ke.r], [getattr(K, "R_KE%d" % d)[blk]])
                cx[k]["eb"] = eb
                cx[k]["kd"] = kd

            def P3(k):
                d, blk = seq[k]
                eb, kd = cx[k]["eb"], cx[k]["kd"]
                scur = scurs[d]
                for c in ((0, 1) if d == 0 else (1, 0)):
                    cs = slice(c * 64, (c + 1) * 64)
                    kv, rkv = bank(K)
                    for hp in range(2):
                        for hh in range(2):
                            h = 2 * hp + hh
                            S.I("pe", "matmul", [kd.r, vtk.r], [rkv], kv[hh * 64:(hh + 1) * 64, hp * 128:(hp + 1) * 128], lhsT=kd.t[cs, h * 64:(h + 1) * 64],
                                rhs=vtk.t[cs, blk, h * 128:(h + 1) * 128], start=True, stop=True)
                    col = c * 64 + 63 if d == 0 else c * 64
                    slot = next_slot(d, blk, c)
                    for hp in range(2):
                        if slot is not None:
                            S.I("dve", "scalar_tensor_tensor", [scur.r, eb.r, rkv], [sst.r], out=sst.t[:, slot + hp, :], in0=scur.t[:, hp, :],
                                scalar=eb.t[:, hp, col:col + 1], in1=kv[:, hp * 128:(hp + 1) * 128], op0=ALU.mult, op1=ALU.add)
                        S.I("dve", "scalar_tensor_tensor", [scur.r, eb.r, rkv], [scur.r], out=scur.t[:, hp, :], in0=scur.t[:, hp, :],
                            scalar=eb.t[:, hp, col:col + 1], in1=kv[:, hp * 128:(hp + 1) * 128], op0=ALU.mult, op1=ALU.add)
                cx[k] = None
            P0(0)
            P0(1)
            for st in range(NS + 2):
                if st + 2 < NS:
                    P0(st + 2)
                if st < NS:
                    P1(st)
                if 0 <= st - 2 < NS:
                    P3(st - 2)
                if 0 <= st - 1 < NS:
                    P2(st - 1)
        K.bank_pool = [0, 1, 2, 3, 4, 5]
        K.bank_rr = 0
        p2 = ph
        if True:
            qets = [sbrot(K, p2, 2, [128, 2, 512], BF16) for _ in range(2)]
            kets = [sbrot(K, p2, 2, [128, 2, 512], BF16) for _ in range(2)]
            sgs = sbrot(K, p2, 2, [128, 4, 512], BF16)
            gos = sbrot(K, p2, 2, [128, 4, 512], BF16)
            atms = sbrot(K, p2, 4, [128, 128], BF16)
            sqo = sbt(K, p2, [128, 512], BF16)
            rst = sbt(K, p2, [128, 512], F32)
            t1 = sbt(K, p2, [128, 512], F32)
            nacc = 0
            def g2load(T):
                t0 = T * 512
                qe = [qets[d].next() for d in range(2)]
                ke = [kets[d].next() for d in range(2)]
                for d in range(2):
                    S.D("sp", qe[d].t[:, :, :], K.QE[d][:, :, t0:t0 + 512], blkres(getattr(K, "R_QE%d" % d), t0, 512), [qe[d].r])
                    S.D("sp", ke[d].t[:, :, :], K.KE[d][:, :, t0:t0 + 512], blkres(getattr(K, "R_KE%d" % d), t0, 512), [ke[d].r])
                sg = sgs.next()
                S.D("sp", sg.t[:, :, :], K.SG[:, :, t0:t0 + 512], blkres(K.R_SG, t0, 512), [sg.r])
                return qe, ke, sg
            nxt = g2load(0)
            for T in range(8):
                t0 = T * 512
                qe, ke, sg = nxt
                if T + 1 < 8:
                    nxt = g2load(T + 1)
                go = gos.next()
                for h in range(4):
                    hp, hb = h // 2, (h % 2) * 64
                    oT, roT = bank_fixed(K, 6 + (nacc % 2))
                    nacc += 1
                    for bb in range(4):
                        blk = T * 4 + bb
                        cols = slice(bb * 128, (bb + 1) * 128)
                        atm = []
                        for d in range(2):
                            att, ratt = bank(K)
                            S.I("pe", "matmul", [ke[d].r, qe[d].r], [ratt], att[:, 0:128], lhsT=ke[d].t[hb:hb + 64, hp, cols], rhs=qe[d].t[hb:hb + 64, hp, cols],
                                start=True, stop=True)
                            am = atms.next()
                            S.I("dve", "tensor_tensor", [ratt, gmask.r], [am.r], out=am.t[:, :], in0=att[:, 0:128], in1=gmask.t[:, d, :], op=ALU.mult)
                            atm.append(am)
                        S.I("pe", "matmul", [vtk.r, atm[0].r], [roT], oT[:, cols], lhsT=vtk.t[:, blk, h * 128:(h + 1) * 128], rhs=atm[0].t[:, :], start=True, stop=False)
                        S.I("pe", "matmul", [vtk.r, atm[1].r], [roT], oT[:, cols], lhsT=vtk.t[:, blk, h * 128:(h + 1) * 128], rhs=atm[1].t[:, :], start=False, stop=False)
                        for c in range(2):
                            for d in range(2):
                                nch = blk * 2 + c
                                base = (d * 64 + nch) * 2 + hp
                                cc = slice(bb * 128 + c * 64, bb * 128 + (c + 1) * 64)
                                S.I("pe", "matmul", [sst.r, qe[d].r], [roT], oT[:, cc], lhsT=sst.t[hb:hb + 64, base, :], rhs=qe[d].t[hb:hb + 64, hp, cc],
                                    start=False, stop=(c == 1 and d == 1))
                    S.I("act", "activation", [roT], [sqo.r], out=sqo.t[:, :], in_=oT, func=AF.Square)
                    ms, rms = bank(K)
                    S.I("pe", "matmul", [K.ones_bf.r, sqo.r], [rms], ms, lhsT=K.ones_bf.t[:, :], rhs=sqo.t[:, :], start=True, stop=True)
                    S.I("act", "activation", [rms], [rst.r], out=rst.t[:, :], in_=ms, func=AF.Ln, scale=1.0 / 128, bias=K.epsc.t[:, 0:1])
                    S.I("act", "activation", [rst.r], [rst.r], out=rst.t[:, :], in_=rst.t[:, :], func=AF.Exp, scale=-0.5)
                    S.I("dve", "tensor_tensor", [roT, rst.r], [t1.r], out=t1.t[:, :], in0=oT, in1=rst.t[:, :], op=ALU.mult)
                    S.I("dve", "scalar_tensor_tensor", [t1.r, hnw.r, sg.r], [go.r], out=go.t[:, h, :], in0=t1.t[:, :], scalar=hnw.t[:, 0:1], in1=sg.t[:, h, :],
                        op0=ALU.mult, op1=ALU.mult)
                S.D("sp", K.MIX[:, 4:8, t0:t0 + 512], go.t[:, :, :], [go.r], blkres(K.R_MIXb, t0, 512))
        K.bank_pool = None
        K.bank_rr = 0
        S.flush()


def phaseF(K, S):
    with ExitStack() as ph:
        X = sbt(K, ph, [128, 32768], BF16)
        Y = sbt(K, ph, [128, 32768], BF16)
        wf = sbt(K, ph, [128, 8, 512], BF16)
        S.D("pool", wf.t[:, :, :], K.inp["cd_w_in"].rearrange("(kc p) n -> p kc n", p=128)[:, :, 0:512], [], [wf.r])
        w128 = sbt(K, ph, [128, 256], BF16)
        S.D("pool", w128.t[:, :], K.inp["w128ri"], [], [w128.r])
        wc = [sbt(K, ph, [128, 256], BF16) for _ in range(2)]
        S.D("pool", wc[0].t[:, :], K.inp["wc1"], [], [wc[0].r])
        S.D("pool", wc[1].t[:, :], K.inp["wc2"], [], [wc[1].r])
        tw = sbt(K, ph, [128, 32, 2, 128], BF16)
        S.D("pool", tw.t[:, :, :, :], K.inp["tw"], [], [tw.r])
        hTv = X.t[:, :].rearrange("p (k a b) -> p k a b", k=8, b=32)
        for kc in range(KC):
            S.D("sp", X.t[:, kc * 4096:(kc + 1) * 4096], K.HT[:, kc, :], K.R_HT[:32], [X.r])
        F1 = Y.t[:, 0:16384].rearrange("p (a b) -> p a b", b=512)
        nev = [0]

        def evac(reads, writes, out, in_):
            nev[0] += 1
            if nev[0] % 2:
                S.I("act", "copy", reads, writes, out=out, in_=in_)
            else:
                S.I("dve", "tensor_copy", reads, writes, out=out, in_=in_)
        for n2 in range(32):
            ps, rps = bank(K)
            for kc in range(KC):
                S.I("pe", "matmul", [X.r, wf.r], [rps], ps, lhsT=hTv[:, kc, :, n2], rhs=wf.t[:, kc, :], start=(kc == 0), stop=(kc == KC - 1))
            evac([rps], [Y.r], F1[:, n2, :], ps)
        A2 = X.t[:, :].rearrange("p (g r h n l) -> p g r h n l", g=4, r=2, h=32, n=32)
        for n2 in range(32):
            for g in range(4):
                ps, rps = bank(K)
                S.I("pe", "matmul", [Y.r, w128.r], [rps], ps[:, 0:256], lhsT=F1[:, n2, g * 128:(g + 1) * 128], rhs=w128.t[:, :], start=True, stop=True)
                evac([rps], [X.r], A2[:, g, :, :, n2, :], ps[:, 0:256].rearrange("p (r h l) -> p r h l", r=2, l=4))
        Z3 = Y.t[:, :].rearrange("p (g h r c) -> p g h r c", g=4, h=32, r=2)
        for g in range(4):
            for kh in range(32):
                ps, rps = bank(K)
                for ri in range(2):
                    off = ((g * 2 + ri) * 32 + kh) * 128
                    S.I("pe", "matmul", [X.r, wc[ri].r], [rps], ps[:, 0:256], lhsT=X.t[:, off:off + 128], rhs=wc[ri].t[:, :], start=(ri == 0), stop=(ri == 1))
                evac([rps], [Y.r], Z3[:, g, kh, :, :], ps[:, 0:256].rearrange("p (r c) -> p r c", r=2))
        fn = X.t[:, 0:16384].rearrange("p (g a h l) -> p g a h l", g=4, a=32, h=32)
        scale = float(1.0 / np.sqrt(4096.0 * 128.0))
        for g in range(4):
            for kh in range(32):
                ps, rps = bank(K)
                for ri in range(2):
                    S.I("pe", "matmul", [Y.r, tw.r], [rps], ps[:, 0:128], lhsT=Z3[:, g, kh, ri, :], rhs=tw.t[:, kh, ri, :], start=(ri == 0), stop=(ri == 1))
                nev[0] += 1
                src = ps[:, 0:128].rearrange("p (a l) -> p a l", l=4)
                if nev[0] % 2:
                    S.I("act", "mul", [rps], [X.r], out=fn[:, g, :, kh, :], in_=src, mul=scale)
                else:
                    S.I("dve", "tensor_scalar", [rps], [X.r], out=fn[:, g, :, kh, :], in0=src, scalar1=scale, scalar2=None, op0=ALU.mult)
        for g in range(4):
            S.D("sp", K.MIX[:, g, 0:NX], X.t[:, g * 4096:(g + 1) * 4096], [X.r], K.R_MIXa[:32])
        S.flush()


def l1_run(K, S):
    phaseA1(K, S)
    phaseG(K, S)
    phaseF(K, S)


def l1_prep(I, shared):
    f = lambda a: np.ascontiguousarray(np.asarray(a, dtype=np.float32))
    w_in = I["cd_w_in"][0]
    shared["cd_w_in"] = f(w_in)
    d = np.arange(64)
    e = d % 32
    partner = np.where(e < 16, d + 16, d - 16)
    perm = (np.arange(8)[:, None] * 64 + partner[None, :]).reshape(-1)
    shared["cd_w_perm"] = f(w_in[:, 512:1024][:, perm])
    shared["decw"] = f(np.stack([np.concatenate([I["cd_decay_w_fwd"][0], I["cd_decay_b_fwd"][0][None]], 0),
                                 np.concatenate([I["cd_decay_w_bwd"][0], I["cd_decay_b_bwd"][0][None]], 0)], 0))
    shared["hnw"] = f(I["cd_head_norm_w"][0].reshape(128, 1))
    m = np.arange(128)[:, None]
    l = np.arange(128)[None, :]
    sc = (m // 64) == (l // 64)
    tri = np.stack([sc & (m <= l), sc & (m > l), sc & (m >= l), sc & (m < l)], 1).astype(np.float32) * (-1.0 / 16.0)
    shared["tri"] = f(tri)
    shared["gmask"] = f(np.stack([sc & (m <= l), sc & (m >= l)], 1).astype(np.float32))
    t = np.arange(NX)
    fi = e % 16
    inv = 10000.0 ** (-(fi.astype(np.float64)) / 16.0)
    pos = np.where((d // 32)[:, None] == 0, (t // 64)[None, :], (t % 64)[None, :]).astype(np.float64)
    ang = pos * inv[:, None]
    C64 = np.cos(ang)
    S64 = np.sin(ang) * np.where(e < 16, -1.0, 1.0)[:, None]
    shared["ropeC"] = f(np.concatenate([C64, C64], 0))
    shared["ropeS"] = f(np.concatenate([S64, S64], 0))
    Ct = np.tile(C64.T, (1, 4)).reshape(32, 128, 256)
    St = np.tile(S64.T, (1, 4)).reshape(32, 128, 256)
    shared["ropeCt"] = f(Ct)
    shared["ropeSt"] = f(St)
    n = np.arange(128)
    a128 = 2 * np.pi * np.outer(n, n) / 128.0
    Cn, Sn = np.cos(a128), np.sin(a128)
    shared["w128ri"] = f(np.concatenate([Cn, -Sn], 1))
    shared["wc1"] = f(np.concatenate([Cn, -Sn], 1))
    shared["wc2"] = f(np.concatenate([Sn, Cn], 1))
    n2 = np.arange(32)
    k2 = np.arange(32)
    TW = np.zeros((32, 4, 32, 32, 4), np.complex128)
    for kh in range(32):
        for kl in range(4):
            k1 = 4 * kh + kl
            TW[:, kl, kh, :, kl] = np.exp(-2j * np.pi * n2 * k1 / 4096.0)[:, None] * np.exp(-2j * np.pi * np.outer(n2, k2) / 32.0)
    TW = TW.reshape(128, 32, 128)
    shared["tw"] = f(np.stack([TW.real, -TW.imag], 2))


LAYER1 = {"decl": l1_decl, "scratch": l1_scratch, "run": l1_run, "prep": l1_prep}


def _na_bias_tiles(rel_bias):
    NEG = np.float32(-30000.0)
    H = rel_bias.shape[0]
    out = np.full((H, 128, 21, 128), NEG, np.float32)
    kc = np.arange(64)
    qc = np.arange(64)
    cs = np.clip(qc - 8, 0, 48)
    colok = (kc[:, None] >= cs[None, :]) & (kc[:, None] < cs[None, :] + 16)
    cidx = np.clip(kc[:, None] - qc[None, :] + 15, 0, 30)

    def tile(i, m):
        t = np.full((H, 128, 128), NEG, np.float32)
        for a in range(2):
            kr = 2 * m + a
            for b in range(2):
                qr = 2 * i + b
                rs = min(max(qr - 4, 0), 56)
                if not (rs <= kr < rs + 8):
                    continue
                ridx = kr - qr + 7
                vals = rel_bias[:, ridx, :][:, cidx]
                vals = np.where(colok[None], vals, NEG)
                t[:, a * 64:(a + 1) * 64, b * 64:(b + 1) * 64] = vals
        return t
    for d in range(5):
        out[:, :, d, :] = tile(10, 10 + d - 2)
    for s, i in enumerate((0, 1, 30, 31)):
        ms = list(range(4)) if i < 2 else list(range(28, 32))
        for ci, m in enumerate(ms):
            out[:, :, 5 + 4 * s + ci, :] = tile(i, m)
    return out


def _consts():
    c = {}
    c["ident"] = np.eye(128, dtype=np.float32)
    return c


def build_program():
    nc = bass.Bass("TRN2", target_bir_lowering=False)
    K = KB()
    K.nc = nc
    K.bank_rr = 0
    K.inp = {}

    def din(name, shape, dt=F32):
        K.inp[name] = nc.dram_tensor(name, list(shape), dt, kind="ExternalInput").ap()
    din("x", [NX, D]); din("ctx", [NCX, D]); din("csT", [128, 8, 2]); din("ada_w", [2, D, 6 * D]); din("ada_bT", [2, 128, 48])
    din("nmw", [2, 128, 8]); din("nfw", [2, 128, 8]); din("fnw_bc", [128, D])
    din("ffn_wg", [2, D, FH]); din("ffn_wu", [2, D, FH]); din("ffn_wd", [2, FH, D])
    din("ab_w_in", [D, 2560]); din("ab_w_out", [D, D]); din("sgu_nw_bc", [128, 512]); din("sguWT", [128, 4, 128]); din("sgub", [128, 512])
    din("nab", [8, 128, 21, 128]); din("ident", [128, 128])
    din("cd_w_out", [D, D])
    if LAYER1 is not None:
        LAYER1["decl"](K, din)
    K.out = nc.dram_tensor("out", [NX, D], F32, kind="ExternalOutput").ap()
    K.R_out = Res()

    def scr(name, shape, dt):
        kind = "ExternalOutput" if (DEBUG and name in DEBUG) else "Internal"
        return nc.dram_tensor(name, list(shape), dt, kind=kind).ap()
    K.XT = scr("XT", [128, 8, NT], F32)
    K.MIX = scr("MIX", [128, 8, NT], BF16)
    K.QT = scr("QT", [128, 4, NT], BF16)
    K.KT = scr("KT", [128, 4, NT], BF16)
    K.VT = scr("VT", [34, 128, 520], BF16)
    K.WGb = [[scr("WG%d_%d" % (l, m), [D, FH], BF16) for m in range(2)] for l in range(2)]
    K.R_XT = [Res() for _ in range(34)]
    K.R_MIXa = [Res() for _ in range(34)]
    K.R_MIXb = [Res() for _ in range(34)]
    K.R_QT = [Res() for _ in range(34)]
    K.R_KT = [Res() for _ in range(34)]
    K.R_VT = [Res() for _ in range(34)]
    K.R_WG = [[[Res() for _ in range(KC)] for _ in range(2)] for _ in range(2)]
    if LAYER1 is not None:
        LAYER1["scratch"](K, scr)
    with ExitStack() as es:
        S = Sched(nc, es)
        K.PB = [TT(es.enter_context(nc.psum_tensor("pb%d" % i, [128, 1024], F32))) for i in range(4)]
        K.RB = [Res() for _ in range(8)]
        phase_consts(K, S, es)
        phase_mod(K, S)
        phaseA0(K, S)
        phaseB0(K, S)
        phaseC(K, S, 0)
        if LAYER1 is not None:
            LAYER1["run"](K, S)
            phaseC(K, S, 1)
    return nc


def prep_inputs(inputs):
    f = lambda a: np.ascontiguousarray(np.asarray(a, dtype=np.float32))
    I = {k: np.asarray(v) for k, v in inputs.items()}
    shared = {}
    shared["ada_w"] = f(I["ada_w"])
    shared["ada_bT"] = f(I["ada_b"].reshape(2, 48, 128).transpose(0, 2, 1))
    shared["nmw"] = f(I["norm_mix_w"].reshape(2, 8, 128).transpose(0, 2, 1))
    shared["nfw"] = f(I["norm_ffn_w"].reshape(2, 8, 128).transpose(0, 2, 1))
    shared["fnw_bc"] = f(np.broadcast_to(I["final_norm_w"][None, :], (128, D)))
    shared["ffn_wg"] = f(I["ffn_w_gate"]); shared["ffn_wu"] = f(I["ffn_w_up"]); shared["ffn_wd"] = f(I["ffn_w_down"])
    shared["ab_w_in"] = f(I["ab_w_in"][0]); shared["ab_w_out"] = f(I["ab_w_out"][0])
    shared["sgu_nw_bc"] = f(np.broadcast_to(I["ab_sgu_norm_w"][0][None, :], (128, 512)))
    shared["sguWT"] = f(I["ab_sgu_w"][0].transpose(2, 0, 1))
    shared["sgub"] = f(np.broadcast_to(I["ab_sgu_b"][0].reshape(1, 512), (128, 512)))
    shared["nab"] = _na_bias_tiles(f(I["ab_rel_bias"][0]))
    shared["cd_w_out"] = f(I["cd_w_out"][0])
    shared.update(_consts())
    if LAYER1 is not None:
        LAYER1["prep"](I, shared)
    in_maps = []
    for b in range(8):
        m = dict(shared)
        m["x"] = f(I["x"][b]); m["ctx"] = f(I["ctx"][b])
        cs = np.stack([I["c"][b], I["c_ctx"]], axis=-1)
        m["csT"] = f(cs.reshape(8, 128, 2).transpose(1, 0, 2))
        in_maps.append(m)
    return in_maps


_NC_CACHE = {}


def kernel(**inputs):
    in_maps = prep_inputs(inputs)
    if "nc" not in _NC_CACHE:
        _NC_CACHE["nc"] = build_program()
    res = run_bass_kernel_spmd(_NC_CACHE["nc"], in_maps, core_ids=list(range(8)))
    return np.stack([np.asarray(r["out"], dtype=np.float32) for r in res.results], axis=0)
```

```python
import numpy as np
import concourse.bass as bass
import concourse.mybir as mybir
from concourse.bass_utils import run_bass_kernel_spmd
from contextlib import ExitStack

F32 = mybir.dt.float32
BF16 = mybir.dt.bfloat16
AF = mybir.ActivationFunctionType
ALU = mybir.AluOpType
AX = mybir.AxisListType

NX, NCX, NT, D, KC, FH, NJ = 4096, 256, 4352, 1024, 8, 2816, 22
EPS = 1e-6
DEBUG = False


class Res:
    __slots__ = ("w", "r")

    def __init__(self):
        self.w = {}
        self.r = {}


class Sched:
    NLANES = 12

    def __init__(self, nc, es):
        self.nc = nc
        self.keys = ("pe", "act", "dve", "pool", "sp")
        self.streams = {k: [] for k in self.keys}
        self.sems, self.count = {}, {}
        self.known = {k: {} for k in self.keys}
        for k in ("pe", "act", "dve", "pool"):
            self.sems[k] = es.enter_context(nc.semaphore("s_" + k))
            self.count[k] = 0
        self.lanes = {}
        for q in ("sp", "pool"):
            self.lanes[q] = []
            for i in range(self.NLANES):
                key = "L%s%d" % (q, i)
                self.sems[key] = es.enter_context(nc.semaphore(key))
                self.count[key] = 0
                self.lanes[q].append(key)
        self.lane_rr = {q: 0 for q in self.lanes}

    def _need(self, issuer, ckey, seq, waits):
        if not seq:
            return
        if ckey == "pe" and issuer == "pe":
            return
        if self.known[issuer].get(ckey, 0) >= seq:
            return
        self.known[issuer][ckey] = seq
        waits[ckey] = max(waits.get(ckey, 0), seq)

    def _deps(self, issuer, reads, writes):
        waits = {}
        for r in reads:
            for ck, sq in r.w.items():
                self._need(issuer, ck, sq, waits)
        for r in writes:
            for ck, sq in r.w.items():
                self._need(issuer, ck, sq, waits)
            for ck, sq in r.r.items():
                self._need(issuer, ck, sq, waits)
        return waits

    def _mark(self, ckey, seq, reads, writes):
        for r in reads:
            if r.r.get(ckey, 0) < seq:
                r.r[ckey] = seq
        for r in writes:
            r.w = {ckey: seq}
            r.r = {}

    def I(self, eng, meth, reads, writes, *a, **kw):
        waits = self._deps(eng, reads, writes)
        self.count[eng] += 1
        seq = self.count[eng]
        self._mark(eng, seq, reads, writes)
        self.streams[eng].append((lambda e: getattr(e, meth)(*a, **kw), waits, eng, 1))

    def D(self, q, out, in_, reads, writes, **kw):
        lanes = self.lanes[q]
        lane = lanes[self.lane_rr[q] % len(lanes)]
        self.lane_rr[q] += 1
        waits = self._deps(q, reads, writes)
        self._need(q, lane, self.count[lane], waits)
        self.count[lane] += 1
        seq = self.count[lane]
        self._mark(lane, seq, reads, writes)
        self.streams[q].append((lambda e: e.dma_start(out=out, in_=in_, **kw), waits, lane, 16))

    def barrier(self):
        for e in self.keys:
            waits = {}
            for ck, c in self.count.items():
                self._need(e, ck, c, waits)
            if waits:
                self.streams[e].append((None, waits, None, 0))

    def _mult(self, ck):
        return 16 if ck.startswith("L") else 1

    def flush(self):
        self.barrier()
        nc = self.nc
        streams = self.streams
        self.streams = {k: [] for k in self.keys}

        def mk(key):
            def body(e):
                for fn, waits, ckey, inc in streams[key]:
                    for wk, sq in waits.items():
                        e.wait_ge(self.sems[wk], sq * self._mult(wk))
                    if fn is not None:
                        fn(e).then_inc(self.sems[ckey], inc)
            return body
        with nc.Block() as block:
            block.tensor(mk("pe"))
            block.scalar(mk("act"))
            block.vector(mk("dve"))
            block.gpsimd(mk("pool"))
            block.sync(mk("sp"))


class TT:
    def __init__(self, t):
        self.t = t
        self.r = Res()


class Rot:
    def __init__(self, items):
        self.items = items
        self.i = 0

    def next(self):
        x = self.items[self.i % len(self.items)]
        self.i += 1
        return x


class KB:
    pass


_uid = [0]


def sbt(K, ph, shape, dt):
    _uid[0] += 1
    return TT(ph.enter_context(K.nc.sbuf_tensor("t%d" % _uid[0], list(shape), dt)))


def sbrot(K, ph, n, shape, dt):
    return Rot([sbt(K, ph, shape, dt) for _ in range(n)])


def bank(K):
    pool = getattr(K, "bank_pool", None) or list(range(8))
    i = pool[K.bank_rr % len(pool)]
    K.bank_rr += 1
    return K.PB[i // 2].t[:, (i % 2) * 512:(i % 2) * 512 + 512], K.RB[i]


def bank_fixed(K, i):
    return K.PB[i // 2].t[:, (i % 2) * 512:(i % 2) * 512 + 512], K.RB[i]


def bank2(K):
    if K.bank_rr % 2:
        K.bank_rr += 1
    i = (K.bank_rr % 8) // 2
    K.bank_rr += 2
    return K.PB[i].t[:, :], [K.RB[2 * i], K.RB[2 * i + 1]]


def norm_mod(K, S, xt, n, A, B, j, hT, tmps, sq, rstd):
    S.I("act", "activation", [xt.r], [sq.r], out=sq.t[:, :, :n], in_=xt.t[:, :, :n], func=AF.Square)
    ms, rms = bank(K)
    for kc in range(KC):
        S.I("pe", "matmul", [K.ones_bf.r, sq.r], [rms], ms[:, :n], lhsT=K.ones_bf.t[:, :], rhs=sq.t[:, kc, :n],
            start=(kc == 0), stop=(kc == KC - 1))
    S.I("act", "activation", [rms], [rstd.r], out=rstd.t[:, :n], in_=ms[:, :n], func=AF.Ln, scale=1.0 / D, bias=K.epsc.t[:, 0:1])
    S.I("act", "activation", [rstd.r], [rstd.r], out=rstd.t[:, :n], in_=rstd.t[:, :n], func=AF.Exp, scale=-0.5)
    for kc in range(KC):
        tm = tmps.next()
        S.I("dve", "tensor_tensor", [xt.r, rstd.r], [tm.r], out=tm.t[:, :n], in0=xt.t[:, kc, :n], in1=rstd.t[:, :n], op=ALU.mult)
        S.I("dve", "tensor_scalar", [tm.r, K.modr], [hT.r], out=hT.t[:, kc, :n], in0=tm.t[:, :n], scalar1=A.t[:, kc, j:j + 1],
            scalar2=B.t[:, kc, j:j + 1], op0=ALU.mult, op1=ALU.add)


def small_rstd(K, S, ss, rs, dim):
    S.I("act", "activation", [ss.r], [rs.r], out=rs.t[:, 0:1], in_=ss.t[:, 0:1], func=AF.Ln, scale=1.0 / dim, bias=K.epsc.t[:, 0:1])
    S.I("act", "activation", [rs.r], [rs.r], out=rs.t[:, 0:1], in_=rs.t[:, 0:1], func=AF.Exp, scale=-0.5)


def blkres(rl, t0, n):
    return rl[t0 // 128:(t0 + n) // 128]


def phase_consts(K, S, ph):
    nc = K.nc
    K.ident = sbt(K, ph, [128, 128], F32)
    S.D("sp", K.ident.t[:, :], K.inp["ident"], [], [K.ident.r])
    K.ident_bf = sbt(K, ph, [128, 128], BF16)
    S.D("pool", K.ident_bf.t[:, :], K.inp["ident"], [], [K.ident_bf.r])
    K.ones_bf = sbt(K, ph, [128, 128], BF16)
    S.I("pool", "memset", [], [K.ones_bf.r], K.ones_bf.t[:, :], 1.0)
    K.ones_f = sbt(K, ph, [1, 128], F32)
    S.I("pool", "memset", [], [K.ones_f.r], K.ones_f.t[:, :], 1.0)
    K.epsc = sbt(K, ph, [128, 1], F32)
    S.I("pool", "memset", [], [K.epsc.r], K.epsc.t[:, :], EPS)
    K.modr = Res()
    K.mods = []
    for l in range(2):
        K.mods.append({k: sbt(K, ph, [128, 8, 2], F32) for k in ("Am", "Bm", "Gm", "Af", "Bf", "Gf")})
        for v in K.mods[l].values():
            v.r = K.modr


def phase_mod(K, S):
    with ExitStack() as ph:
        csil = sbt(K, ph, [128, 8, 2], F32)
        S.D("sp", csil.t[:, :, :], K.inp["csT"], [], [csil.r])
        S.I("act", "activation", [csil.r], [csil.r], out=csil.t[:, :, :], in_=csil.t[:, :, :], func=AF.Silu)
        awts = sbrot(K, ph, 3, [128, 8, 512], F32)
        modrow = sbt(K, ph, [2, 6144], F32)
        mod = sbt(K, ph, [128, 48, 2], F32)
        abT = sbt(K, ph, [128, 48], F32)
        nw = sbt(K, ph, [128, 8], F32)
        fins = sbrot(K, ph, 3, [128, FH], F32)
        fouts = sbrot(K, ph, 3, [128, FH], BF16)
        engs = Rot(["pool", "dve", "pool", "act"])
        wcl = [(l, m, nm, kc) for l in range(2) for m, nm in ((0, "ffn_wg"), (1, "ffn_wu")) for kc in range(KC)]
        wst = {}

        def w_load(i):
            l, m, nm, kc = wcl[i]
            a = fins.next()
            S.D("sp", a.t[:, :], K.inp[nm][l][kc * 128:(kc + 1) * 128, :], [], [a.r])
            wst[i] = a

        def w_cast_store(i):
            l, m, nm, kc = wcl[i]
            a = wst.pop(i)
            b = fouts.next()
            e = engs.next()
            if e == "act":
                S.I("act", "copy", [a.r], [b.r], out=b.t[:, :], in_=a.t[:, :])
            else:
                S.I(e, "tensor_copy", [a.r], [b.r], out=b.t[:, :], in_=a.t[:, :])
            S.D("sp", K.WGb[l][m][kc * 128:(kc + 1) * 128, :], b.t[:, :], [b.r], [K.R_WG[l][m][kc]])

        def m_block(i):
            l, cb = divmod(i, 12)
            awv = K.inp["ada_w"][l].rearrange("(kc p) n -> p kc n", p=128)
            aw = awts.next()
            S.D("sp", aw.t[:, :, :], awv[:, :, cb * 512:(cb + 1) * 512], [], [aw.r])
            ps, rps = bank(K)
            for kc in range(KC):
                S.I("pe", "matmul", [aw.r, csil.r], [rps], ps[0:2, 0:512], lhsT=csil.t[:, kc, :], rhs=aw.t[:, kc, :], start=(kc == 0), stop=(kc == KC - 1))
            if cb % 2:
                S.I("act", "copy", [rps], [modrow.r], out=modrow.t[0:2, cb * 512:(cb + 1) * 512], in_=ps[0:2, 0:512])
            else:
                S.I("dve", "tensor_copy", [rps], [modrow.r], out=modrow.t[0:2, cb * 512:(cb + 1) * 512], in_=ps[0:2, 0:512])
            if cb == 11:
                m_final(l)

        def m_final(l):
            mp, rmp = bank(K)
            for j in range(48):
                S.I("pe", "transpose", [modrow.r, K.ident.r], [rmp], out=mp[:, 2 * j:2 * j + 2], in_=modrow.t[0:2, j * 128:(j + 1) * 128], identity=K.ident.t[0:2, 0:2])
            S.D("sp", abT.t[:, :], K.inp["ada_bT"][l], [], [abT.r])
            mpv = mp[:, 0:96].rearrange("p (a b) -> p a b", b=2)
            for j in range(2):
                S.I("dve", "tensor_tensor", [rmp, abT.r], [mod.r], out=mod.t[:, :, j], in0=mpv[:, :, j], in1=abT.t[:, :], op=ALU.add)
            M = K.mods[l]
            for (nm, a, b, g, off) in (("nmw", "Am", "Bm", "Gm", 0), ("nfw", "Af", "Bf", "Gf", 24)):
                S.D("sp", nw.t[:, :], K.inp[nm][l], [], [nw.r])
                for j in range(2):
                    S.I("dve", "scalar_tensor_tensor", [mod.r, nw.r], [K.modr], out=M[a].t[:, :, j], in0=mod.t[:, off + 8:off + 16, j],
                        scalar=1.0, in1=nw.t[:, :], op0=ALU.add, op1=ALU.mult)
                    S.I("dve", "tensor_copy", [mod.r], [K.modr], out=M[b].t[:, :, j], in_=mod.t[:, off:off + 8, j])
                    S.I("dve", "tensor_copy", [mod.r], [K.modr], out=M[g].t[:, :, j], in_=mod.t[:, off + 16:off + 24, j])
        NW = len(wcl)
        w_load(0)
        for step in range(NW):
            if step + 1 < NW:
                w_load(step + 1)
            if step < 24:
                m_block(step)
            w_cast_store(step)
        S.flush()


def phase_wcast(K, S):
    with ExitStack() as ph:
        fin = sbrot(K, ph, 2, [128, FH], F32)
        fout = sbrot(K, ph, 2, [128, FH], BF16)
        engs = Rot(["pool", "dve", "act"])
        for l in range(2):
            for m, nm in ((0, "ffn_wg"), (1, "ffn_wu")):
                for kc in range(KC):
                    a, b = fin.next(), fout.next()
                    S.D("sp", a.t[:, :], K.inp[nm][l][kc * 128:(kc + 1) * 128, :], [], [a.r])
                    e = engs.next()
                    if e == "act":
                        S.I("act", "copy", [a.r], [b.r], out=b.t[:, :], in_=a.t[:, :])
                    else:
                        S.I(e, "tensor_copy", [a.r], [b.r], out=b.t[:, :], in_=a.t[:, :])
                    S.D("sp", K.WGb[l][m][kc * 128:(kc + 1) * 128, :], b.t[:, :], [b.r], [K.R_WG[l][m][kc]])
        S.flush()


def phase0(K, S):
    with ExitStack() as ph:
        xin = sbrot(K, ph, 2, [128, 1024], F32)
        xo = sbrot(K, ph, 2, [128, 8, 128], F32)
        for blk in range(34):
            src = K.inp["x"][blk * 128:(blk + 1) * 128, :] if blk < 32 else K.inp["ctx"][(blk - 32) * 128:(blk - 31) * 128, :]
            a = xin.next()
            S.D("sp", a.t[:, :], src, [], [a.r])
            pt, rpt = bank2(K)
            for kc in range(KC):
                S.I("pe", "transpose", [a.r, K.ident.r], rpt, out=pt[:, kc * 128:(kc + 1) * 128], in_=a.t[:, kc * 128:(kc + 1) * 128],
                    identity=K.ident.t[:, :])
            o = xo.next()
            ov = o.t[:, :, :].rearrange("p a b -> p (a b)")
            if blk % 2:
                S.I("act", "copy", rpt, [o.r], out=ov, in_=pt)
            else:
                S.I("dve", "tensor_copy", rpt, [o.r], out=ov, in_=pt)
            S.D("sp", K.XT[:, :, blk * 128:(blk + 1) * 128], o.t[:, :, :], [o.r], [K.R_XT[blk]])
        S.flush()


def phaseA0(K, S):
    M = K.mods[0]
    with ExitStack() as ph:
        win = sbt(K, ph, [128, 8, 2560], BF16)
        wv = K.inp["ab_w_in"].rearrange("(kc p) n -> p kc n", p=128)
        for kc in range(KC):
            S.D("pool", win.t[:, kc, :], wv[:, kc, :], [], [win.r])
        sguw = sbt(K, ph, [128, 4, 128], BF16)
        S.D("pool", sguw.t[:, :, :], K.inp["sguWT"], [], [sguw.r])
        nwbc = sbt(K, ph, [128, 512], F32)
        S.D("sp", nwbc.t[:, :], K.inp["sgu_nw_bc"], [], [nwbc.r])
        sgub = sbt(K, ph, [128, 512], F32)
        S.D("sp", sgub.t[:, :], K.inp["sgub"], [], [sgub.r])
        gts = sbrot(K, ph, 2, [128, 512], F32)
        xts = sbrot(K, ph, 2, [128, 8, 512], F32)
        xins = sbrot(K, ph, 8, [128, 1024], F32)
        prex = {}

        def xload(t):
            n = 512 if t < 8 else 256
            lst = []
            for sbk in range(n // 128):
                blk = t * 4 + sbk
                src = K.inp["x"][blk * 128:(blk + 1) * 128, :] if blk < 32 else K.inp["ctx"][(blk - 32) * 128:(blk - 31) * 128, :]
                a = xins.next()
                S.D("sp", a.t[:, :], src, [], [a.r])
                lst.append(a)
            prex[t] = lst
        sqs = sbrot(K, ph, 2, [128, 8, 512], BF16)
        rstds = sbrot(K, ph, 2, [128, 512], F32)
        tmps = sbrot(K, ph, 2, [128, 512], F32)
        hTs = sbrot(K, ph, 2, [128, 8, 512], BF16)
        uT = sbt(K, ph, [128, 4, 512], BF16)
        qos = sbrot(K, ph, 2, [128, 4, 512], BF16)
        kos = sbrot(K, ph, 2, [128, 4, 512], BF16)
        aos = sbrot(K, ph, 2, [128, 4, 512], BF16)
        vos = sbrot(K, ph, 2, [128, 8, 65], BF16)
        for v in vos.items:
            S.I("pool", "memset", [], [v.r], v.t[:, :, :], 1.0)
        gvs = sbrot(K, ph, 2, [128, 512], F32)
        junk = sbt(K, ph, [128, 512], BF16)
        vns = sbrot(K, ph, 2, [128, 512], BF16)
        sss = sbrot(K, ph, 2, [128, 1], F32)
        rss = sbrot(K, ph, 2, [128, 1], F32)
        def tile(t):
            n = 512 if t < 8 else 256
            t0 = t * 512
            j = 0 if t < 8 else 1
            xt = xts.next()
            hT, sq, rstd = hTs.next(), sqs.next(), rstds.next()
            for sbk in range(n // 128):
                blk = t * 4 + sbk
                a = prex[t][sbk]
                pt, rpt = bank2(K)
                for kc in range(KC):
                    S.I("pe", "transpose", [a.r, K.ident.r], rpt, out=pt[:, kc * 128:(kc + 1) * 128], in_=a.t[:, kc * 128:(kc + 1) * 128],
                        identity=K.ident.t[:, :])
                if sbk % 2:
                    S.I("act", "copy", rpt, [xt.r], out=xt.t[:, :, sbk * 128:(sbk + 1) * 128], in_=pt.rearrange("p (k t) -> p k t", t=128))
                else:
                    S.I("dve", "tensor_copy", rpt, [xt.r], out=xt.t[:, :, sbk * 128:(sbk + 1) * 128], in_=pt.rearrange("p (k t) -> p k t", t=128))
            S.D("sp", K.XT[:, :, t0:t0 + n], xt.t[:, :, :n], [xt.r], blkres(K.R_XT, t0, n))
            norm_mod(K, S, xt, n, M["Am"], M["Bm"], j, hT, tmps, sq, rstd)
            yield
            if t + 1 < 9:
                xload(t + 1)
            qo, ko, ao = qos.next(), kos.next(), aos.next()
            for ci in list(range(4)) + list(range(8, 16)):
                ps, rps = bank(K)
                for kc in range(KC):
                    S.I("pe", "matmul", [win.r, hT.r], [rps], ps[:, :n], lhsT=win.t[:, kc, ci * 128:(ci + 1) * 128], rhs=hT.t[:, kc, :n],
                        start=(kc == 0), stop=(kc == KC - 1))
                if ci < 4:
                    S.I("act", "activation", [rps], [uT.r], out=uT.t[:, ci, :n], in_=ps[:, :n], func=AF.Gelu_apprx_tanh)
                elif ci < 12:
                    S.I("dve", "tensor_scalar", [rps], [qo.r], out=qo.t[:, ci - 8, :n], in0=ps[:, :n], scalar1=0.125, scalar2=None, op0=ALU.mult)
                else:
                    S.I("act", "copy", [rps], [ko.r], out=ko.t[:, ci - 12, :n], in_=ps[:, :n])
            S.D("sp", K.QT[:, :, t0:t0 + n], qo.t[:, :, :n], [qo.r], blkres(K.R_QT, t0, n))
            S.D("sp", K.KT[:, :, t0:t0 + n], ko.t[:, :, :n], [ko.r], blkres(K.R_KT, t0, n))
            yield
            pend = []
            for sbk in range(n // 128):
                blk = t * 4 + sbk
                tk = slice(sbk * 128, (sbk + 1) * 128)
                pv, rpv = bank(K)
                pa, rpa = bank(K)
                for kc in range(KC):
                    S.I("pe", "matmul", [win.r, hT.r], [rpv], pv, lhsT=hT.t[:, kc, tk], rhs=win.t[:, kc, 512:1024], start=(kc == 0), stop=(kc == KC - 1))
                for kc in range(KC):
                    S.I("pe", "matmul", [win.r, hT.r], [rpa], pa, lhsT=hT.t[:, kc, tk], rhs=win.t[:, kc, 2048:2560], start=(kc == 0), stop=(kc == KC - 1))
                while pend:
                    pend.pop(0)()
                vo = vos.next()
                S.I("act", "copy", [rpa], [vo.r], out=vo.t[:, :, 0:64], in_=pa.rearrange("p (h d) -> p h d", d=64))
                S.D("sp", K.VT[blk].rearrange("p (h e) -> p h e", e=65), vo.t[:, :, :], [vo.r], [K.R_VT[blk]])
                gv, ss, rs, vn = gvs.next(), sss.next(), rss.next(), vns.next()
                S.I("act", "activation", [rpv], [gv.r], out=gv.t[:, :], in_=pv, func=AF.Gelu_apprx_tanh)
                S.I("dve", "memset", [], [ss.r], ss.t[:, :], 0.0)
                S.I("act", "activation", [gv.r, ss.r], [junk.r, ss.r], out=junk.t[:, :], in_=gv.t[:, :], func=AF.Square, accum_out=ss.t[:, 0:1])
                small_rstd(K, S, ss, rs, 512)
                S.I("dve", "scalar_tensor_tensor", [gv.r, rs.r, nwbc.r], [vn.r], out=vn.t[:, :], in0=gv.t[:, :], scalar=rs.t[:, 0:1],
                    in1=nwbc.t[:, :], op0=ALU.mult, op1=ALU.mult)
                def mk(vn=vn, tk=tk):
                    def f():
                        pg, rpg = bank(K)
                        for g in range(4):
                            gs = slice(g * 128, (g + 1) * 128)
                            S.I("pe", "matmul", [vn.r, sguw.r], [rpg], pg[:, gs], lhsT=vn.t[:, gs], rhs=sguw.t[:, g, :], start=True, stop=True)
                        gt = gts.next()
                        S.I("dve", "tensor_tensor", [rpg, sgub.r], [gt.r], out=gt.t[:, :], in0=pg, in1=sgub.t[:, :], op=ALU.add)
                        S.I("dve", "tensor_tensor", [gt.r, uT.r], [ao.r], out=ao.t[:, :, tk], in0=gt.t[:, :].rearrange("p (g t) -> p g t", t=128),
                            in1=uT.t[:, :, tk], op=ALU.mult)
                    return f
                pend.append(mk())
            while pend:
                pend.pop(0)()
            S.D("sp", K.MIX[:, 0:4, t0:t0 + n], ao.t[:, :, :n], [ao.r], blkres(K.R_MIXa, t0, n))
        xload(0)
        gens = [tile(t) for t in range(9)]
        next(gens[0])
        for t in range(9):
            next(gens[t])
            if t + 1 < 9:
                next(gens[t + 1])
            for _ in gens[t]:
                pass
        S.flush()


def na_blocks():
    out = []
    for i in range(32):
        if i < 2:
            out.append((list(range(4)), 5 + 4 * i))
        elif i >= 30:
            out.append((list(range(28, 32)), 5 + 4 * (i - 28)))
        else:
            out.append((list(range(i - 2, i + 3)), 0))
    out.append(([], 0))
    out.append(([], 0))
    return out


def phaseB0(K, S):
    with ExitStack() as ph:
        vall = sbt(K, ph, [128, 34, 520], BF16)
        S.D("sp", vall.t[:, :, :], K.VT.rearrange("b p e -> p b e"), K.R_VT, [vall.r])
        vv = vall.t[:, :, :].rearrange("p b (h e) -> p b h e", e=65)
        qjs = sbrot(K, ph, 2, [128, NT], BF16)
        kjs = sbrot(K, ph, 2, [128, NT], BF16)
        nbs = sbrot(K, ph, 2, [128, 21, 128], F32)
        otok = sbt(K, ph, [128, 34, 512], BF16)
        tmps = sbrot(K, ph, 3, [128, 640], F32)
        pTs = sbrot(K, ph, 4, [128, 896], BF16)
        rcs = sbrot(K, ph, 3, [128, 1], F32)
        blocks = na_blocks()
        spairs = Rot([0, 1, 2])
        pobanks = Rot([6, 7])
        its = []
        for jp in range(4):
            for hh in range(2):
                for i in range(34):
                    its.append((jp, hh, i))
        N = len(its)
        ctxs = [None] * N
        cur = {}

        def stA(k):
            jp, hh, i = its[k]
            if hh == 0 and i == 0:
                cur["qj"], cur["kj"] = qjs.next(), kjs.next()
                S.D("sp", cur["qj"].t[:, :], K.QT[:, jp, :], K.R_QT, [cur["qj"].r])
                S.D("sp", cur["kj"].t[:, :], K.KT[:, jp, :], K.R_KT, [cur["kj"].r])
            if i == 0:
                cur["nb"] = nbs.next()
                S.D("sp", cur["nb"].t[:, :, :], K.inp["nab"][2 * jp + hh], [], [cur["nb"].r])
            qj, kj, nb = cur["qj"], cur["kj"], cur["nb"]
            hb = hh * 64
            chunks, t0 = blocks[i]
            allc = chunks + [32, 33]
            pi = spairs.next()
            Sp = K.PB[pi].t[:, :]
            rS = [K.RB[2 * pi], K.RB[2 * pi + 1]]
            for ci, m in enumerate(allc):
                S.I("pe", "matmul", [qj.r, kj.r], rS, Sp[:, ci * 128:(ci + 1) * 128], lhsT=kj.t[hb:hb + 64, m * 128:(m + 1) * 128],
                    rhs=qj.t[hb:hb + 64, i * 128:(i + 1) * 128], start=True, stop=True)
            ctxs[k] = dict(Sp=Sp, rS=rS, nb=nb, nnb=len(chunks), t0=t0, allc=allc, h=2 * jp + hh, i=i)

        def stB(k):
            c = ctxs[k]
            nnb, Sp, rS = c["nnb"], c["Sp"], c["rS"]
            pT = pTs.next()
            c["pT"] = pT
            if nnb:
                tm = tmps.next()
                S.I("dve", "tensor_tensor", rS + [c["nb"].r], [tm.r], out=tm.t[:, :nnb * 128].rearrange("p (a b) -> p a b", b=128),
                    in0=Sp[:, :nnb * 128].rearrange("p (a b) -> p a b", b=128), in1=c["nb"].t[:, c["t0"]:c["t0"] + nnb, :], op=ALU.add)
                S.I("act", "activation", [tm.r], [pT.r], out=pT.t[:, :nnb * 128], in_=tm.t[:, :nnb * 128], func=AF.Exp)
            S.I("act", "activation", rS, [pT.r], out=pT.t[:, nnb * 128:(nnb + 2) * 128], in_=Sp[:, nnb * 128:(nnb + 2) * 128], func=AF.Exp)

        def stC(k):
            c = ctxs[k]
            po, rpo = bank_fixed(K, pobanks.next())
            c["po"], c["rpo"] = po, rpo
            pT, allc = c["pT"], c["allc"]
            for ci, m in enumerate(allc):
                S.I("pe", "matmul", [pT.r, vall.r], [rpo], po[:, 0:65], lhsT=pT.t[:, ci * 128:(ci + 1) * 128], rhs=vv[:, m, c["h"], :],
                    start=(ci == 0), stop=(ci == len(allc) - 1))

        def stD(k):
            c = ctxs[k]
            po, rpo, h, i = c["po"], c["rpo"], c["h"], c["i"]
            rc = rcs.next()
            S.I("dve", "reciprocal", [rpo], [rc.r], out=rc.t[:, 0:1], in_=po[:, 64:65])
            S.I("dve", "tensor_scalar", [rpo, rc.r], [otok.r], out=otok.t[:, i, h * 64:(h + 1) * 64], in0=po[:, 0:64], scalar1=rc.t[:, 0:1],
                scalar2=None, op0=ALU.mult)
            ctxs[k] = None
        for step in range(N + 3):
            if step < N:
                stA(step)
            if 0 <= step - 3 < N:
                stD(step - 3)
            if 0 <= step - 1 < N:
                stB(step - 1)
            if 0 <= step - 2 < N:
                stC(step - 2)
        K.bank_rr = 0
        bos = sbrot(K, ph, 2, [128, 4, 128], BF16)
        for i in range(34):
            pt, rpt = bank(K)
            for fc in range(4):
                S.I("pe", "matmul", [otok.r, K.ident_bf.r], [rpt], pt[:, fc * 128:(fc + 1) * 128], lhsT=otok.t[:, i, fc * 128:(fc + 1) * 128],
                    rhs=K.ident_bf.t[:, :], start=True, stop=True)
            bo = bos.next()
            bv = bo.t[:, :, :].rearrange("p a b -> p (a b)")
            if i % 2:
                S.I("act", "copy", [rpt], [bo.r], out=bv, in_=pt)
            else:
                S.I("dve", "tensor_copy", [rpt], [bo.r], out=bv, in_=pt)
            S.D("sp", K.MIX[:, 4:8, i * 128:(i + 1) * 128], bo.t[:, :, :], [bo.r], [K.R_MIXb[i]])
        S.flush()


def phaseC(K, S, l):
    M = K.mods[l]
    ntiles = 9 if l == 0 else 8
    last = (l == 1)
    with ExitStack() as ph:
        wout = sbt(K, ph, [128, 8, 1024], BF16)
        wv = K.inp["ab_w_out" if l == 0 else "cd_w_out"].rearrange("(kc p) n -> p kc n", p=128)
        for kc in range(KC):
            S.D("pool", wout.t[:, kc, :], wv[:, kc, :], [], [wout.r])
        wdr = sbt(K, ph, [128, NJ, 1024], BF16)
        wdv = K.inp["ffn_wd"][l].rearrange("(j p) n -> p j n", p=128)
        wdres = [Res() for _ in range(NJ)]
        for jj in range(NJ):
            S.D("pool", wdr.t[:, jj, :], wdv[:, jj, :], [], [wdres[jj]])
        xts = sbrot(K, ph, 2, [128, 8, 512], F32)
        mxs = sbrot(K, ph, 2, [128, 8, 512], BF16)
        sq = sbt(K, ph, [128, 8, 512], BF16)
        rstd = sbt(K, ph, [128, 512], F32)
        tmps = sbrot(K, ph, 2, [128, 512], F32)
        h2 = sbt(K, ph, [128, 8, 512], BF16)
        aT = sbt(K, ph, [128, NJ, 512], BF16)
        wgs = sbrot(K, ph, 2, [128, 8, 256], BF16)
        wus = sbrot(K, ph, 2, [128, 8, 256], BF16)
        sgs = sbrot(K, ph, 2, [128, 512], F32)
        if last:
            fnw = sbt(K, ph, [128, 1024], F32)
            S.D("sp", fnw.t[:, :], K.inp["fnw_bc"], [], [fnw.r])
            ots = sbrot(K, ph, 2, [128, 1024], F32)
            junk = sbt(K, ph, [128, 1024], BF16)
            sss = sbrot(K, ph, 2, [128, 1], F32)
            rss = sbrot(K, ph, 2, [128, 1], F32)
        wgv = [K.WGb[l][m].rearrange("(kc p) n -> p kc n", p=128) for m in range(2)]
        def cload(t):
            n = 512 if t < 8 else 256
            t0 = t * 512
            xt, mx = xts.next(), mxs.next()
            S.D("sp", xt.t[:, :, :n], K.XT[:, :, t0:t0 + n], blkres(K.R_XT, t0, n), [xt.r])
            S.D("sp", mx.t[:, :, :n], K.MIX[:, :, t0:t0 + n], blkres(K.R_MIXa, t0, n) + blkres(K.R_MIXb, t0, n), [mx.r])
            return xt, mx
        nxt = cload(0)
        for t in range(ntiles):
            n = 512 if t < 8 else 256
            t0 = t * 512
            j = 0 if t < 8 else 1
            xt, mx = nxt
            for mo in range(8):
                ps, rps = bank(K)
                for kc in range(KC):
                    S.I("pe", "matmul", [wout.r, mx.r], [rps], ps[:, :n], lhsT=wout.t[:, kc, mo * 128:(mo + 1) * 128], rhs=mx.t[:, kc, :n],
                        start=(kc == 0), stop=(kc == KC - 1))
                S.I("dve", "scalar_tensor_tensor", [rps, xt.r, K.modr], [xt.r], out=xt.t[:, mo, :n], in0=ps[:, :n], scalar=M["Gm"].t[:, mo, j:j + 1],
                    in1=xt.t[:, mo, :n], op0=ALU.mult, op1=ALU.add)
            norm_mod(K, S, xt, n, M["Af"], M["Bf"], j, h2, tmps, sq, rstd)
            if t + 1 < ntiles:
                nxt = cload(t + 1)
            for pc in range(11):
                wg, wu = wgs.next(), wus.next()
                S.D("sp", wg.t[:, :, :], wgv[0][:, :, pc * 256:(pc + 1) * 256], K.R_WG[l][0], [wg.r])
                S.D("sp", wu.t[:, :, :], wgv[1][:, :, pc * 256:(pc + 1) * 256], K.R_WG[l][1], [wu.r])
                for q in range(2):
                    jj = pc * 2 + q
                    pg, rpg = bank(K)
                    pu, rpu = bank(K)
                    for kc in range(KC):
                        S.I("pe", "matmul", [wg.r, h2.r], [rpg], pg[:, :n], lhsT=wg.t[:, kc, q * 128:(q + 1) * 128], rhs=h2.t[:, kc, :n], start=(kc == 0), stop=(kc == KC - 1))
                    for kc in range(KC):
                        S.I("pe", "matmul", [wu.r, h2.r], [rpu], pu[:, :n], lhsT=wu.t[:, kc, q * 128:(q + 1) * 128], rhs=h2.t[:, kc, :n], start=(kc == 0), stop=(kc == KC - 1))
                    sg = sgs.next()
                    S.I("act", "activation", [rpg], [sg.r], out=sg.t[:, :n], in_=pg[:, :n], func=AF.Silu)
                    S.I("dve", "tensor_tensor", [sg.r, rpu], [aT.r], out=aT.t[:, jj, :n], in0=sg.t[:, :n], in1=pu[:, :n], op=ALU.mult)
            for mo in range(8):
                ps, rps = bank(K)
                for jj in range(NJ):
                    S.I("pe", "matmul", [wdres[jj], aT.r], [rps], ps[:, :n], lhsT=wdr.t[:, jj, mo * 128:(mo + 1) * 128], rhs=aT.t[:, jj, :n], start=(jj == 0), stop=(jj == NJ - 1))
                S.I("dve", "scalar_tensor_tensor", [rps, xt.r, K.modr], [xt.r], out=xt.t[:, mo, :n], in0=ps[:, :n], scalar=M["Gf"].t[:, mo, j:j + 1],
                    in1=xt.t[:, mo, :n], op0=ALU.mult, op1=ALU.add)
            if not last:
                S.D("sp", K.XT[:, :, t0:t0 + n], xt.t[:, :, :n], [xt.r], blkres(K.R_XT, t0, n))
            else:
                for sbk in range(4):
                    blk = t * 4 + sbk
                    pt, rpt = bank2(K)
                    for kc in range(KC):
                        S.I("pe", "transpose", [xt.r, K.ident.r], rpt, out=pt[:, kc * 128:(kc + 1) * 128], in_=xt.t[:, kc, sbk * 128:(sbk + 1) * 128], identity=K.ident.t[:, :])
                    ss, rs, ot = sss.next(), rss.next(), ots.next()
                    S.I("dve", "memset", [], [ss.r], ss.t[:, :], 0.0)
                    S.I("act", "activation", rpt + [ss.r], [junk.r, ss.r], out=junk.t[:, :], in_=pt, func=AF.Square, accum_out=ss.t[:, 0:1])
                    small_rstd(K, S, ss, rs, 1024)
                    S.I("dve", "scalar_tensor_tensor", rpt + [rs.r, fnw.r], [ot.r], out=ot.t[:, :], in0=pt, scalar=rs.t[:, 0:1], in1=fnw.t[:, :],
                        op0=ALU.mult, op1=ALU.mult)
                    S.D("sp", K.out[blk * 128:(blk + 1) * 128, :], ot.t[:, :], [ot.r], [K.R_out])
        S.flush()


def phaseD(K, S):
    with ExitStack() as ph:
        fnw = sbt(K, ph, [128, 1024], F32)
        S.D("sp", fnw.t[:, :], K.inp["fnw_bc"], [], [fnw.r])
        xbs = sbrot(K, ph, 2, [128, 8, 128], F32)
        ots = sbrot(K, ph, 2, [128, 1024], F32)
        junk = sbt(K, ph, [128, 1024], BF16)
        sss = sbrot(K, ph, 2, [128, 1], F32)
        rss = sbrot(K, ph, 2, [128, 1], F32)
        for blk in range(32):
            xb = xbs.next()
            S.D("sp", xb.t[:, :, :], K.XT[:, :, blk * 128:(blk + 1) * 128], [K.R_XT[blk]], [xb.r])
            pt, rpt = bank2(K)
            for kc in range(KC):
                S.I("pe", "transpose", [xb.r, K.ident.r], rpt, out=pt[:, kc * 128:(kc + 1) * 128], in_=xb.t[:, kc, :], identity=K.ident.t[:, :])
            ss, rs, ot = sss.next(), rss.next(), ots.next()
            S.I("dve", "memset", [], [ss.r], ss.t[:, :], 0.0)
            S.I("act", "activation", rpt + [ss.r], [junk.r, ss.r], out=junk.t[:, :], in_=pt, func=AF.Square, accum_out=ss.t[:, 0:1])
            small_rstd(K, S, ss, rs, 1024)
            S.I("dve", "scalar_tensor_tensor", rpt + [rs.r, fnw.r], [ot.r], out=ot.t[:, :], in0=pt, scalar=rs.t[:, 0:1], in1=fnw.t[:, :],
                op0=ALU.mult, op1=ALU.mult)
            S.D("sp", K.out[blk * 128:(blk + 1) * 128, :], ot.t[:, :], [ot.r], [K.R_out])
        S.flush()


def l1_decl(K, din):
    din("cd_w_in", [D, 2080]); din("cd_w_perm", [D, 512]); din("decw", [2, 17, 256]); din("hnw", [128, 1])
    din("tri", [128, 4, 128]); din("gmask", [128, 2, 128])
    din("ropeC", [128, NX]); din("ropeS", [128, NX]); din("ropeCt", [32, 128, 256]); din("ropeSt", [32, 128, 256])
    din("w128ri", [128, 256]); din("wc1", [128, 256]); din("wc2", [128, 256]); din("tw", [128, 32, 2, 128])


def l1_scratch(K, scr):
    K.HT = scr("HT", [128, 8, NX], BF16)
    K.GQ = scr("GQ", [128, 2, NT], BF16)
    K.GK = scr("GK", [128, 2, NT], BF16)
    K.KTOK = scr("KTOK", [34, 128, 256], BF16)
    K.VTOK1 = scr("VTOK1", [34, 128, 512], BF16)
    K.SG = scr("SG", [128, 4, NX], BF16)
    K.ALR = scr("ALR", [2, 16, NT], F32)
    K.QE = [scr("QE%d" % d, [128, 2, NX], BF16) for d in range(2)]
    K.KE = [scr("KE%d" % d, [128, 2, NX], BF16) for d in range(2)]
    for nm in ("HT", "GQ", "GK", "KTOK", "VTOK1", "SG", "ALR", "QE0", "QE1", "KE0", "KE1"):
        setattr(K, "R_" + nm, [Res() for _ in range(34)])


def phaseA1(K, S):
    M = K.mods[1]
    with ExitStack() as ph:
        win = sbt(K, ph, [128, 8, 2080], BF16)
        wv = K.inp["cd_w_in"].rearrange("(kc p) n -> p kc n", p=128)
        for kc in range(KC):
            S.D("pool", win.t[:, kc, :], wv[:, kc, :], [], [win.r])
        wpm = sbt(K, ph, [128, 8, 512], BF16)
        S.D("pool", wpm.t[:, :, :], K.inp["cd_w_perm"].rearrange("(kc p) n -> p kc n", p=128), [], [wpm.r])
        xts = sbrot(K, ph, 2, [128, 8, 512], F32)
        sqs = sbrot(K, ph, 2, [128, 8, 512], BF16)
        rstds = sbrot(K, ph, 2, [128, 512], F32)
        tmps = sbrot(K, ph, 2, [128, 512], F32)
        hTs = sbrot(K, ph, 2, [128, 8, 512], BF16)
        rcs = sbrot(K, ph, 2, [128, 512], F32)
        rss = sbrot(K, ph, 2, [128, 512], F32)
        t1s = sbrot(K, ph, 2, [128, 512], F32)
        t2s = sbrot(K, ph, 2, [128, 512], F32)
        qos = sbrot(K, ph, 2, [128, 2, 512], BF16)
        kos = sbrot(K, ph, 2, [128, 2, 512], BF16)
        sgs = sbrot(K, ph, 2, [128, 4, 512], BF16)
        als = sbrot(K, ph, 2, [16, 2, 512], F32)
        vos = sbrot(K, ph, 2, [128, 512], BF16)
        cts = sbrot(K, ph, 2, [128, 256], F32)
        sts = sbrot(K, ph, 2, [128, 256], F32)
        u1s = sbrot(K, ph, 2, [128, 256], F32)
        u2s = sbrot(K, ph, 2, [128, 256], F32)
        ktos = sbrot(K, ph, 2, [128, 256], BF16)
        prex = {}

        def xload(t):
            n = 512 if t < 8 else 256
            t0 = t * 512
            xt = xts.next()
            S.D("sp", xt.t[:, :, :n], K.XT[:, :, t0:t0 + n], blkres(K.R_XT, t0, n), [xt.r])
            prex[t] = xt

        def tile(t):
            n = 512 if t < 8 else 256
            t0 = t * 512
            isx = t < 8
            j = 0 if isx else 1
            xt = prex[t]
            hT = hTs.next()
            sq, rstd = sqs.next(), rstds.next()
            norm_mod(K, S, xt, n, M["Am"], M["Bm"], j, hT, tmps, sq, rstd)
            yield
            if t + 1 < 9:
                xload(t + 1)
            if isx:
                S.D("sp", K.HT[:, :, t0:t0 + n], hT.t[:, :, :n], [hT.r], blkres(K.R_HT, t0, n))
                rc, rs = rcs.next(), rss.next()
                S.D("sp", rc.t[:, :], K.inp["ropeC"][:, t0:t0 + n], [], [rc.r])
                S.D("sp", rs.t[:, :], K.inp["ropeS"][:, t0:t0 + n], [], [rs.r])
            qo, ko = qos.next(), kos.next()
            for c in (range(4) if isx else (2, 3)):
                p1, rp1 = bank(K)
                for kc in range(KC):
                    S.I("pe", "matmul", [win.r, hT.r], [rp1], p1[:, :n], lhsT=win.t[:, kc, 512 + c * 128:512 + (c + 1) * 128], rhs=hT.t[:, kc, :n],
                        start=(kc == 0), stop=(kc == KC - 1))
                dst = qo if c < 2 else ko
                if not isx:
                    S.I("act", "copy", [rp1], [dst.r], out=dst.t[:, c % 2, :n], in_=p1[:, :n])
                    continue
                p2, rp2 = bank(K)
                for kc in range(KC):
                    S.I("pe", "matmul", [wpm.r, hT.r], [rp2], p2[:, :n], lhsT=wpm.t[:, kc, c * 128:(c + 1) * 128], rhs=hT.t[:, kc, :n],
                        start=(kc == 0), stop=(kc == KC - 1))
                sc = 0.125 if c < 2 else 1.0
                t1, t2 = t1s.next(), t2s.next()
                S.I("dve", "scalar_tensor_tensor", [rp1, rc.r], [t1.r], out=t1.t[:, :n], in0=p1[:, :n], scalar=sc, in1=rc.t[:, :n], op0=ALU.mult, op1=ALU.mult)
                S.I("dve", "scalar_tensor_tensor", [rp2, rs.r], [t2.r], out=t2.t[:, :n], in0=p2[:, :n], scalar=sc, in1=rs.t[:, :n], op0=ALU.mult, op1=ALU.mult)
                S.I("pool", "tensor_tensor", [t1.r, t2.r], [dst.r], out=dst.t[:, c % 2, :n], in0=t1.t[:, :n], in1=t2.t[:, :n], op=ALU.add)
            if isx:
                S.D("sp", K.GQ[:, :, t0:t0 + n], qo.t[:, :, :n], [qo.r], blkres(K.R_GQ, t0, n))
            S.D("sp", K.GK[:, :, t0:t0 + n], ko.t[:, :, :n], [ko.r], blkres(K.R_GK, t0, n))
            if isx:
                sg = sgs.next()
                for c in range(4):
                    p1, rp1 = bank(K)
                    for kc in range(KC):
                        S.I("pe", "matmul", [win.r, hT.r], [rp1], p1[:, :n], lhsT=win.t[:, kc, 1536 + c * 128:1536 + (c + 1) * 128], rhs=hT.t[:, kc, :n],
                            start=(kc == 0), stop=(kc == KC - 1))
                    S.I("act", "activation", [rp1], [sg.r], out=sg.t[:, c, :n], in_=p1[:, :n], func=AF.Silu)
                S.D("sp", K.SG[:, :, t0:t0 + n], sg.t[:, :, :n], [sg.r], blkres(K.R_SG, t0, n))
            al = als.next()
            for d in range(2):
                p1, rp1 = bank(K)
                for kc in range(KC):
                    S.I("pe", "matmul", [win.r, hT.r], [rp1], p1[0:16, :n], lhsT=win.t[:, kc, 2048 + d * 16:2064 + d * 16], rhs=hT.t[:, kc, :n],
                        start=(kc == 0), stop=(kc == KC - 1))
                S.I("dve", "tensor_copy", [rp1], [al.r], out=al.t[:, d, :n], in_=p1[0:16, :n])
            for d in range(2):
                S.D("sp", K.ALR[d, :, t0:t0 + n], al.t[:, d, :n], [al.r], blkres(K.R_ALR, t0, n))
            yield
            for sbk in range(n // 128):
                blk = t * 4 + sbk
                tk = slice(sbk * 128, (sbk + 1) * 128)
                pv, rpv = bank(K)
                for kc in range(KC):
                    S.I("pe", "matmul", [win.r, hT.r], [rpv], pv, lhsT=hT.t[:, kc, tk], rhs=win.t[:, kc, 1024:1536], start=(kc == 0), stop=(kc == KC - 1))
                vo = vos.next()
                S.I("act", "copy", [rpv], [vo.r], out=vo.t[:, :], in_=pv)
                S.D("sp", K.VTOK1[blk], vo.t[:, :], [vo.r], [K.R_VTOK1[blk]])
                pk, rpk = bank(K)
                for kc in range(KC):
                    S.I("pe", "matmul", [win.r, hT.r], [rpk], pk[:, 0:256], lhsT=hT.t[:, kc, tk], rhs=win.t[:, kc, 768:1024], start=(kc == 0), stop=(kc == KC - 1))
                kto = ktos.next()
                if isx:
                    for kc in range(KC):
                        S.I("pe", "matmul", [wpm.r, hT.r], [rpk], pk[:, 256:512], lhsT=hT.t[:, kc, tk], rhs=wpm.t[:, kc, 256:512], start=(kc == 0), stop=(kc == KC - 1))
                    ct, st, u1, u2 = cts.next(), sts.next(), u1s.next(), u2s.next()
                    S.D("sp", ct.t[:, :], K.inp["ropeCt"][blk], [], [ct.r])
                    S.D("sp", st.t[:, :], K.inp["ropeSt"][blk], [], [st.r])
                    S.I("dve", "tensor_tensor", [rpk, ct.r], [u1.r], out=u1.t[:, :], in0=pk[:, 0:256], in1=ct.t[:, :], op=ALU.mult)
                    S.I("dve", "tensor_tensor", [rpk, st.r], [u2.r], out=u2.t[:, :], in0=pk[:, 256:512], in1=st.t[:, :], op=ALU.mult)
                    S.I("pool", "tensor_tensor", [u1.r, u2.r], [kto.r], out=kto.t[:, :], in0=u1.t[:, :], in1=u2.t[:, :], op=ALU.add)
                else:
                    S.I("dve", "tensor_copy", [rpk], [kto.r], out=kto.t[:, :], in_=pk[:, 0:256])
                S.D("sp", K.KTOK[blk], kto.t[:, :], [kto.r], [K.R_KTOK[blk]])
        xload(0)
        gens = [tile(t) for t in range(9)]
        next(gens[0])
        for t in range(9):
            next(gens[t])
            if t + 1 < 9:
                next(gens[t + 1])
            for _ in gens[t]:
                pass
        S.flush()


def phaseG(K, S):
    with ExitStack() as ph:
        tri = sbt(K, ph, [128, 4, 128], BF16)
        S.D("pool", tri.t[:, :, :], K.inp["tri"], [], [tri.r])
        gmask = sbt(K, ph, [128, 2, 128], F32)
        S.D("sp", gmask.t[:, :, :], K.inp["gmask"], [], [gmask.r])
        decw = [sbt(K, ph, [17, 256], F32) for _ in range(2)]
        for d in range(2):
            S.D("sp", decw[d].t[:, :], K.inp["decw"][d], [], [decw[d].r])
        hnw = sbt(K, ph, [128, 1], F32)
        S.D("sp", hnw.t[:, :], K.inp["hnw"], [], [hnw.r])
        vtk = sbt(K, ph, [128, 34, 512], BF16)
        S.D("sp", vtk.t[:, :, :], K.VTOK1.rearrange("b p e -> p b e"), K.R_VTOK1, [vtk.r])
        ktk = sbt(K, ph, [128, 34, 256], BF16)
        S.D("sp", ktk.t[:, :, :], K.KTOK.rearrange("b p e -> p b e"), K.R_KTOK, [ktk.r])
        sst = sbt(K, ph, [128, 2 * 64 * 2, 128], BF16)
        scurs = [sbt(K, ph, [128, 2, 128], F32) for _ in range(2)]
        for sc_ in scurs:
            S.I("dve", "memset", [], [sc_.r], sc_.t[:, :, :], 0.0)
        p1 = ph
        if True:
            alrs = sbrot(K, p1, 5, [17, 128], F32)
            for a in alrs.items:
                S.I("pool", "memset", [], [a.r], a.t[:, :], 1.0)
            e1s = sbrot(K, p1, 2, [128, 256], F32)
            Ls = sbrot(K, p1, 3, [128, 256], BF16)
            ebs = sbrot(K, p1, 3, [128, 2, 128], F32)
            enbs = sbrot(K, p1, 2, [128, 2, 128], F32)
            gqs = sbrot(K, p1, 6, [128, 2, 128], BF16)
            gks = sbrot(K, p1, 6, [128, 2, 128], BF16)
            qes = sbrot(K, p1, 3, [128, 2, 128], BF16)
            kes = sbrot(K, p1, 3, [128, 2, 128], BF16)
            edss = sbrot(K, p1, 2, [128, 256], F32)
            kds = sbrot(K, p1, 3, [128, 256], BF16)
            seq = []
            for d in range(2):
                order = [32, 33] + list(range(32)) if d == 0 else [33, 32] + list(range(31, -1, -1))
                for blk in order:
                    seq.append((d, blk))
            NS = len(seq)
            cx = [None] * NS
            def next_slot(d, blk, c):
                cs = (0, 1) if d == 0 else (1, 0)
                order = [32, 33] + list(range(32)) if d == 0 else [33, 32] + list(range(31, -1, -1))
                chunks = [(bk, cc) for bk in order for cc in cs]
                i = chunks.index((blk, c))
                if i + 1 >= len(chunks):
                    return None
                nb_, nc_ = chunks[i + 1]
                if nb_ >= 32:
                    return None
                return (d * 64 + nb_ * 2 + nc_) * 2

            pre = [None] * NS

            def P0(k):
                d, blk = seq[k]
                tok0 = blk * 128
                al = alrs.next()
                S.D("sp", al.t[0:16, :], K.ALR[d, :, tok0:tok0 + 128], [K.R_ALR[blk]], [al.r])
                gq = gk = None
                if blk < 32:
                    gq, gk = gqs.next(), gks.next()
                    S.D("sp", gq.t[:, :, :], K.GQ[:, :, tok0:tok0 + 128], [K.R_GQ[blk]], [gq.r])
                    S.D("sp", gk.t[:, :, :], K.GK[:, :, tok0:tok0 + 128], [K.R_GK[blk]], [gk.r])
                pre[k] = (al, gq, gk)

            def P1(k):
                d, blk = seq[k]
                tok0 = blk * 128
                al = pre[k][0]
                z, rz = bank(K)
                S.I("pe", "matmul", [al.r, decw[d].r], [rz], z[:, 0:256], lhsT=al.t[0:17, :], rhs=decw[d].t[0:17, :], start=True, stop=True)
                e1, L = e1s.next(), Ls.next()
                S.I("act", "activation", [rz], [e1.r], out=e1.t[:, :], in_=z[:, 0:256], func=AF.Exp, scale=-1.0)
                S.I("act", "activation", [e1.r], [L.r], out=L.t[:, :], in_=e1.t[:, :], func=AF.Ln, bias=1.0)
                cx[k] = dict(L=L)

            def P2(k):
                d, blk = seq[k]
                tok0 = blk * 128
                isx = blk < 32
                L = cx[k]["L"]
                bT, rbT = bank(K)
                for hp in range(2):
                    S.I("pe", "matmul", [L.r, tri.r], [rbT], bT[:, hp * 128:(hp + 1) * 128], lhsT=L.t[:, hp * 128:(hp + 1) * 128], rhs=tri.t[:, 2 * d, :],
                        start=True, stop=True)
                ds, rds = bank(K)
                S.I("pe", "matmul", [L.r, tri.r], [rds], ds[:, 0:256], lhsT=tri.t[:, 2 * d + 1, :], rhs=L.t[:, :], start=True, stop=True)
                eb = ebs.next()
                S.I("act", "activation", [rbT], [eb.r], out=eb.t[:, :, :].rearrange("p a b -> p (a b)"), in_=bT[:, 0:256], func=AF.Exp)
                eds, kd = edss.next(), kds.next()
                S.I("act", "activation", [rds], [eds.r], out=eds.t[:, :], in_=ds[:, 0:256], func=AF.Exp)
                S.I("dve", "tensor_tensor", [ktk.r, eds.r], [kd.r], out=kd.t[:, :], in0=ktk.t[:, blk, :], in1=eds.t[:, :], op=ALU.mult)
                if isx:
                    enb, qe, ke = enbs.next(), qes.next(), kes.next()
                    gq, gk = pre[k][1], pre[k][2]
                    S.I("act", "activation", [rbT], [enb.r], out=enb.t[:, :, :].rearrange("p a b -> p (a b)"), in_=bT[:, 0:256], func=AF.Exp, scale=-1.0)
                    S.I("dve", "tensor_tensor", [gq.r, eb.r], [qe.r], out=qe.t[:, :, :], in0=gq.t[:, :, :], in1=eb.t[:, :, :], op=ALU.mult)
                    S.I("pool", "tensor_tensor", [gk.r, enb.r], [ke.r], out=ke.t[:, :, :], in0=gk.t[:, :, :], in1=enb.t[:, :, :], op=ALU.mult)
                    S.D("sp", K.QE[d][:, :, tok0:tok0 + 128], qe.t[:, :, :], [qe.r], [getattr(K, "R_QE%d" % d)[blk]])
                    S.D("sp", K.KE[d][:, :, tok0:tok0 + 128], ke.t[:, :, :], [ke.r], [getattr(K, "R_KE%d" % d)[blk]])
                cx[k]["eb"] = eb
                cx[k]["kd"] = kd

            def P3(k):
                d, blk = seq[k]
                eb, kd = cx[k]["eb"], cx[k]["kd"]
                scur = scurs[d]
                for c in ((0, 1) if d == 0 else (1, 0)):
                    cs = slice(c * 64, (c + 1) * 64)
                    kv, rkv = bank(K)
                    for hp in range(2):
                        for hh in range(2):
                            h = 2 * hp + hh
                            S.I("pe", "matmul", [kd.r, vtk.r], [rkv], kv[hh * 64:(hh + 1) * 64, hp * 128:(hp + 1) * 128], lhsT=kd.t[cs, h * 64:(h + 1) * 64],
                                rhs=vtk.t[cs, blk, h * 128:(h + 1) * 128], start=True, stop=True)
                    col = c * 64 + 63 if d == 0 else c * 64
                    slot = next_slot(d, blk, c)
                    for hp in range(2):
                        if slot is not None:
                            S.I("dve", "scalar_tensor_tensor", [scur.r, eb.r, rkv], [sst.r], out=sst.t[:, slot + hp, :], in0=scur.t[:, hp, :],
                                scalar=eb.t[:, hp, col:col + 1], in1=kv[:, hp * 128:(hp + 1) * 128], op0=ALU.mult, op1=ALU.add)
                        S.I("dve", "scalar_tensor_tensor", [scur.r, eb.r, rkv], [scur.r], out=scur.t[:, hp, :], in0=scur.t[:, hp, :],
                            scalar=eb.t[:, hp, col:col + 1], in1=kv[:, hp * 128:(hp + 1) * 128], op0=ALU.mult, op1=ALU.add)
                cx[k] = None
            P0(0)
            P0(1)
            for st in range(NS + 2):
                if st + 2 < NS:
                    P0(st + 2)
                if st < NS:
                    P1(st)
                if 0 <= st - 2 < NS:
                    P3(st - 2)
                if 0 <= st - 1 < NS:
                    P2(st - 1)
        K.bank_pool = [0, 1, 2, 3, 4, 5]
        K.bank_rr = 0
        p2 = ph
        if True:
            qets = [sbrot(K, p2, 2, [128, 2, 512], BF16) for _ in range(2)]
            kets = [sbrot(K, p2, 2, [128, 2, 512], BF16) for _ in range(2)]
            sgs = sbrot(K, p2, 2, [128, 4, 512], BF16)
            gos = sbrot(K, p2, 2, [128, 4, 512], BF16)
            atms = sbrot(K, p2, 4, [128, 128], BF16)
            sqo = sbt(K, p2, [128, 512], BF16)
            rst = sbt(K, p2, [128, 512], F32)
            t1 = sbt(K, p2, [128, 512], F32)
            nacc = 0
            def g2load(T):
                t0 = T * 512
                qe = [qets[d].next() for d in range(2)]
                ke = [kets[d].next() for d in range(2)]
                for d in range(2):
                    S.D("sp", qe[d].t[:, :, :], K.QE[d][:, :, t0:t0 + 512], blkres(getattr(K, "R_QE%d" % d), t0, 512), [qe[d].r])
                    S.D("sp", ke[d].t[:, :, :], K.KE[d][:, :, t0:t0 + 512], blkres(getattr(K, "R_KE%d" % d), t0, 512), [ke[d].r])
                sg = sgs.next()
                S.D("sp", sg.t[:, :, :], K.SG[:, :, t0:t0 + 512], blkres(K.R_SG, t0, 512), [sg.r])
                return qe, ke, sg
            nxt = g2load(0)
            for T in range(8):
                t0 = T * 512
                qe, ke, sg = nxt
                if T + 1 < 8:
                    nxt = g2load(T + 1)
                go = gos.next()
                for h in range(4):
                    hp, hb = h // 2, (h % 2) * 64
                    oT, roT = bank_fixed(K, 6 + (nacc % 2))
                    nacc += 1
                    for bb in range(4):
                        blk = T * 4 + bb
                        cols = slice(bb * 128, (bb + 1) * 128)
                        atm = []
                        for d in range(2):
                            att, ratt = bank(K)
                            S.I("pe", "matmul", [ke[d].r, qe[d].r], [ratt], att[:, 0:128], lhsT=ke[d].t[hb:hb + 64, hp, cols], rhs=qe[d].t[hb:hb + 64, hp, cols],
                                start=True, stop=True)
                            am = atms.next()
                            S.I("dve", "tensor_tensor", [ratt, gmask.r], [am.r], out=am.t[:, :], in0=att[:, 0:128], in1=gmask.t[:, d, :], op=ALU.mult)
                            atm.append(am)
                        S.I("pe", "matmul", [vtk.r, atm[0].r], [roT], oT[:, cols], lhsT=vtk.t[:, blk, h * 128:(h + 1) * 128], rhs=atm[0].t[:, :], start=True, stop=False)
                        S.I("pe", "matmul", [vtk.r, atm[1].r], [roT], oT[:, cols], lhsT=vtk.t[:, blk, h * 128:(h + 1) * 128], rhs=atm[1].t[:, :], start=False, stop=False)
                        for c in range(2):
                            for d in range(2):
                                nch = blk * 2 + c
                                base = (d * 64 + nch) * 2 + hp
                                cc = slice(bb * 128 + c * 64, bb * 128 + (c + 1) * 64)
                                S.I("pe", "matmul", [sst.r, qe[d].r], [roT], oT[:, cc], lhsT=sst.t[hb:hb + 64, base, :], rhs=qe[d].t[hb:hb + 64, hp, cc],
                                    start=False, stop=(c == 1 and d == 1))
                    S.I("act", "activation", [roT], [sqo.r], out=sqo.t[:, :], in_=oT, func=AF.Square)
                    ms, rms = bank(K)
                    S.I("pe", "matmul", [K.ones_bf.r, sqo.r], [rms], ms, lhsT=K.ones_bf.t[:, :], rhs=sqo.t[:, :], start=True, stop=True)
                    S.I("act", "activation", [rms], [rst.r], out=rst.t[:, :], in_=ms, func=AF.Ln, scale=1.0 / 128, bias=K.epsc.t[:, 0:1])
                    S.I("act", "activation", [rst.r], [rst.r], out=rst.t[:, :], in_=rst.t[:, :], func=AF.Exp, scale=-0.5)
                    S.I("dve", "tensor_tensor", [roT, rst.r], [t1.r], out=t1.t[:, :], in0=oT, in1=rst.t[:, :], op=ALU.mult)
                    S.I("dve", "scalar_tensor_tensor", [t1.r, hnw.r, sg.r], [go.r], out=go.t[:, h, :], in0=t1.t[:, :], scalar=hnw.t[:, 0:1], in1=sg.t[:, h, :],
                        op0=ALU.mult, op1=ALU.mult)
                S.D("sp", K.MIX[:, 4:8, t0:t0 + 512], go.t[:, :, :], [go.r], blkres(K.R_MIXb, t0, 512))
        K.bank_pool = None
        K.bank_rr = 0
        S.flush()


def phaseF(K, S):
    with ExitStack() as ph:
        X = sbt(K, ph, [128, 32768], BF16)
        Y = sbt(K, ph, [128, 32768], BF16)
        wf = sbt(K, ph, [128, 8, 512], BF16)
        S.D("pool", wf.t[:, :, :], K.inp["cd_w_in"].rearrange("(kc p) n -> p kc n", p=128)[:, :, 0:512], [], [wf.r])
        w128 = sbt(K, ph, [128, 256], BF16)
        S.D("pool", w128.t[:, :], K.inp["w128ri"], [], [w128.r])
        wc = [sbt(K, ph, [128, 256], BF16) for _ in range(2)]
        S.D("pool", wc[0].t[:, :], K.inp["wc1"], [], [wc[0].r])
        S.D("pool", wc[1].t[:, :], K.inp["wc2"], [], [wc[1].r])
        tw = sbt(K, ph, [128, 32, 2, 128], BF16)
        S.D("pool", tw.t[:, :, :, :], K.inp["tw"], [], [tw.r])
        hTv = X.t[:, :].rearrange("p (k a b) -> p k a b", k=8, b=32)
        for kc in range(KC):
            S.D("sp", X.t[:, kc * 4096:(kc + 1) * 4096], K.HT[:, kc, :], K.R_HT[:32], [X.r])
        F1 = Y.t[:, 0:16384].rearrange("p (a b) -> p a b", b=512)
        nev = [0]

        def evac(reads, writes, out, in_):
            nev[0] += 1
            if nev[0] % 2:
                S.I("act", "copy", reads, writes, out=out, in_=in_)
            else:
                S.I("dve", "tensor_copy", reads, writes, out=out, in_=in_)
        for n2 in range(32):
            ps, rps = bank(K)
            for kc in range(KC):
                S.I("pe", "matmul", [X.r, wf.r], [rps], ps, lhsT=hTv[:, kc, :, n2], rhs=wf.t[:, kc, :], start=(kc == 0), stop=(kc == KC - 1))
            evac([rps], [Y.r], F1[:, n2, :], ps)
        A2 = X.t[:, :].rearrange("p (g r h n l) -> p g r h n l", g=4, r=2, h=32, n=32)
        for n2 in range(32):
            for g in range(4):
                ps, rps = bank(K)
                S.I("pe", "matmul", [Y.r, w128.r], [rps], ps[:, 0:256], lhsT=F1[:, n2, g * 128:(g + 1) * 128], rhs=w128.t[:, :], start=True, stop=True)
                evac([rps], [X.r], A2[:, g, :, :, n2, :], ps[:, 0:256].rearrange("p (r h l) -> p r h l", r=2, l=4))
        Z3 = Y.t[:, :].rearrange("p (g h r c) -> p g h r c", g=4, h=32, r=2)
        for g in range(4):
            for kh in range(32):
                ps, rps = bank(K)
                for ri in range(2):
                    off = ((g * 2 + ri) * 32 + kh) * 128
                    S.I("pe", "matmul", [X.r, wc[ri].r], [rps], ps[:, 0:256], lhsT=X.t[:, off:off + 128], rhs=wc[ri].t[:, :], start=(ri == 0), stop=(ri == 1))
                evac([rps], [Y.r], Z3[:, g, kh, :, :], ps[:, 0:256].rearrange("p (r c) -> p r c", r=2))
        fn = X.t[:, 0:16384].rearrange("p (g a h l) -> p g a h l", g=4, a=32, h=32)
        scale = float(1.0 / np.sqrt(4096.0 * 128.0))
        for g in range(4):
            for kh in range(32):
                ps, rps = bank(K)
                for ri in range(2):
                    S.I("pe", "matmul", [Y.r, tw.r], [rps], ps[:, 0:128], lhsT=Z3[:, g, kh, ri, :], rhs=tw.t[:, kh, ri, :], start=(ri == 0), stop=(ri == 1))
                nev[0] += 1
                src = ps[:, 0:128].rearrange("p (a l) -> p a l", l=4)
                if nev[0] % 2:
                    S.I("act", "mul", [rps], [X.r], out=fn[:, g, :, kh, :], in_=src, mul=scale)
                else:
                    S.I("dve", "tensor_scalar", [rps], [X.r], out=fn[:, g, :, kh, :], in0=src, scalar1=scale, scalar2=None, op0=ALU.mult)
        for g in range(4):
            S.D("sp", K.MIX[:, g, 0:NX], X.t[:, g * 4096:(g + 1) * 4096], [X.r], K.R_MIXa[:32])
        S.flush()


def l1_run(K, S):
    phaseA1(K, S)
    phaseG(K, S)
    phaseF(K, S)


def l1_prep(I, shared):
    f = lambda a: np.ascontiguousarray(np.asarray(a, dtype=np.float32))
    w_in = I["cd_w_in"][0]
    shared["cd_w_in"] = f(w_in)
    d = np.arange(64)
    e = d % 32
    partner = np.where(e < 16, d + 16, d - 16)
    perm = (np.arange(8)[:, None] * 64 + partner[None, :]).reshape(-1)
    shared["cd_w_perm"] = f(w_in[:, 512:1024][:, perm])
    shared["decw"] = f(np.stack([np.concatenate([I["cd_decay_w_fwd"][0], I["cd_decay_b_fwd"][0][None]], 0),
                                 np.concatenate([I["cd_decay_w_bwd"][0], I["cd_decay_b_bwd"][0][None]], 0)], 0))
    shared["hnw"] = f(I["cd_head_norm_w"][0].reshape(128, 1))
    m = np.arange(128)[:, None]
    l = np.arange(128)[None, :]
    sc = (m // 64) == (l // 64)
    tri = np.stack([sc & (m <= l), sc & (m > l), sc & (m >= l), sc & (m < l)], 1).astype(np.float32) * (-1.0 / 16.0)
    shared["tri"] = f(tri)
    shared["gmask"] = f(np.stack([sc & (m <= l), sc & (m >= l)], 1).astype(np.float32))
    t = np.arange(NX)
    fi = e % 16
    inv = 10000.0 ** (-(fi.astype(np.float64)) / 16.0)
    pos = np.where((d // 32)[:, None] == 0, (t // 64)[None, :], (t % 64)[None, :]).astype(np.float64)
    ang = pos * inv[:, None]
    C64 = np.cos(ang)
    S64 = np.sin(ang) * np.where(e < 16, -1.0, 1.0)[:, None]
    shared["ropeC"] = f(np.concatenate([C64, C64], 0))
    shared["ropeS"] = f(np.concatenate([S64, S64], 0))
    Ct = np.tile(C64.T, (1, 4)).reshape(32, 128, 256)
    St = np.tile(S64.T, (1, 4)).reshape(32, 128, 256)
    shared["ropeCt"] = f(Ct)
    shared["ropeSt"] = f(St)
    n = np.arange(128)
    a128 = 2 * np.pi * np.outer(n, n) / 128.0
    Cn, Sn = np.cos(a128), np.sin(a128)
    shared["w128ri"] = f(np.concatenate([Cn, -Sn], 1))
    shared["wc1"] = f(np.concatenate([Cn, -Sn], 1))
    shared["wc2"] = f(np.concatenate([Sn, Cn], 1))
    n2 = np.arange(32)
    k2 = np.arange(32)
    TW = np.zeros((32, 4, 32, 32, 4), np.complex128)
    for kh in range(32):
        for kl in range(4):
            k1 = 4 * kh + kl
            TW[:, kl, kh, :, kl] = np.exp(-2j * np.pi * n2 * k1 / 4096.0)[:, None] * np.exp(-2j * np.pi * np.outer(n2, k2) / 32.0)
    TW = TW.reshape(128, 32, 128)
    shared["tw"] = f(np.stack([TW.real, -TW.imag], 2))


LAYER1 = {"decl": l1_decl, "scratch": l1_scratch, "run": l1_run, "prep": l1_prep}


def _na_bias_tiles(rel_bias):
    NEG = np.float32(-30000.0)
    H = rel_bias.shape[0]
    out = np.full((H, 128, 21, 128), NEG, np.float32)
    kc = np.arange(64)
    qc = np.arange(64)
    cs = np.clip(qc - 8, 0, 48)
    colok = (kc[:, None] >= cs[None, :]) & (kc[:, None] < cs[None, :] + 16)
    cidx = np.clip(kc[:, None] - qc[None, :] + 15, 0, 30)

    def tile(i, m):
        t = np.full((H, 128, 128), NEG, np.float32)
        for a in range(2):
            kr = 2 * m + a
            for b in range(2):
                qr = 2 * i + b
                rs = min(max(qr - 4, 0), 56)
                if not (rs <= kr < rs + 8):
                    continue
                ridx = kr - qr + 7
                vals = rel_bias[:, ridx, :][:, cidx]
                vals = np.where(colok[None], vals, NEG)
                t[:, a * 64:(a + 1) * 64, b * 64:(b + 1) * 64] = vals
        return t
    for d in range(5):
        out[:, :, d, :] = tile(10, 10 + d - 2)
    for s, i in enumerate((0, 1, 30, 31)):
        ms = list(range(4)) if i < 2 else list(range(28, 32))
        for ci, m in enumerate(ms):
            out[:, :, 5 + 4 * s + ci, :] = tile(i, m)
    return out


def _consts():
    c = {}
    c["ident"] = np.eye(128, dtype=np.float32)
    return c


def build_program():
    nc = bass.Bass("TRN2", target_bir_lowering=False)
    K = KB()
    K.nc = nc
    K.bank_rr = 0
    K.inp = {}

    def din(name, shape, dt=F32):
        K.inp[name] = nc.dram_tensor(name, list(shape), dt, kind="ExternalInput").ap()
    din("x", [NX, D]); din("ctx", [NCX, D]); din("csT", [128, 8, 2]); din("ada_w", [2, D, 6 * D]); din("ada_bT", [2, 128, 48])
    din("nmw", [2, 128, 8]); din("nfw", [2, 128, 8]); din("fnw_bc", [128, D])
    din("ffn_wg", [2, D, FH]); din("ffn_wu", [2, D, FH]); din("ffn_wd", [2, FH, D])
    din("ab_w_in", [D, 2560]); din("ab_w_out", [D, D]); din("sgu_nw_bc", [128, 512]); din("sguWT", [128, 4, 128]); din("sgub", [128, 512])
    din("nab", [8, 128, 21, 128]); din("ident", [128, 128])
    din("cd_w_out", [D, D])
    if LAYER1 is not None:
        LAYER1["decl"](K, din)
    K.out = nc.dram_tensor("out", [NX, D], F32, kind="ExternalOutput").ap()
    K.R_out = Res()

    def scr(name, shape, dt):
        kind = "ExternalOutput" if (DEBUG and name in DEBUG) else "Internal"
        return nc.dram_tensor(name, list(shape), dt, kind=kind).ap()
    K.XT = scr("XT", [128, 8, NT], F32)
    K.MIX = scr("MIX", [128, 8, NT], BF16)
    K.QT = scr("QT", [128, 4, NT], BF16)
    K.KT = scr("KT", [128, 4, NT], BF16)
    K.VT = scr("VT", [34, 128, 520], BF16)
    K.WGb = [[scr("WG%d_%d" % (l, m), [D, FH], BF16) for m in range(2)] for l in range(2)]
    K.R_XT = [Res() for _ in range(34)]
    K.R_MIXa = [Res() for _ in range(34)]
    K.R_MIXb = [Res() for _ in range(34)]
    K.R_QT = [Res() for _ in range(34)]
    K.R_KT = [Res() for _ in range(34)]
    K.R_VT = [Res() for _ in range(34)]
    K.R_WG = [[[Res() for _ in range(KC)] for _ in range(2)] for _ in range(2)]
    if LAYER1 is not None:
        LAYER1["scratch"](K, scr)
    with ExitStack() as es:
        S = Sched(nc, es)
        K.PB = [TT(es.enter_context(nc.psum_tensor("pb%d" % i, [128, 1024], F32))) for i in range(4)]
        K.RB = [Res() for _ in range(8)]
        phase_consts(K, S, es)
        phase_mod(K, S)
        phaseA0(K, S)
        phaseB0(K, S)
        phaseC(K, S, 0)
        if LAYER1 is not None:
            LAYER1["run"](K, S)
            phaseC(K, S, 1)
    return nc


def prep_inputs(inputs):
    f = lambda a: np.ascontiguousarray(np.asarray(a, dtype=np.float32))
    I = {k: np.asarray(v) for k, v in inputs.items()}
    shared = {}
    shared["ada_w"] = f(I["ada_w"])
    shared["ada_bT"] = f(I["ada_b"].reshape(2, 48, 128).transpose(0, 2, 1))
    shared["nmw"] = f(I["norm_mix_w"].reshape(2, 8, 128).transpose(0, 2, 1))
    shared["nfw"] = f(I["norm_ffn_w"].reshape(2, 8, 128).transpose(0, 2, 1))
    shared["fnw_bc"] = f(np.broadcast_to(I["final_norm_w"][None, :], (128, D)))
    shared["ffn_wg"] = f(I["ffn_w_gate"]); shared["ffn_wu"] = f(I["ffn_w_up"]); shared["ffn_wd"] = f(I["ffn_w_down"])
    shared["ab_w_in"] = f(I["ab_w_in"][0]); shared["ab_w_out"] = f(I["ab_w_out"][0])
    shared["sgu_nw_bc"] = f(np.broadcast_to(I["ab_sgu_norm_w"][0][None, :], (128, 512)))
    shared["sguWT"] = f(I["ab_sgu_w"][0].transpose(2, 0, 1))
    shared["sgub"] = f(np.broadcast_to(I["ab_sgu_b"][0].reshape(1, 512), (128, 512)))
    shared["nab"] = _na_bias_tiles(f(I["ab_rel_bias"][0]))
    shared["cd_w_out"] = f(I["cd_w_out"][0])
    shared.update(_consts())
    if LAYER1 is not None:
        LAYER1["prep"](I, shared)
    in_maps = []
    for b in range(8):
        m = dict(shared)
        m["x"] = f(I["x"][b]); m["ctx"] = f(I["ctx"][b])
        cs = np.stack([I["c"][b], I["c_ctx"]], axis=-1)
        m["csT"] = f(cs.reshape(8, 128, 2).transpose(1, 0, 2))
        in_maps.append(m)
    return in_maps


_NC_CACHE = {}


def kernel(**inputs):
    in_maps = prep_inputs(inputs)
    if "nc" not in _NC_CACHE:
        _NC_CACHE["nc"] = build_program()
    res = run_bass_kernel_spmd(_NC_CACHE["nc"], in_maps, core_ids=list(range(8)))
    return np.stack([np.asarray(r["out"], dtype=np.float32) for r in res.results], axis=0)
```
